# Optimizing a Trainium2 kernel written in Bass

```python
import math
import jax, jax.numpy as jnp
from jax import lax
import numpy as np

D_MODEL = 1024
BATCH = 32
SEQ = 256
DEPTH = 2
DEC_BATCH = 4
DEC_SEQ = 2048
PAST_LEN = 256

F32 = jnp.float32
EPS = 1e-6
GRID_W = 64

HY_WIDTH = 256
HY_ORDER = 2
HY_SHORT = 3
HY_EMB = 33
HY_BANDS = (HY_EMB - 1) // 2
HY_FFN = 64
HY_FAST_DECAY = 0.3
HY_SLOW_DECAY = 1.5
HY_TARGET = 1e-2

MLA_HEADS = 4
MLA_Q_LORA = 256
MLA_KV_LORA = 128
MLA_NOPE = 64
MLA_ROPE = 32
MLA_V = 64
ROPE_BASE = 10000.0
Q_BLOCK = 128

RET_HEADS = 4
RET_DK = 64
RET_DV = 64
RET_CHUNK = 128

CONF_WIDTH = 256
CONF_KERNEL = 31

D_FF = ((8 * D_MODEL // 3 + 255) // 256) * 256

N_BRANCH = 4
HY_COLS = 3 * HY_WIDTH
MLA_COLS = MLA_Q_LORA + MLA_KV_LORA + MLA_ROPE
RET_COLS = 2 * RET_HEADS * RET_DK + 2 * RET_HEADS * RET_DV
CONF_COLS = 2 * CONF_WIDTH
IN_COLS = HY_COLS + MLA_COLS + RET_COLS + CONF_COLS
IN_SPLITS = [HY_COLS, HY_COLS + MLA_COLS, HY_COLS + MLA_COLS + RET_COLS]

kernel_name = 'hybrid_diffusion_prefix_trunk_step'


def rmsnorm(x, g):
    xf = x.astype(F32)
    y = xf * lax.rsqrt(jnp.mean(xf * xf, axis=-1, keepdims=True) + EPS)
    return (y * g.astype(F32)).astype(x.dtype)


def layernorm(x, g=None, b=None):
    xf = x.astype(F32)
    mu = jnp.mean(xf, axis=-1, keepdims=True)
    xc = xf - mu
    y = xc * lax.rsqrt(jnp.mean(xc * xc, axis=-1, keepdims=True) + EPS)
    if g is not None:
        y = y * g.astype(F32) + b.astype(F32)
    return y.astype(x.dtype)


def depthwise_conv(x, w, b):
    K = w.shape[0]
    y = lax.conv_general_dilated(
        x, w[:, None, :].astype(x.dtype), window_strides=(1,),
        padding=[(K // 2, K // 2)], dimension_numbers=('NWC', 'WIO', 'NWC'),
        feature_group_count=x.shape[-1])
    return y + b.astype(x.dtype)


def axial_rope_tables(L):
    rows = L // GRID_W
    row = jnp.repeat(jnp.arange(rows), GRID_W).astype(F32)
    col = jnp.tile(jnp.arange(GRID_W), rows).astype(F32)
    per_axis = MLA_ROPE // 4
    inv = ROPE_BASE ** (-jnp.arange(per_axis, dtype=F32) / per_axis)
    ang = jnp.concatenate([row[:, None] * inv, col[:, None] * inv], axis=-1)
    return jnp.cos(ang), jnp.sin(ang)


def apply_rope(x, cos, sin):
    xf = x.astype(F32)
    h = xf.shape[-1] // 2
    x1, x2 = xf[..., :h], xf[..., h:]
    return jnp.concatenate([x1 * cos - x2 * sin, x1 * sin + x2 * cos], axis=-1).astype(x.dtype)


def hyena_filters(L, w1, b1, w2, b2, w3):
    t = jnp.linspace(0.0, 1.0, L, dtype=F32)[:, None]
    w = 2.0 * math.pi * jnp.arange(L, dtype=F32)[:, None] / L
    f = jnp.linspace(1e-4, HY_BANDS - 1, HY_BANDS, dtype=F32)[None, :]
    z = jnp.concatenate([t, jnp.cos(f * w), -jnp.sin(f * w)], axis=-1)
    h = jnp.sin(z @ w1.astype(F32) + b1.astype(F32))
    h = jnp.sin(h @ w2.astype(F32) + b2.astype(F32))
    h = (h @ w3.astype(F32)).reshape(L, HY_ORDER, 2, HY_WIDTH)
    max_decay = math.log(HY_TARGET) / HY_FAST_DECAY
    min_decay = math.log(HY_TARGET) / HY_SLOW_DECAY
    deltas = jnp.abs(jnp.linspace(min_decay, max_decay, HY_WIDTH, dtype=F32))
    h = h * jnp.exp(-t[:, :, None, None] * deltas)
    return h / jnp.sum(jnp.abs(h), axis=(0, 2), keepdims=True)


def long_conv(v, h_fwd, h_bwd, bias):
    L, C = h_fwd.shape
    k = jnp.concatenate([h_fwd, jnp.zeros((1, C), F32), h_bwd[:0:-1]], axis=0)
    vf = jnp.fft.rfft(v.astype(F32), n=2 * L, axis=1)
    kf = jnp.fft.rfft(k, axis=0)
    y = jnp.fft.irfft(vf * kf[None], n=2 * L, axis=1)[:, :L]
    return (y + v.astype(F32) * bias.astype(F32)).astype(v.dtype)


def hyena_mixer(u, p):
    L = u.shape[1]
    z = depthwise_conv(u, p['hy_conv_w'], p['hy_conv_b'])
    v, x1, x2 = jnp.split(z, 3, axis=-1)
    filt = hyena_filters(L, p['hy_w1'], p['hy_b1'], p['hy_w2'], p['hy_b2'], p['hy_w3'])
    gates = (x1, x2)
    for o in range(HY_ORDER):
        v = gates[o] * long_conv(v, filt[:, o, 0], filt[:, o, 1], p['hy_bias'][o])
    return v


def block_attention(q_nope, q_rope, k_nope, k_rope, v):
    B, L, H, _ = q_nope.shape
    nb = L // Q_BLOCK
    scale = (MLA_NOPE + MLA_ROPE) ** -0.5
    qn = q_nope.reshape(B, nb, Q_BLOCK, H, MLA_NOPE).transpose(1, 0, 2, 3, 4)
    qr = q_rope.reshape(B, nb, Q_BLOCK, H, MLA_ROPE).transpose(1, 0, 2, 3, 4)

    def one_block(args):
        qn_b, qr_b = args
        s = (jnp.einsum('bqhd,bkhd->bhqk', qn_b, k_nope)
             + jnp.einsum('bqhr,bkr->bhqk', qr_b, k_rope))
        pr = jax.nn.softmax(s.astype(F32) * scale, axis=-1)
        return jnp.einsum('bhqk,bkhd->bqhd', pr.astype(v.dtype), v)

    o = lax.map(one_block, (qn, qr))
    return o.transpose(1, 0, 2, 3, 4).reshape(B, L, H * MLA_V)


def retention_scan(q, k, v, log_g, S0):
    B, L, H, _ = q.shape
    n = L // RET_CHUNK
    idx = jnp.arange(RET_CHUNK, dtype=F32)
    diff = idx[:, None] - idx[None, :]
    decay_in = jnp.where(diff >= 0, jnp.exp(jnp.maximum(diff, 0.0)[None] * log_g[:, None, None]), 0.0)
    decay_q = jnp.exp((idx[:, None] + 1.0) * log_g[None, :])
    decay_k = jnp.exp((RET_CHUNK - 1.0 - idx)[:, None] * log_g[None, :])
    decay_c = jnp.exp(RET_CHUNK * log_g)

    def chunk(a):
        return a.reshape(B, n, RET_CHUNK, H, a.shape[-1]).transpose(1, 0, 2, 3, 4)

    def step(S, blk):
        qc, kc, vc = blk
        att = jnp.einsum('bnhd,bmhd->bhnm', qc, kc) * decay_in
        o = (jnp.einsum('bhnm,bmhe->bnhe', att, vc)
             + jnp.einsum('bnhd,bhde->bnhe', qc, S) * decay_q[None, :, :, None])
        S = S * decay_c[None, :, None, None] + jnp.einsum('bmhd,bmhe->bhde', kc * decay_k[None, :, :, None], vc)
        return S, o

    S, o = lax.scan(step, S0, (chunk(q), chunk(k), chunk(v)))
    return o.transpose(1, 0, 2, 3, 4).reshape(B, L, H, v.shape[-1]), S


def retention_mixer(u, decay_logit, S0):
    B, L, _ = u.shape
    qk = RET_HEADS * RET_DK
    vd = RET_HEADS * RET_DV
    q, k, v, g = jnp.split(u, [qk, 2 * qk, 2 * qk + vd], axis=-1)
    q = q.astype(F32).reshape(B, L, RET_HEADS, RET_DK)
    k = k.astype(F32).reshape(B, L, RET_HEADS, RET_DK) * (RET_DK ** -0.5)
    v = v.astype(F32).reshape(B, L, RET_HEADS, RET_DV)
    log_g = jax.nn.log_sigmoid(decay_logit.astype(F32))
    S0 = S0.astype(F32)
    o_f, S_f = retention_scan(q, k, v, log_g[0], S0[:, 0])
    o_b, S_b = retention_scan(q[:, ::-1], k[:, ::-1], v[:, ::-1], log_g[1], S0[:, 1])
    o = layernorm(o_f + o_b[:, ::-1])
    y = jax.nn.silu(g.astype(F32)) * o.reshape(B, L, vd)
    return y.astype(u.dtype), jnp.stack([S_f, S_b], axis=1).astype(u.dtype)


def conformer_mixer(u, p):
    a, b = jnp.split(u, 2, axis=-1)
    z = a * jax.nn.sigmoid(b)
    z = depthwise_conv(z, p['conf_dw_w'], p['conf_dw_b'])
    z = layernorm(z, p['conf_ln_g'], p['conf_ln_b'])
    return jax.nn.silu(z)


def trunk_layer(x, cond, p, ctx_kv=None, ctx_state=None):
    B, L, _ = x.shape
    latent = ctx_kv is not None
    mod = (jax.nn.silu(cond) @ p['ada_w'] + p['ada_b'])[:, None, :]
    sh1, sc1, g1, sh2, sc2, g2 = jnp.split(mod, 6, axis=-1)
    h = rmsnorm(x, p['norm1_g']) * (1 + sc1) + sh1
    proj = h @ p['w_in']
    u_hy, u_mla, u_ret, u_conf = jnp.split(proj, IN_SPLITS, axis=-1)

    y_hy = hyena_mixer(u_hy, p)

    cq, ckv, krope = jnp.split(u_mla, [MLA_Q_LORA, MLA_Q_LORA + MLA_KV_LORA], axis=-1)
    ckv = rmsnorm(ckv, p['mla_kv_norm'])
    q = (rmsnorm(cq, p['mla_q_norm']) @ p['mla_w_uq']).reshape(B, L, MLA_HEADS, MLA_NOPE + MLA_ROPE)
    q_nope, q_rope = q[..., :MLA_NOPE], q[..., MLA_NOPE:]
    if latent:
        cos, sin = axial_rope_tables(L)
        q_rope = apply_rope(q_rope, cos[:, None, :], sin[:, None, :])
        keys_ckv = jnp.concatenate([ckv, ctx_kv[0].astype(ckv.dtype)], axis=1)
        keys_rope = jnp.concatenate([apply_rope(krope, cos, sin), ctx_kv[1].astype(krope.dtype)], axis=1)
    else:
        keys_ckv, keys_rope = ckv, krope
    kv = (keys_ckv @ p['mla_w_ukv']).reshape(B, keys_ckv.shape[1], MLA_HEADS, MLA_NOPE + MLA_V)
    y_mla = block_attention(q_nope, q_rope, kv[..., :MLA_NOPE], keys_rope, kv[..., MLA_NOPE:])

    S0 = ctx_state if latent else jnp.zeros((B, 2, RET_HEADS, RET_DK, RET_DV), F32)
    y_ret, S = retention_mixer(u_ret, p['ret_decay'], S0)

    y_conf = conformer_mixer(u_conf, p)

    gates = jax.nn.sigmoid(h @ p['gate_w'] + p['gate_b'])
    g_hy, g_mla, g_ret, g_conf = jnp.split(gates, N_BRANCH, axis=-1)
    merged = (g_hy * (y_hy @ p['hy_out']) + g_mla * (y_mla @ p['mla_out'])
              + g_ret * (y_ret @ p['ret_out']) + g_conf * (y_conf @ p['conf_out']))
    x = x + g1 * (merged @ p['w_o'])

    h2 = rmsnorm(x, p['norm2_g']) * (1 + sc2) + sh2
    a, b = jnp.split(h2 @ p['ffn_w1'], 2, axis=-1)
    x = x + g2 * ((jax.nn.silu(a) * b) @ p['ffn_w2'])
    return x, ckv, krope, S


def setup_inputs(seed: int = 0) -> dict:
    key = jax.random.key(seed)
    ks = iter(jax.random.split(key, 48))

    def nrm(shape, scale):
        return jax.random.normal(next(ks), shape, F32) * scale

    def gain(shape):
        return 1.0 + nrm(shape, 0.05)

    D = D_MODEL
    pr = 2.0 ** (-5.0 - np.arange(RET_HEADS))
    base = np.log((1.0 - pr) / pr).astype(np.float32)
    ret_decay = jnp.asarray(base)[None, None, :] + nrm((DEPTH, 2, RET_HEADS), 0.1)
    return {
        'x_prompt': nrm((BATCH, SEQ, D), 1.0),
        'x_sample': nrm((DEC_BATCH, DEC_SEQ, D), 1.0),
        'cache_mla_ckv': nrm((DEC_BATCH, DEPTH, PAST_LEN, MLA_KV_LORA), 1.0),
        'cache_mla_krope': nrm((DEC_BATCH, DEPTH, PAST_LEN, MLA_ROPE), 1.0),
        'state_ret': nrm((DEC_BATCH, DEPTH, 2, RET_HEADS, RET_DK, RET_DV), 1.0),
        'c': nrm((DEC_BATCH, D), 1.0),
        'c_ctx': nrm((D,), 1.0),
        'ada_w': nrm((DEPTH, D, 6 * D), 0.5 * D ** -0.5),
        'ada_b': nrm((DEPTH, 6 * D), 0.02),
        'norm1_g': gain((DEPTH, D)),
        'w_in': nrm((DEPTH, D, IN_COLS), D ** -0.5),
        'hy_conv_w': nrm((DEPTH, HY_SHORT, HY_COLS), HY_SHORT ** -0.5),
        'hy_conv_b': nrm((DEPTH, HY_COLS), 0.02),
        'hy_w1': nrm((DEPTH, HY_EMB, HY_FFN), 1.0),
        'hy_b1': nrm((DEPTH, HY_FFN), 0.1),
        'hy_w2': nrm((DEPTH, HY_FFN, HY_FFN), HY_FFN ** -0.5),
        'hy_b2': nrm((DEPTH, HY_FFN), 0.1),
        'hy_w3': nrm((DEPTH, HY_FFN, HY_ORDER * 2 * HY_WIDTH), HY_FFN ** -0.5),
        'hy_bias': nrm((DEPTH, HY_ORDER, HY_WIDTH), 0.5),
        'hy_out': nrm((DEPTH, HY_WIDTH, D), HY_WIDTH ** -0.5),
        'mla_q_norm': gain((DEPTH, MLA_Q_LORA)),
        'mla_w_uq': nrm((DEPTH, MLA_Q_LORA, MLA_HEADS * (MLA_NOPE + MLA_ROPE)), MLA_Q_LORA ** -0.5),
        'mla_kv_norm': gain((DEPTH, MLA_KV_LORA)),
        'mla_w_ukv': nrm((DEPTH, MLA_KV_LORA, MLA_HEADS * (MLA_NOPE + MLA_V)), MLA_KV_LORA ** -0.5),
        'mla_out': nrm((DEPTH, MLA_HEADS * MLA_V, D), (MLA_HEADS * MLA_V) ** -0.5),
        'ret_decay': ret_decay,
        'ret_out': nrm((DEPTH, RET_HEADS * RET_DV, D), (RET_HEADS * RET_DV) ** -0.5),
        'conf_dw_w': nrm((DEPTH, CONF_KERNEL, CONF_WIDTH), CONF_KERNEL ** -0.5),
        'conf_dw_b': nrm((DEPTH, CONF_WIDTH), 0.02),
        'conf_ln_g': gain((DEPTH, CONF_WIDTH)),
        'conf_ln_b': nrm((DEPTH, CONF_WIDTH), 0.02),
        'conf_out': nrm((DEPTH, CONF_WIDTH, D), CONF_WIDTH ** -0.5),
        'gate_w': nrm((DEPTH, D, N_BRANCH * D), D ** -0.5),
        'gate_b': nrm((DEPTH, N_BRANCH * D), 0.02),
        'w_o': nrm((DEPTH, D, D), D ** -0.5),
        'norm2_g': gain((DEPTH, D)),
        'ffn_w1': nrm((DEPTH, D, 2 * D_FF), D ** -0.5),
        'ffn_w2': nrm((DEPTH, D_FF, D), D_FF ** -0.5),
        'final_norm_g': gain((D,)),
    }


def reference(x_prompt, x_sample, cache_mla_ckv, cache_mla_krope, state_ret, c, c_ctx,
              ada_w, ada_b, norm1_g, w_in, hy_conv_w, hy_conv_b, hy_w1, hy_b1, hy_w2, hy_b2,
              hy_w3, hy_bias, hy_out, mla_q_norm, mla_w_uq, mla_kv_norm, mla_w_ukv, mla_out,
              ret_decay, ret_out, conf_dw_w, conf_dw_b, conf_ln_g, conf_ln_b, conf_out,
              gate_w, gate_b, w_o, norm2_g, ffn_w1, ffn_w2, final_norm_g):
    xp, xs = x_prompt, x_sample
    ckvs, kropes, rets = [], [], []
    for l in range(DEPTH):
        p = {
            'ada_w': ada_w[l], 'ada_b': ada_b[l], 'norm1_g': norm1_g[l], 'w_in': w_in[l],
            'hy_conv_w': hy_conv_w[l], 'hy_conv_b': hy_conv_b[l], 'hy_w1': hy_w1[l], 'hy_b1': hy_b1[l],
            'hy_w2': hy_w2[l], 'hy_b2': hy_b2[l], 'hy_w3': hy_w3[l], 'hy_bias': hy_bias[l],
            'hy_out': hy_out[l], 'mla_q_norm': mla_q_norm[l], 'mla_w_uq': mla_w_uq[l],
            'mla_kv_norm': mla_kv_norm[l], 'mla_w_ukv': mla_w_ukv[l], 'mla_out': mla_out[l],
            'ret_decay': ret_decay[l], 'ret_out': ret_out[l], 'conf_dw_w': conf_dw_w[l],
            'conf_dw_b': conf_dw_b[l], 'conf_ln_g': conf_ln_g[l], 'conf_ln_b': conf_ln_b[l],
            'conf_out': conf_out[l], 'gate_w': gate_w[l], 'gate_b': gate_b[l], 'w_o': w_o[l],
            'norm2_g': norm2_g[l], 'ffn_w1': ffn_w1[l], 'ffn_w2': ffn_w2[l],
        }
        xp, ckv_l, krope_l, ret_l = trunk_layer(xp, c_ctx[None, :], p)
        ckvs.append(ckv_l)
        kropes.append(krope_l)
        rets.append(ret_l)
        xs, _, _, _ = trunk_layer(xs, c, p, ctx_kv=(cache_mla_ckv[:, l], cache_mla_krope[:, l]),
                                  ctx_state=state_ret[:, l])
    y_prompt = rmsnorm(xp, final_norm_g)
    y_sample = rmsnorm(xs, final_norm_g)
    new_mla_ckv = jnp.stack(ckvs, axis=1)
    new_mla_krope = jnp.stack(kropes, axis=1)
    new_ret_state = jnp.stack(rets, axis=1)
    return (y_prompt, y_sample, new_mla_ckv, new_mla_krope, new_ret_state)
```

```python
import contextlib
import math
import numpy as np
import ml_dtypes
import concourse.bass as bass
import concourse.mybir as mybir
from concourse.bass_utils import run_bass_kernel_spmd

F32 = mybir.dt.float32
BF16 = mybir.dt.bfloat16
I32 = mybir.dt.int32
AF = mybir.ActivationFunctionType
ALU = mybir.AluOpType

NB = ml_dtypes.bfloat16
D = 1024
T = 2048
NSEG = 8
SEGL = 256
DEPTH = 2
EPS = 1e-6
DFF = 2816
NKEY = 2304
BIGC = 32.0

V_N1G, V_N2G, V_ADAB, V_GATEB, V_FING = 0, 8, 16, 64, 96
V_HYCW, V_HYCB, V_HYBIAS = 104, 122, 128
V_QN, V_KVN = 132, 134
V_CDW, V_CDB, V_CLG, V_CLB = 135, 197, 199, 201
V_RDALL, V_RDFB, V_HB1, V_HB2 = 203, 211, 215, 216
NV = 224
FL_HALO, FL_KF, FL_KB, FL_M0 = 0, 1, 17, 33
NFL = 40

ENGS = ("pe", "act", "dve", "pool", "sp")
SAME_ENGINE_SYNC = {"pe": False, "act": True, "dve": True, "pool": True, "sp": False}


class Buf:
    __slots__ = ("name", "wr", "rd", "dma_sem", "dma_cnt", "const", "excl", "scratch")

    def __init__(self, name):
        self.name = name
        self.wr = None
        self.rd = []
        self.dma_sem = None
        self.dma_cnt = 0
        self.const = False
        self.scratch = True
        self.excl = False


class Prog:
    def __init__(self, nc):
        self.nc = nc
        self.ops = {e: [] for e in ENGS}
        self.seq = {e: 0 for e in ENGS}
        self.known = {e: {} for e in ENGS}
        self.n_dsem = 0
        self.out_dma = []
        self.fence = None
        self.fence_d = {}
        self.pending_rd = {}

    def buf(self, name):
        return Buf(name)

    def bufs(self, name, n):
        return [Buf(f"{name}{i}") for i in range(n)]

    def _need(self, eng, dep, waits):
        if dep is None:
            return
        if dep[0] == "dma":
            key = ("d", dep[1]); val = dep[2]
        else:
            if dep[0] == eng and not SAME_ENGINE_SYNC[eng]:
                return
            key = ("e", dep[0]); val = dep[1]
        if self.known[eng].get(key, 0) >= val:
            return
        self.known[eng][key] = val
        waits[key] = max(waits.get(key, 0), val)

    def _deps(self, eng, reads, writes, nofence=False):
        waits = {}
        if not any(b.scratch for b in writes):
            nofence = True
        if self.fence is not None and not nofence:
            for e, v in self.fence.items():
                if v > 0:
                    self._need(eng, (e, v), waits)
            for si, v in self.fence_d.items():
                self._need(eng, ("dma", si, v), waits)
        for b in reads:
            self._need(eng, b.wr, waits)
            if b.excl:
                for r in b.rd:
                    if r[0] != eng:
                        self._need(eng, r, waits)
        for b in writes:
            self._need(eng, b.wr, waits)
            for r in b.rd:
                self._need(eng, r, waits)
        return waits

    def set_fence(self):
        self.fence = dict(self.seq)
        self.fence_d = dict(self.pending_rd)

    def op(self, eng, fn, reads=(), writes=()):
        waits = self._deps(eng, reads, writes)
        self.seq[eng] += 1
        me = (eng, self.seq[eng])
        for b in reads:
            if not b.const:
                b.rd.append(me)
        for b in writes:
            b.wr = me
            b.rd = []
        self.ops[eng].append((waits, fn, ("e", eng)))
        return me

    def dma(self, fn, reads=(), writes=(), queue="sp", out=False, nofence=False):
        waits = self._deps(queue, reads, writes, nofence)
        if writes:
            tgt = writes[0]
        else:
            tgt = reads[0]
        if tgt.dma_sem is None:
            tgt.dma_sem = {}
        if queue not in tgt.dma_sem:
            tgt.dma_sem[queue] = [self.n_dsem, 0]
            self.n_dsem += 1
        ent = tgt.dma_sem[queue]
        ent[1] += 16
        dep = ("dma", ent[0], ent[1])
        if reads:
            self.pending_rd[ent[0]] = ent[1]
        for b in reads:
            if not b.const:
                b.rd.append(dep)
        for b in writes:
            b.wr = dep
            b.rd = []
        self.ops[queue].append((waits, fn, ("d", ent[0])))
        if out:
            self.out_dma.append(dep)
        return dep

    def _outbuf(self, queue):
        key = "_out_" + queue
        if not hasattr(self, key):
            setattr(self, key, Buf(key))
        return getattr(self, key)

    def emit(self):
        nc = self.nc
        with contextlib.ExitStack() as st:
            esem = {e: st.enter_context(nc.semaphore("s_" + e)) for e in ENGS}
            dsem = [st.enter_context(nc.semaphore(f"d{i}")) for i in range(self.n_dsem)]
            block = st.enter_context(nc.Block())

            def sem_of(key):
                return esem[key[1]] if key[0] == "e" else dsem[key[1]]

            def run(eng, h):
                for waits, fn, inc in self.ops[eng]:
                    for key, val in waits.items():
                        h.wait_ge(sem_of(key), val)
                    ins = fn(h)
                    if inc[0] == "e":
                        ins.then_inc(esem[inc[1]], 1)
                    else:
                        ins.then_inc(dsem[inc[1]], 16)
                if eng == "sp":
                    for dep in self.out_dma:
                        h.wait_ge(dsem[dep[1]], dep[2])
                    for e in ENGS:
                        if e != "sp" and self.seq[e] > 0:
                            h.wait_ge(esem[e], self.seq[e])

            @block.tensor
            def _(h):
                run("pe", h)

            @block.scalar
            def _(h):
                run("act", h)

            @block.vector
            def _(h):
                run("dve", h)

            @block.gpsimd
            def _(h):
                run("pool", h)

            @block.sync
            def _(h):
                run("sp", h)


class Arena:
    def __init__(self, nc, st, nbytes):
        self.words = nbytes // 4
        self.t = st.enter_context(nc.sbuf_tensor("arena", [128, self.words], F32))
        self.top = 0
        self.peak = 0

    def alloc(self, shape, dt):
        esz = 2 if dt == BF16 else 4
        n = 1
        for s in shape:
            n *= s
        nbytes = (n * esz + 31) // 32 * 32
        off = self.top
        self.top += nbytes
        self.peak = max(self.peak, self.top)
        assert self.top <= self.words * 4, f"arena overflow {self.top}"
        ap = self.t[:, off // 4:(off + nbytes) // 4]
        if dt != F32:
            ap = ap.bitcast(dt)
        ap = ap[:, 0:n]
        if len(shape) == 1:
            return ap
        if len(shape) == 2:
            ap = ap.rearrange("p (a b) -> p a b", a=shape[0])
        elif len(shape) == 3:
            ap = ap.rearrange("p (a b c) -> p a b c", a=shape[0], b=shape[1])
        elif len(shape) == 4:
            ap = ap.rearrange("p (a b c d) -> p a b c d", a=shape[0], b=shape[1], c=shape[2])
        return ap

    def mark(self):
        return self.top

    def release(self, m):
        self.top = m


RING_SLOTS = 3
RING_ELEMS = 4096


class KB:
    def __init__(self, dbg=()):
        self.nc = bass.Bass("TRN2", target_bir_lowering=False)
        self.dbg = set(dbg)
        self.dram = {}
        self.outs = {}

    def din(self, name, shape, dt=F32):
        self.dram[name] = self.nc.dram_tensor(name, list(shape), dt, kind="ExternalInput").ap()
        return self.dram[name]

    def dout(self, name, shape, dt=F32):
        self.outs[name] = self.nc.dram_tensor(name, list(shape), dt, kind="ExternalOutput").ap()
        return self.outs[name]

    def act(self, out, in_, func, bias=0.0, scale=1.0, R=(), W=()):
        return self.P.op("act", lambda h: h.activation(out=out, in_=in_, func=func, bias=bias, scale=scale), R, W)

    def tt(self, out, in0, in1, op, R=(), W=(), eng="dve"):
        return self.P.op(eng, lambda h: h.tensor_tensor(out=out, in0=in0, in1=in1, op=op), R, W)

    def ts(self, out, in0, s1, s2, op0, op1=None, R=(), W=()):
        if op1 is None:
            return self.P.op("dve", lambda h: h.tensor_single_scalar(out=out, in_=in0, scalar=s1, op=op0), R, W)
        return self.P.op("dve", lambda h: h.tensor_scalar(out=out, in0=in0, scalar1=s1, scalar2=s2, op0=op0, op1=op1), R, W)

    def stt(self, out, in0, scalar, in1, op0, op1, R=(), W=()):
        return self.P.op("dve", lambda h: h.scalar_tensor_tensor(out=out, in0=in0, scalar=scalar, in1=in1, op0=op0, op1=op1), R, W)

    def cp(self, out, in_, R=(), W=(), eng="dve"):
        if eng == "act":
            return self.P.op("act", lambda h: h.copy(out=out, in_=in_), R, W)
        return self.P.op(eng, lambda h: h.tensor_copy(out=out, in_=in_), R, W)

    def memset(self, ap, val, W=(), eng="dve"):
        return self.P.op(eng, lambda h: h.memset(ap, val), (), W)

    def recip(self, out, in_, R=(), W=()):
        return self.P.op("dve", lambda h: h.reciprocal(out=out, in_=in_), R, W)

    def mm(self, mms, R=(), W=()):
        def fn(h):
            ins = None
            for (o, l, r, s0, s1) in mms:
                ins = h.matmul(o, lhsT=l, rhs=r, start=s0, stop=s1)
            return ins
        return self.P.op("pe", fn, R, W)

    def psum(self):
        while True:
            i = self.ps_next
            self.ps_next = (i + 1) % 8
            if i not in self.ps_held:
                return self.PS[i], self.PSB[i]

    def psum_hold(self):
        ps, pb = self.psum()
        i = self.PS.index(ps) if False else [k for k in range(8) if self.PSB[k] is pb][0]
        self.ps_held.add(i)
        return ps, pb, i

    def wload(self, src, n, cast=True, parts=128):
        assert n <= RING_ELEMS
        i = self.ring_next
        self.ring_next = (i + 1) % RING_SLOTS
        dst = self.ring[i][0:parts, 0:n]
        b = self.ringb[i]
        q = "pool" if cast else "sp"
        self.P.dma(lambda h: h.dma_start(out=dst, in_=src), writes=[b], queue=q, nofence=True)
        return self.ring[i], b

    def dump(self, name, ap, bufs, shape):
        if name not in self.dbg:
            return
        o = self.dout("dbg_" + name, shape)
        self.P.dma(lambda h: h.dma_start(out=o, in_=ap), reads=list(bufs), queue="pool", out=True)

    def build(self):
        nc = self.nc
        d = self.din
        d("xT", [128, 8, T]); d("condT", [128, 8]); d("vecs", [DEPTH, 128, NV]); d("flags", [128, NFL])
        d("ada_w", [DEPTH, 12, 128, 8 * 512])
        d("w_in", [DEPTH, 128, 8, 2720])
        d("gate_w", [DEPTH, 8, 128, 8 * 512])
        d("w_o", [DEPTH, 2, 128, 8 * 512])
        d("ffn_w1", [DEPTH, 22, 128, 8 * 256])
        d("ffn_w2", [DEPTH, 8, 128, 11 * 256])
        d("outw", [DEPTH, 8, 128, 1024])
        d("final_g", [128, 8])
        d("w_uq", [DEPTH, 128, 768]); d("w_ukv", [DEPTH, 128, 512])
        d("cacheT", [DEPTH, 128, 256]); d("kropeC", [DEPTH, 32, 256])
        d("oh", [32, NKEY]); d("ropeCS", [2, 32, T])
        self.dout("ckv_out", [DEPTH, T, 128]); self.dout("krope_out", [DEPTH, T, 32])
        d("rcon", [128, 642]); d("s0", [DEPTH, 128, 256])
        d("hy_w1", [DEPTH, 33, 64]); d("hy_w2", [DEPTH, 64, 64]); d("hy_w3", [DEPTH, 64, 1024])
        d("zT", [33, T]); d("decay", [128, 16, 256])
        d("ClT", [16, 128, 4096], BF16); d("FwT", [16, 128, 4096], BF16); d("GiT", [2, 8, 128, 4096], BF16)
        self.dout("st_out", [DEPTH, 128, 8, 256])
        self.dout("yT", [128, 8, T])

        with contextlib.ExitStack() as st:
            self.P = Prog(nc)
            self.A = Arena(nc, st, 207 * 1024)
            self.PS = [st.enter_context(nc.psum_tensor(f"ps{i}", [128, 512], F32))[:, :] for i in range(8)]
            self.PSB = self.P.bufs("ps", 8)
            for b_ in self.PSB:
                b_.excl = True
                b_.scratch = False
            self.ps_next = 0
            self.ps_held = set()
            self.persist()
            for l in range(DEPTH):
                self.layer(l)
            self.final()
            self.P.emit()
        return nc

    def persist(self):
        A, P, nc = self.A, self.P, self.nc
        Dm = self.dram
        self.xT = A.alloc([8, T], F32)
        self.xb = [[P.buf(f"x{k}_{tb}") for tb in range(4)] for k in range(8)]
        self.hT = A.alloc([8, T], BF16)
        self.hb = [[P.buf(f"h{k}_{tb}") for tb in range(4)] for k in range(8)]
        self.ring = [A.alloc([RING_ELEMS], BF16) for _ in range(RING_SLOTS)]
        self.ringb = P.bufs("ring", RING_SLOTS)
        self.ring_next = 0
        self.vecs = A.alloc([NV], F32)
        self.vecsb = P.buf("vecs")
        self.flags = A.alloc([NFL], F32)
        self.flagsb = P.buf("flags")
        self.modT = A.alloc([48], F32)
        self.modb = P.buf("modT")
        self.mods = A.alloc([16], F32)
        self.scb = A.alloc([8], BF16)
        self.scbb = P.buf("scb")
        self.condT = A.alloc([8], F32)
        self.condb = P.buf("cond")
        self.fing = A.alloc([8], F32)
        self.fingb = P.buf("fing")
        self.ident = A.alloc([128], BF16)
        self.ones = A.alloc([128], BF16)
        self.one1 = A.alloc([8], F32)
        self.cb = P.buf("consts")
        identf = A.alloc([128], F32)
        onesf = A.alloc([128], F32)
        self.onesf = onesf
        self.negbig_t = A.alloc([8], F32)[:, 0:1]
        self.memset(self.negbig_t, -float(BIGC * BIGC) * (96 ** -0.5), W=[P.buf("negbig")])
        Dm_ident = self.din("ident", [128, 128])
        for k in range(8):
            src = Dm["xT"][:, k, :]
            dst = self.xT[:, k, :]
            P.dma(lambda h, s=src, t=dst: h.dma_start(out=t, in_=s), writes=self.xb[k], queue="sp")
        P.dma(lambda h: h.dma_start(out=self.condT, in_=Dm["condT"]), writes=[self.condb], queue="sp")
        P.dma(lambda h: h.dma_start(out=self.flags, in_=Dm["flags"]), writes=[self.flagsb], queue="sp")
        P.dma(lambda h: h.dma_start(out=self.fing, in_=Dm["final_g"]), writes=[self.fingb], queue="sp")
        tb_ = P.buf("identf")
        P.dma(lambda h: h.dma_start(out=identf, in_=Dm_ident), writes=[tb_], queue="sp")
        self.cp(self.ident, identf, R=[tb_], W=[self.cb])
        self.memset(onesf, 1.0, W=[tb_])
        self.cp(self.ones, onesf, R=[tb_], W=[self.cb])
        self.memset(self.one1, 1.0, W=[self.cb])
        self.act(self.scb, self.condT, AF.Silu, R=[self.condb], W=[self.scbb])
        self.cb.const = True
        self.flagsb.const = True
        for b_ in ([x for r in self.xb for x in r] + [x for r in self.hb for x in r] + self.ringb +
                   [self.vecsb, self.flagsb, self.modb, self.scbb, self.condb, self.fingb, self.cb]):
            b_.scratch = False

    def modulation(self, l):
        P, Dm = self.P, self.dram
        mk = self.A.mark()
        self.modrow = self.A.alloc([6144], F32)
        self.modrowb = P.buf("modrow")
        P.dma(lambda h: h.dma_start(out=self.vecs, in_=Dm["vecs"][l]), writes=[self.vecsb], queue="sp")
        for cbk in range(12):
            w, wb = self.wload(Dm["ada_w"][l, cbk], 4096)
            w3 = w.rearrange("p (k c) -> p k c", k=8)
            ps, pb = self.psum()
            self.mm([(ps[0:1, :], self.scb[:, k:k + 1], w3[:, k, :], k == 0, k == 7) for k in range(8)],
                    R=[wb, self.scbb], W=[pb])
            self.cp(self.modrow[0:1, cbk * 512:(cbk + 1) * 512], ps[0:1, :], R=[pb], W=[self.modrowb], eng="act")
        ps, pb = self.psum()
        self.mm([(ps[:, c:c + 1], self.modrow[0:1, c * 128:(c + 1) * 128], self.one1[0:1, 0:1], True, True)
                 for c in range(48)], R=[self.modrowb, self.cb], W=[pb])
        self.tt(self.modT, ps[:, 0:48], self.vecs[:, V_ADAB:V_ADAB + 48], ALU.add, R=[pb, self.vecsb], W=[self.modb])
        self.stt(self.mods[:, 0:8], self.modT[:, 8:16], 1.0, self.vecs[:, V_N1G:V_N1G + 8], ALU.add, ALU.mult,
                 R=[self.modb, self.vecsb], W=[self.modb])
        self.stt(self.mods[:, 8:16], self.modT[:, 32:40], 1.0, self.vecs[:, V_N2G:V_N2G + 8], ALU.add, ALU.mult,
                 R=[self.modb, self.vecsb], W=[self.modb])
        self.A.release(mk)
        P.set_fence()

    def norm(self, Acols, Bcols, inplace=False):
        A, P = self.A, self.P
        m = A.mark()
        sq = [A.alloc([512], BF16) for _ in range(4)]
        sqb = P.bufs("sq", 4)
        rs = [A.alloc([512], F32) for _ in range(2)]
        rsb = P.bufs("rs", 2)
        tmp = [A.alloc([512], F32) for _ in range(3)]
        tmpb = P.bufs("ntmp", 3)
        ti = 0
        for tb in range(4):
            sl = slice(tb * 512, (tb + 1) * 512)
            ps, pb = self.psum()
            for k in range(8):
                j = (tb * 8 + k) % 4
                self.act(sq[j], self.xT[:, k, sl], AF.Square, R=[self.xb[k][tb]], W=[sqb[j]])
                self.mm([(ps, self.ones, sq[j], k == 0, k == 7)], R=[sqb[j], self.cb], W=[pb])
            r = rs[tb % 2]; rb = rsb[tb % 2]
            self.act(r, ps, AF.Sqrt, bias=self.eps_t, scale=1.0 / D, R=[pb, self.cb], W=[rb])
            self.recip(r, r, R=[rb], W=[rb])
            for k in range(8):
                if inplace:
                    self.stt(self.xT[:, k, sl], self.xT[:, k, sl], Acols[:, k:k + 1], r, ALU.mult, ALU.mult,
                             R=[self.xb[k][tb], rb, self.fingb], W=[self.xb[k][tb]])
                else:
                    t = tmp[ti % 3]; tbf = tmpb[ti % 3]; ti += 1
                    self.stt(t, self.xT[:, k, sl], Acols[:, k:k + 1], r, ALU.mult, ALU.mult,
                             R=[self.xb[k][tb], rb, self.modb], W=[tbf])
                    self.act(self.hT[:, k, sl], t, AF.Identity, bias=Bcols[:, k:k + 1], scale=1.0,
                             R=[tbf, self.modb], W=[self.hb[k][tb]])
        A.release(m)
        P.set_fence()

    def layer(self, l):
        A, P, Dm = self.A, self.P, self.dram
        if l == 0:
            self.eps_t = A.alloc([8], F32)[:, 0:1]
            self.memset(self.eps_t, EPS, W=[P.buf("eps")])
        self.modulation(l)
        self.norm(self.mods[:, 0:8], self.modT[:, 0:8])
        self.dump(f"h{l}", self.hT, [b for r in self.hb for b in r], [128, 8, T])
        self.dump(f"mod{l}", self.modT, [self.modb], [128, 48])
        mtop = A.mark()
        ys = []
        for name in ("hy", "mla", "ret", "conf"):
            yT = A.alloc([2, T], BF16)
            yb = P.bufs("y" + name, 4)
            m = A.mark()
            getattr(self, "branch_" + name)(l, yT, yb)
            A.release(m)
            P.set_fence()
            ys.append((yT, yb))
            self.dump(f"y{name}{l}", yT, yb, [128, 2, T])
        self.merge(l, ys)
        A.release(mtop)
        P.set_fence()
        self.norm(self.mods[:, 8:16], self.modT[:, 24:32])
        self.ffn(l)
        self.dump(f"x{l}", self.xT, [b for r in self.xb for b in r], [128, 8, T])

    def zero_branch(self, yT, yb):
        for tb in range(4):
            self.memset(yT[:, :, tb * 512:(tb + 1) * 512], 0.0, W=[yb[tb]])

    def branch_hy(self, l, yT, yb):
        A, P, Dm = self.A, self.P, self.dram
        V = self.vecs
        FLG = self.flags
        TWO_PI = 2.0 * math.pi
        hall = lambda tb: [self.hb[k][tb] for k in range(8)]
        vT = A.alloc([2, T], BF16); vTb = P.bufs("hvT", 4)
        Ksp = A.alloc([16, 2, 256], BF16); kspb = P.buf("Ksp")
        h2all = A.alloc([T], BF16); h2allb = P.bufs("h2all", 4)
        w1 = A.alloc([64], F32); w2 = A.alloc([64], F32); wsb = P.buf("hyw")
        P.dma(lambda h: h.dma_start(out=w1[0:33, :], in_=Dm["hy_w1"][l]), writes=[wsb], queue="sp")
        P.dma(lambda h: h.dma_start(out=w2[0:64, :], in_=Dm["hy_w2"][l]), writes=[wsb], queue="sp")

        def hy_proj(which, dst, dstb):
            m = A.mark()
            wv, wvb = self.win_load(l, which * 256, 256)
            upad = A.alloc([NSEG, 258], BF16); upb = P.buf("upad")
            tmp = A.alloc([4, 256], F32); tmpb = P.buf("hytmp")
            mcol = FLG[:, FL_HALO:FL_HALO + 1]
            for cc in range(2):
                ch = which * 2 + cc
                for tb in range(4):
                    sl = slice(tb * 512, (tb + 1) * 512)
                    ps, pb = self.psum()
                    self.mm([(ps, wv[:, k, cc * 128:(cc + 1) * 128], self.hT[:, k, sl], k == 0, k == 7) for k in range(8)],
                            R=[wvb] + hall(tb), W=[pb])
                    self.cp(upad[:, 2 * tb:2 * tb + 2, 1:257], ps.rearrange("p (s t) -> p s t", s=2), R=[pb], W=[upb], eng="act")
                self.memset(upad[:, 0, 0:1], 0.0, W=[upb])
                self.memset(upad[:, 7, 257:258], 0.0, W=[upb])
                self.ts(upad[:, 1:8, 0:1], upad[:, 0:7, 256:257], mcol, None, ALU.mult, R=[upb, self.flagsb], W=[upb])
                self.ts(upad[:, 0:7, 257:258], upad[:, 1:8, 1:2], mcol, None, ALU.mult, R=[upb, self.flagsb], W=[upb])
                for hf in range(2):
                    sg = slice(hf * 4, hf * 4 + 4)
                    self.act(tmp, upad[:, sg, 1:257], AF.Identity, bias=V[:, V_HYCB + ch:V_HYCB + ch + 1],
                             scale=V[:, V_HYCW + 6 + ch:V_HYCW + 7 + ch], R=[upb, self.vecsb], W=[tmpb])
                    self.stt(tmp, upad[:, sg, 0:256], V[:, V_HYCW + ch:V_HYCW + ch + 1], tmp, ALU.mult, ALU.add,
                             R=[upb, tmpb, self.vecsb], W=[tmpb])
                    self.stt(dst[:, cc, hf * 1024:(hf + 1) * 1024].rearrange("p (s t) -> p s t", s=4), upad[:, sg, 2:258],
                             V[:, V_HYCW + 12 + ch:V_HYCW + 13 + ch], tmp, ALU.mult, ALU.add,
                             R=[upb, tmpb, self.vecsb], W=[dstb[2 * hf], dstb[2 * hf + 1]])
            A.release(m)

        def filt(o):
            m = A.mark()
            w3 = A.alloc([512], BF16); w3b = P.buf("w3")
            P.dma(lambda h: h.dma_start(out=w3[0:64, :], in_=Dm["hy_w3"][l][:, o * 512:(o + 1) * 512]), writes=[w3b], queue="pool")
            hp = A.alloc([16, 256], BF16); hm = A.alloc([16, 256], BF16); hpb = P.bufs("hp", 16)
            dec = [A.alloc([256], F32) for _ in range(2)]; decb = P.bufs("dec", 2)
            if o == 0:
                zt = [A.alloc([512], F32) for _ in range(2)]; ztb = P.bufs("zt", 2)
                u = A.alloc([512], F32); kf = A.alloc([512], F32); ki = A.alloc([512], I32)
                h1 = A.alloc([512], F32)
                mb = P.buf("mlp")
            hds = [A.alloc([2, 256], F32) for _ in range(3)]; hdbs = P.bufs("hd", 3)
            abs_ = [A.alloc([512], BF16) for _ in range(2)]; abbs = P.bufs("ab", 2)
            rn = A.alloc([256], F32); rnb = P.buf("rn")
            psN, psNb, iN = self.psum_hold()

            def sin_layer(ps, pb, bcol, out, outb):
                self.ts(u[0:64, :], ps[0:64, :], bcol, 1.0 / TWO_PI, ALU.add, ALU.mult, R=[pb, self.vecsb], W=[mb])
                self.cp(ki[0:64, :], u[0:64, :], R=[mb], W=[mb])
                self.cp(kf[0:64, :], ki[0:64, :], R=[mb], W=[mb])
                self.tt(u[0:64, :], u[0:64, :], kf[0:64, :], ALU.subtract, R=[mb], W=[mb])
                self.act(out[0:64, :], u[0:64, :], AF.Sin, scale=TWO_PI, R=[mb], W=[outb])

            for jb in range(4):
                h2, h2b = h2all[:, jb * 512:(jb + 1) * 512], h2allb[jb]
                if o == 0:
                    z, zb = zt[jb % 2], ztb[jb % 2]
                    P.dma(lambda h, z=z, jb=jb: h.dma_start(out=z[0:33, :], in_=Dm["zT"][:, jb * 512:(jb + 1) * 512]), writes=[zb], queue="sp")
                    ps, pb = self.psum()
                    self.mm([(ps[0:64, :], w1[0:33, :], z[0:33, :], True, True)], R=[wsb, zb], W=[pb])
                    sin_layer(ps, pb, V[0:64, V_HB1:V_HB1 + 1], h1, mb)
                    ps, pb = self.psum()
                    self.mm([(ps[0:64, :], w2[0:64, :], h1[0:64, :], True, True)], R=[wsb, mb], W=[pb])
                    sin_layer(ps, pb, V[0:64, V_HB2:V_HB2 + 1], h2, h2b)
                for ii in range(4):
                    i = jb * 4 + ii
                    hd, hdb = hds[i % 3], hdbs[i % 3]
                    ab, abb = abs_[i % 2], abbs[i % 2]
                    dc, dcb = dec[i % 2], decb[i % 2]
                    P.dma(lambda h, dc=dc, i=i: h.dma_start(out=dc, in_=Dm["decay"][:, i, :]), writes=[dcb], queue="sp")
                    p3, p3b = self.psum()
                    self.mm([(p3, h2[0:64, ii * 128:(ii + 1) * 128], w3[0:64, :], True, True)], R=[h2b, w3b], W=[p3b])
                    for dr in range(2):
                        self.tt(hd[:, dr, :], p3[:, dr * 256:(dr + 1) * 256], dc, ALU.mult, R=[p3b, dcb], W=[hdb])
                    self.act(ab, hd.rearrange("p a c -> p (a c)"), AF.Abs, R=[hdb], W=[abb])
                    self.mm([(psN, self.ones, ab, i == 0, i == 15)], R=[abb, self.cb], W=[psNb])
                    m0 = FLG[:, FL_M0:FL_M0 + 1] if i == 0 else 1.0
                    nm0 = FLG[:, FL_M0 + 1:FL_M0 + 2] if i == 0 else -1.0
                    self.stt(hp[:, i, :], hd[:, 1, :], m0, hd[:, 0, :], ALU.mult, ALU.add, R=[hdb, self.flagsb], W=[hpb[i]])
                    self.stt(hm[:, i, :], hd[:, 1, :], nm0, hd[:, 0, :], ALU.mult, ALU.add, R=[hdb, self.flagsb], W=[hpb[i]])
            self.cp(rn, psN[:, 0:256], R=[psNb], W=[rnb], eng="act")
            self.tt(rn, rn, psN[:, 256:512], ALU.add, R=[rnb, psNb], W=[rnb])
            self.recip(rn, rn, R=[rnb], W=[rnb])
            self.ps_held.discard(iN)
            for j in range(16):
                cl, clb = self.wload(Dm["ClT"][j], 4096, cast=False)
                cl4 = cl.rearrange("p (t c f) -> p t c f", t=16, c=2)
                ps, pb = self.psum()
                mms = []
                for cs_ in range(2):
                    src = hp if cs_ == 0 else hm
                    for tc in range(16):
                        mms.append((ps[:, cs_ * 256:(cs_ + 1) * 256], cl4[:, tc, cs_, :], src[:, tc, :], tc == 0, tc == 15))
                self.mm(mms, R=[clb] + hpb, W=[pb])
                for cs_ in range(2):
                    self.tt(Ksp[:, j, cs_, :], ps[:, cs_ * 256:(cs_ + 1) * 256], rn, ALU.mult, R=[pb, rnb], W=[kspb])
            A.release(m)
            P.set_fence()

        def conv(o, xwhich, final):
            m = A.mark()
            vtok = A.alloc([16, 256], BF16); vtkb = P.buf("vtok")
            Y = A.alloc([32, 256], BF16); Yb = P.buf("Y")
            m1 = [A.alloc([512], F32) for _ in range(2)]; m2 = [A.alloc([512], F32) for _ in range(2)]
            m1b = P.bufs("m1", 2); m2b = P.bufs("m2", 2)
            for i in range(16):
                ps, pb = self.psum()
                self.mm([(ps[:, cc * 128:(cc + 1) * 128], vT[:, cc, i * 128:(i + 1) * 128], self.ident, True, True) for cc in range(2)],
                        R=[vTb[i // 4], self.cb], W=[pb])
                self.cp(vtok[:, i, :], ps[:, 0:256], R=[pb], W=[vtkb], eng=("act" if i % 2 else "dve"))
            for j in range(16):
                fw, fwb = self.wload(Dm["FwT"][j], 4096, cast=False)
                fw4 = fw.rearrange("p (t c f) -> p t c f", t=16, c=2)
                ps, pb = self.psum()
                mms = []
                for cs_ in range(2):
                    for tc in range(16):
                        mms.append((ps[:, cs_ * 256:(cs_ + 1) * 256], fw4[:, tc, cs_, :], vtok[:, tc, :], tc == 0, tc == 15))
                self.mm(mms, R=[fwb, vtkb], W=[pb])
                a1, a1b = m1[j % 2], m1b[j % 2]
                a2, a2b = m2[j % 2], m2b[j % 2]
                self.tt(a1, ps, Ksp[:, j, :, :].rearrange("p a c -> p (a c)"), ALU.mult, R=[pb, kspb], W=[a1b])
                self.tt(a2[:, 0:256], ps[:, 0:256], Ksp[:, j, 1, :], ALU.mult, R=[pb, kspb], W=[a2b])
                self.tt(a2[:, 256:512], ps[:, 256:512], Ksp[:, j, 0, :], ALU.mult, R=[pb, kspb], W=[a2b])
                self.tt(Y[:, j, :], a1[:, 0:256], a1[:, 256:512], ALU.subtract, R=[a1b], W=[Yb], eng="pool")
                self.tt(Y[:, 16 + j, :], a2[:, 0:256], a2[:, 256:512], ALU.add, R=[a2b], W=[Yb], eng="pool")
            for th in range(2):
                accs = [self.psum_hold() for _ in range(4)]
                for g in range(8):
                    gi, gib = self.wload(Dm["GiT"][th, g], 4096, cast=False)
                    gi3 = gi.rearrange("p (r t) -> p r t", r=4)
                    mms = []
                    for rr in range(4):
                        r = g * 4 + rr
                        for n in range(4):
                            cc, tbb = divmod(n, 2)
                            mms.append((accs[n][0], Y[:, r, cc * 128:(cc + 1) * 128], gi3[:, rr, tbb * 512:(tbb + 1) * 512], r == 0, r == 31))
                    self.mm(mms, R=[gib, Yb], W=[a[1] for a in accs])
                for n in range(4):
                    cc, tbb = divmod(n, 2)
                    tb = th * 2 + tbb
                    sl = slice(tb * 512, (tb + 1) * 512)
                    self.stt(vT[:, cc, sl], vT[:, cc, sl], V[:, V_HYBIAS + o * 2 + cc:V_HYBIAS + o * 2 + cc + 1], accs[n][0], ALU.mult, ALU.add,
                             R=[vTb[tb], accs[n][1], self.vecsb], W=[vTb[tb]])
                    self.ps_held.discard(accs[n][2])
            A.release(m)
            P.set_fence()
            m = A.mark()
            xT = A.alloc([2, T], BF16); xTb = P.bufs("hxT", 4)
            hy_proj(xwhich, xT, xTb)
            for tb in range(4):
                sl = slice(tb * 512, (tb + 1) * 512)
                dst = yT if final else vT
                dstb = yb if final else vTb
                self.tt(dst[:, :, sl], xT[:, :, sl], vT[:, :, sl], ALU.mult, R=[xTb[tb], vTb[tb]], W=[dstb[tb]])
            A.release(m)
            P.set_fence()

        hy_proj(0, vT, vTb)
        P.set_fence()
        filt(0)
        conv(0, 1, False)
        filt(1)
        conv(1, 2, True)

    def branch_mla(self, l, yT, yb):
        A, P, Dm = self.A, self.P, self.dram
        V = self.vecs
        scale = (64 + 32) ** -0.5
        wq, wqb = self.wload(Dm["w_uq"][l], 768)
        wq3 = wq[:, 0:768].rearrange("p (k c) -> p k c", k=2)
        WqA = A.alloc([4, 2, 128], BF16); WqB = A.alloc([4, 2, 64], BF16)
        wb_ = P.buf("mlaw")
        self.memset(WqA, 0.0, W=[wb_]); self.memset(WqB, 0.0, W=[wb_])
        for h in range(4):
            c0 = h * 96
            self.cp(WqA[:, h, :, 64:128], wq3[:, :, c0:c0 + 64], R=[wqb], W=[wb_])
            self.cp(WqA[:, h, :, 32:64], wq3[:, :, c0 + 64:c0 + 96], R=[wqb], W=[wb_], eng="act")
            self.ts(WqB[:, h, :, 32:48], wq3[:, :, c0 + 80:c0 + 96], -1.0, None, ALU.mult, R=[wqb], W=[wb_])
            self.cp(WqB[:, h, :, 48:64], wq3[:, :, c0 + 64:c0 + 80], R=[wqb], W=[wb_], eng="act")
        wkv, wkvb = self.wload(Dm["w_ukv"][l], 512)
        WkN = A.alloc([4, 128], BF16); WV = A.alloc([4, 64], BF16)
        self.memset(WkN, 0.0, W=[wb_])
        wkv3 = wkv[:, 0:512].rearrange("p (h c) -> p h c", h=4)
        self.cp(WkN[:, :, 64:128], wkv3[:, :, 0:64], R=[wkvb], W=[wb_])
        self.cp(WV, wkv3[:, :, 64:128], R=[wkvb], W=[wb_], eng="act")
        wi, wib = self.win_load(l, 768, 416)
        WkrA = A.alloc([8, 64], BF16); WkrB = A.alloc([8, 64], BF16)
        self.memset(WkrA, 0.0, W=[wb_]); self.memset(WkrB, 0.0, W=[wb_])
        self.cp(WkrA[:, :, 32:64], wi[:, :, 384:416], R=[wib], W=[wb_])
        self.ts(WkrB[:, :, 32:48], wi[:, :, 400:416], -1.0, None, ALU.mult, R=[wib], W=[wb_])
        self.cp(WkrB[:, :, 48:64], wi[:, :, 384:400], R=[wib], W=[wb_], eng="act")
        csl = [A.alloc([2, 512], F32) for _ in range(2)]; cslb = P.bufs("ropecs", 2)
        self._ncs = 0

        def load_cs(tb):
            j = self._ncs % 2; self._ncs += 1
            c, cb_ = csl[j], cslb[j]
            P.dma(lambda h: h.dma_start(out=c[32:64, :, :], in_=Dm["ropeCS"][:, :, tb * 512:(tb + 1) * 512].rearrange("a p t -> p a t")),
                  writes=[cb_], queue="sp")
            return c, cb_
        Ka = [A.alloc([NKEY], BF16)]; Kab = P.bufs("Kaug", 1)
        Qa = [A.alloc([T], BF16)]; Qab = P.bufs("Qaug", 1)
        for i in range(1):
            P.dma(lambda h, i=i: h.dma_start(out=Ka[i][0:32, :], in_=Dm["oh"]), writes=[Kab[i]], queue="pool")
            P.dma(lambda h, i=i: h.dma_start(out=Qa[i][0:32, :], in_=Dm["oh"][:, 0:T]), writes=[Qab[i]], queue="pool")
            P.dma(lambda h, i=i: h.dma_start(out=Ka[i][32:64, T:NKEY], in_=Dm["kropeC"][l]), writes=[Kab[i]], queue="pool")
        ckn = A.alloc([NKEY], BF16); cknb = P.bufs("ckn", 5)
        P.dma(lambda h: h.dma_start(out=ckn[:, T:NKEY], in_=Dm["cacheT"][l]), writes=[cknb[4]], queue="pool")
        cqn = A.alloc([2, T], BF16); cqnb = P.bufs("cqn", 4)
        Va = A.alloc([18, 4, 65], BF16); Vab = P.buf("Vaug")
        self.memset(Va[:, :, :, 64:65], 1.0, W=[Vab])
        sq = [A.alloc([512], BF16) for _ in range(2)]; sqb = P.bufs("msq", 2)
        rs = [A.alloc([512], F32) for _ in range(2)]; rsb = P.bufs("mrs", 2)
        t1 = A.alloc([512], F32); t2 = A.alloc([512], F32); t12b = P.bufs("mt", 2)
        stg = A.alloc([4, 128], F32); stgb = P.buf("stg")
        stk = A.alloc([4, 32], F32); stkb = P.buf("stk")
        krr = A.alloc([512], BF16); krrb = P.buf("krr")
        hall = lambda tb: [self.hb[k][tb] for k in range(8)]
        stop = 99
        sub = ""
        if stop <= 1:
            self.zero_branch(yT, yb); return
        for tb in range(4):
            sl = slice(tb * 512, (tb + 1) * 512)
            pcs = []
            pq, pqb = self.psum()
            for c in range(2):
                pc, pcb = self.psum()
                self.mm([(pc, wi[:, k, c * 128:(c + 1) * 128], self.hT[:, k, sl], k == 0, k == 7) for k in range(8)],
                        R=[wib] + hall(tb), W=[pcb])
                self.act(sq[c], pc, AF.Square, R=[pcb], W=[sqb[c]])
                self.mm([(pq, self.ones, sq[c], c == 0, c == 1)], R=[sqb[c], self.cb], W=[pqb])
                pcs.append((pc, pcb))
            self.act(rs[0], pq, AF.Sqrt, bias=self.eps_t, scale=1.0 / 256, R=[pqb], W=[rsb[0]])
            self.recip(rs[0], rs[0], R=[rsb[0]], W=[rsb[0]])
            for c in range(2):
                self.stt(cqn[:, c, sl], pcs[c][0], V[:, V_QN + c:V_QN + c + 1], rs[0], ALU.mult, ALU.mult,
                         R=[pcs[c][1], rsb[0], self.vecsb], W=[cqnb[tb]])
            pc, pcb = self.psum()
            self.mm([(pc, wi[:, k, 256:384], self.hT[:, k, sl], k == 0, k == 7) for k in range(8)], R=[wib] + hall(tb), W=[pcb])
            self.act(sq[0], pc, AF.Square, R=[pcb], W=[sqb[0]])
            pq, pqb = self.psum()
            self.mm([(pq, self.ones, sq[0], True, True)], R=[sqb[0], self.cb], W=[pqb])
            self.act(rs[1], pq, AF.Sqrt, bias=self.eps_t, scale=1.0 / 128, R=[pqb], W=[rsb[1]])
            self.recip(rs[1], rs[1], R=[rsb[1]], W=[rsb[1]])
            self.stt(ckn[:, sl], pc, V[:, V_KVN:V_KVN + 1], rs[1], ALU.mult, ALU.mult, R=[pcb, rsb[1], self.vecsb], W=[cknb[tb]])
            if "nock" in sub:
                continue
            pt, ptb = self.psum()
            self.mm([(pt[:, i * 128:(i + 1) * 128], ckn[:, tb * 512 + i * 128: tb * 512 + (i + 1) * 128], self.ident, True, True)
                     for i in range(4)], R=[cknb[tb], self.cb], W=[ptb])
            self.cp(stg, pt.rearrange("p (i c) -> p i c", i=4), R=[ptb], W=[stgb], eng="act")
            P.dma(lambda h, tb=tb: h.dma_start(out=self.outs["ckv_out"][l, tb * 512:(tb + 1) * 512, :].rearrange("(i p) c -> p i c", p=128), in_=stg),
                  reads=[stgb], queue="sp", out=True)
            if "nokr" in sub:
                continue
            pa, pab = self.psum()
            self.mm([(pa[0:64, :], WkrA[:, k, :], self.hT[:, k, sl], k == 0, k == 7) for k in range(8)], R=[wb_] + hall(tb), W=[pab])
            pb2, pbb = self.psum()
            self.mm([(pb2[0:64, :], WkrB[:, k, :], self.hT[:, k, sl], k == 0, k == 7) for k in range(8)], R=[wb_] + hall(tb), W=[pbb])
            if "mmonly" in sub:
                continue
            if "noact" not in sub:
                self.cp(krr[32:64, :], pa[32:64, :], R=[pab], W=[krrb], eng="act")
            if "nodve" in sub:
                continue
            if "nocs" in sub:
                self.tt(t1[32:64, :], pa[32:64, :], pa[32:64, :], ALU.mult, R=[pab], W=[t12b[0]])
                self.tt(t2[32:64, :], pb2[32:64, :], pb2[32:64, :], ALU.mult, R=[pbb], W=[t12b[1]])
                continue
            cs, csb = load_cs(tb)
            self.tt(t1[32:64, :], pa[32:64, :], cs[32:64, 0, :], ALU.mult, R=[pab, csb], W=[t12b[0]])
            self.tt(t2[32:64, :], pb2[32:64, :], cs[32:64, 1, :], ALU.mult, R=[pbb, csb], W=[t12b[1]])
            if "noadd" in sub:
                continue
            self.tt(Ka[0][32:64, sl], t1[32:64, :], t2[32:64, :], ALU.add, R=t12b, W=[Kab[0]])
            if "nokt" in sub:
                continue
            pt, ptb = self.psum()
            self.mm([(pt[:, i * 32:(i + 1) * 32], krr[32:64, i * 128:(i + 1) * 128], self.ident[32:64, 32:64], True, True)
                     for i in range(4)], R=[krrb, self.cb], W=[ptb])
            self.cp(stk, pt[:, 0:128].rearrange("p (i c) -> p i c", i=4), R=[ptb], W=[stkb], eng="act")
            P.dma(lambda h, tb=tb: h.dma_start(out=self.outs["krope_out"][l, tb * 512:(tb + 1) * 512, :].rearrange("(i p) c -> p i c", p=128), in_=stk),
                  reads=[stkb], queue="sp", out=True)
        if stop <= 2:
            self.zero_branch(yT, yb); return
        for kt in range(18):
            pv, pvb = self.psum()
            src_b = cknb[kt // 4] if kt < 16 else cknb[4]
            self.mm([(pv[:, 0:256], ckn[:, kt * 128:(kt + 1) * 128], WV.rearrange("p h c -> p (h c)"), True, True)],
                    R=[src_b, wb_], W=[pvb])
            self.cp(Va[:, kt, :, 0:64], pv[:, 0:256].rearrange("p (h c) -> p h c", h=4), R=[pvb], W=[Vab],
                    eng=("act" if kt % 2 else "dve"))
        if stop <= 3:
            self.zero_branch(yT, yb); return
        PT = [A.alloc([512], BF16) for _ in range(4)]; PTb = P.bufs("PT", 4)
        assert len(PT) == 4
        rrow = A.alloc([512], F32); rrowb = P.buf("rrow")
        rbs = A.alloc([512], F32); rbsb = P.buf("rbs")
        npt = 0
        for h in range(4):
            Kh, Khb = Ka[0], Kab[0]
            Qh, Qhb = Qa[0], Qab[0]
            for kb_ in range(5):
                n0 = kb_ * 512
                nn = 512 if kb_ < 4 else 256
                pk, pkb = self.psum()
                self.mm([(pk[:, 0:nn], WkN[:, h, :], ckn[:, n0:n0 + nn], True, True)], R=[wb_, cknb[kb_]], W=[pkb])
                self.cp(Kh[64:128, n0:n0 + nn], pk[64:128, 0:nn], R=[pkb], W=[Khb], eng=("act" if kb_ % 2 else "dve"))
            for tb in range(4):
                sl = slice(tb * 512, (tb + 1) * 512)
                pa, pab = self.psum()
                self.mm([(pa, WqA[:, h, k, :], cqn[:, k, sl], k == 0, k == 1) for k in range(2)], R=[wb_, cqnb[tb]], W=[pab])
                pb2, pbb = self.psum()
                self.mm([(pb2[0:64, :], WqB[:, h, k, :], cqn[:, k, sl], k == 0, k == 1) for k in range(2)], R=[wb_, cqnb[tb]], W=[pbb])
                self.cp(Qh[64:128, sl], pa[64:128, :], R=[pab], W=[Qhb], eng="act")
                cs, csb = load_cs(tb)
                self.tt(t1[32:64, :], pa[32:64, :], cs[32:64, 0, :], ALU.mult, R=[pab, csb], W=[t12b[0]])
                self.tt(t2[32:64, :], pb2[32:64, :], cs[32:64, 1, :], ALU.mult, R=[pbb, csb], W=[t12b[1]])
                self.tt(Qh[32:64, sl], t1[32:64, :], t2[32:64, :], ALU.add, R=t12b, W=[Qhb], eng="pool")
            if stop <= 4:
                if h == 3:
                    self.zero_branch(yT, yb)
                continue
            def tail(tb, po, pob, ih):
                sl = slice(tb * 512, (tb + 1) * 512)
                self.recip(rrow[64:65, :], po[64:65, :], R=[pob], W=[rrowb])
                pr, prb = self.psum()
                self.mm([(pr[0:64, :], self.onesf[64:65, 0:64], rrow[64:65, :], True, True)], R=[rrowb, self.cb], W=[prb])
                self.cp(rbs[0:64, :], pr[0:64, :], R=[prb], W=[rbsb], eng="act")
                p0 = (h % 2) * 64
                self.tt(yT[p0:p0 + 64, h // 2, sl], po[0:64, :], rbs[0:64, :], ALU.mult, R=[pob, rbsb], W=[yb[tb]])
                self.ps_held.discard(ih)

            def score(tb, kt):
                sl = slice(tb * 512, (tb + 1) * 512)
                pS, pSb = self.psum()
                self.mm([(pS, Kh[:, kt * 128:(kt + 1) * 128], Qh[:, sl], True, True)], R=[Khb, Qhb], W=[pSb])
                return pS, pSb

            pend = None
            for tb in range(4):
                po, pob, ih = self.psum_hold()
                q = [score(tb, 0), score(tb, 1), score(tb, 2)]
                if pend is not None:
                    tail(*pend)
                for kt in range(18):
                    pS, pSb = q.pop(0)
                    pt_, ptb_ = PT[npt % 4], PTb[npt % 4]; npt += 1
                    self.act(pt_, pS, AF.Exp, bias=self.negbig_t, scale=scale, R=[pSb], W=[ptb_])
                    if kt + 3 < 18:
                        q.append(score(tb, kt + 3))
                    self.mm([(po[0:65, :], Va[:, kt, h, :], pt_, kt == 0, kt == 17)], R=[Vab, ptb_], W=[pob])
                pend = (tb, po, pob, ih)
            tail(*pend)

    def branch_ret(self, l, yT, yb):
        A, P, Dm = self.A, self.P, self.dram
        V = self.vecs
        FLG = self.flags
        Dtab = A.alloc([4, 128], BF16)
        Qtab = A.alloc([4, 128], F32)
        Vtok = A.alloc([16, 256], BF16); vtb = P.bufs("vtok", 16)
        Sin = A.alloc([16, 256], BF16); sinb = [[P.buf(f"sin{i}_{hf}") for hf in range(2)] for i in range(16)]
        m1 = A.mark()
        KTAB = A.alloc([2, 4], F32)
        GCt = A.alloc([4, 64], F32)
        gc4 = A.alloc([4], F32)
        rc = A.alloc([642], F32); rcb = P.buf("rcon")
        P.dma(lambda h: h.dma_start(out=rc, in_=Dm["rcon"]), writes=[rcb], queue="sp")
        s0t = A.alloc([256], F32); s0b = P.buf("s0")
        P.dma(lambda h: h.dma_start(out=s0t, in_=Dm["s0"][l]), writes=[s0b], queue="sp")
        lg = A.alloc([12], F32); lgb = P.buf("lg")
        self.act(lg[:, 0:8], V[:, V_RDALL:V_RDALL + 8], AF.Exp, scale=-1.0, R=[self.vecsb], W=[lgb])
        self.act(lg[:, 8:12], V[:, V_RDFB:V_RDFB + 4], AF.Exp, scale=-1.0, R=[self.vecsb], W=[lgb])
        self.act(lg, lg, AF.Ln, bias=self.one1[:, 0:1], scale=1.0, R=[lgb, self.cb], W=[lgb])
        self.ts(lg, lg, -1.0, None, ALU.mult, R=[lgb], W=[lgb])
        tb_ = P.buf("rtab")
        e1 = A.alloc([128], F32); e2 = A.alloc([128], F32); eb = P.bufs("re", 2)
        for h in range(4):
            self.act(e1, rc[:, 0:128], AF.Exp, scale=lg[:, h:h + 1], R=[rcb, lgb], W=[eb[0]])
            self.stt(e1, e1, 0.125, rc[:, 128:256], ALU.mult, ALU.mult, R=[eb[0], rcb], W=[eb[0]])
            self.act(e2, rc[:, 256:384], AF.Exp, scale=lg[:, 4 + h:5 + h], R=[rcb, lgb], W=[eb[1]])
            self.stt(e2, e2, 0.125, rc[:, 384:512], ALU.mult, ALU.mult, R=[eb[1], rcb], W=[eb[1]])
            self.tt(Dtab[:, h, :], e1, e2, ALU.add, R=eb, W=[tb_])
            self.act(Qtab[:, h, :], rc[:, 512:640], AF.Exp, scale=lg[:, 8 + h:9 + h], R=[rcb, lgb], W=[tb_])
        for dr in range(2):
            self.act(KTAB[:, dr, :], lg[:, dr * 4:(dr + 1) * 4], AF.Exp, scale=rc[:, 640 + dr:641 + dr], R=[rcb, lgb], W=[tb_])
        self.ts(KTAB, KTAB, 0.125, None, ALU.mult, R=[tb_], W=[tb_])
        self.act(gc4, lg[:, 8:12], AF.Exp, scale=128.0, R=[lgb], W=[tb_])
        self.memset(GCt, 1.0, W=[tb_])
        for h in range(4):
            self.ts(GCt[:, h, :], GCt[:, h, :], gc4[:, h:h + 1], None, ALU.mult, R=[tb_], W=[tb_])
        wqk, wqkb = self.win_load(l, 1184, 512)
        wvg, wvgb = self.win_load(l, 1696, 512)
        Wk = wqk[:, :, 256:512]; wkb_ = wqkb
        Wvg = wvg; wvgb2 = wvgb
        hall = lambda tb: [self.hb[k][tb] for k in range(8)]
        KVst = A.alloc([16, 256], F32); kvb = P.bufs("kvst", 16)
        Kd2 = [A.alloc([4, 2, 64], BF16) for _ in range(2)]; kd2b = P.bufs("kd2", 2)
        for i in range(16):
            tb = i // 4
            tsl = slice(i * 128, (i + 1) * 128)
            pk, pkb = self.psum()
            self.mm([(pk[:, 0:256], self.hT[:, k, tsl], Wk[:, k, :], k == 0, k == 7) for k in range(8)], R=[wkb_] + hall(tb), W=[pkb])
            pv, pvb = self.psum()
            self.mm([(pv[:, 0:256], self.hT[:, k, tsl], Wvg[:, k, 0:256], k == 0, k == 7) for k in range(8)], R=[wvgb2] + hall(tb), W=[pvb])
            self.cp(Vtok[:, i, :], pv[:, 0:256], R=[pvb], W=[vtb[i]], eng="act")
            kd = Kd2[i % 2]; kdb = kd2b[i % 2]
            for h in range(4):
                self.ts(kd[:, h, 0, :], pk[:, h * 64:(h + 1) * 64], KTAB[:, 0, h:h + 1], None, ALU.mult, R=[pkb, tb_], W=[kdb])
            for h in range(4):
                self.act(kd[:, h, 1, :], pk[:, h * 64:(h + 1) * 64], AF.Identity, scale=KTAB[:, 1, h:h + 1], R=[pkb, tb_], W=[kdb])
            pkv, pkvb = self.psum()
            self.mm([(pkv[:, h * 64:(h + 1) * 64], kd[:, h, :, :].rearrange("p a d -> p (a d)"), Vtok[:, i, h * 64:(h + 1) * 64], True, True)
                     for h in range(4)], R=[kdb, vtb[i]], W=[pkvb])
            self.cp(KVst[:, i, :], pkv[:, 0:256], R=[pkvb], W=[kvb[i]])
        Ust = A.alloc([8, 256], F32); ustbs = P.bufs("ust", 2)
        Utmp = A.alloc([256], F32)
        Scur = A.alloc([256], F32); scb_ = P.bufs("scur", 2)
        tmp = A.alloc([256], F32)
        GC2 = GCt.rearrange("p h e -> p (h e)")
        orders = [list(range(16)), list(range(15, -1, -1))]
        for hf in range(2):
            r0, r1 = hf * 64, hf * 64 + 64
            self.cp(Scur[r0:r1, :], s0t[r0:r1, :], R=[s0b], W=[scb_[hf]])
            self.cp(Sin[r0:r1, orders[hf][0], :], s0t[r0:r1, :], R=[s0b], W=[sinb[orders[hf][0]][hf]], eng="act")
        for n in range(16):
            for hf in range(2):
                r0, r1 = hf * 64, hf * 64 + 64
                order = orders[hf]
                sb = scb_[hf]
                i = order[n]
                self.tt(tmp[r0:r1, :], Scur[r0:r1, :], GC2[r0:r1, :], ALU.mult, R=[sb, tb_], W=[sb])
                is_out = (i % 2 == 1) if hf == 0 else (i % 2 == 0)
                dst = Ust[r0:r1, i // 2, :] if is_out else Utmp[r0:r1, :]
                self.tt(dst, tmp[r0:r1, :], KVst[r0:r1, i, :], ALU.add, R=[sb, kvb[i]], W=[sb, ustbs[hf]])
                if n < 15:
                    nxt = order[n + 1]
                    kcol = (FL_KF + nxt) if hf == 0 else (FL_KB + nxt)
                    self.ts(Scur[r0:r1, :], dst, FLG[:, kcol:kcol + 1][r0:r1, :], None, ALU.mult, R=[sb, ustbs[hf]], W=[sb])
                    self.cp(Sin[r0:r1, nxt, :], Scur[r0:r1, :], R=[sb], W=[sinb[nxt][hf]], eng="act")
        P.dma(lambda h: h.dma_start(out=self.outs["st_out"][l], in_=Ust), reads=ustbs, queue="sp", out=True)
        A.release(m1)
        P.set_fence()
        wqk, wqkb = self.win_load(l, 1184, 512)
        wvg, wvgb = self.win_load(l, 1696, 512)
        Wk = wqk[:, :, 256:512]; wkb_ = wqkb
        Wvg = wvg; wvgb2 = wvgb
        Wq2 = A.alloc([8, 4, 128], BF16); wq2b = P.buf("wq2")
        qv = wqk[:, :, 0:256].rearrange("p k (h c) -> p k h c", h=4)
        self.cp(Wq2[:, :, :, 0:64], qv, R=[wqkb], W=[wq2b])
        self.cp(Wq2[:, :, :, 64:128], qv, R=[wqkb], W=[wq2b], eng="act")
        QT = A.alloc([4, 512], BF16); qtb = P.buf("QT")
        Qd2 = A.alloc([4, 512], BF16); qd2b = P.buf("Qd2")
        KT = A.alloc([4, 512], BF16); ktb = P.buf("KT")
        sg4 = A.alloc([4, 512], BF16); sg4b = P.buf("sg4")
        NBF = 2
        AT = [A.alloc([512], BF16) for _ in range(2)]; atb = P.bufs("AT", 2)
        o32s = [A.alloc([512], F32) for _ in range(NBF)]; o32bs = P.bufs("o32", NBF)
        obfs = [A.alloc([512], BF16) for _ in range(NBF)]; obfbs = P.bufs("obf", NBF)
        xcs = [A.alloc([512], F32)] * NBF; xcbs = [P.buf("rxc")] * NBF
        sqs = [A.alloc([512], BF16)] * NBF; sqbs = [P.buf("rsq")] * NBF
        rss = [A.alloc([512], F32)] * NBF; rsbs = [P.buf("rrs")] * NBF

        def stageA(tb, ci):
            i = tb * 4 + ci
            csl = slice(ci * 128, (ci + 1) * 128)
            pa, pab = self.psum()
            self.mm([(pa[:, h * 128:(h + 1) * 128], KT[0:64, h, csl], QT[0:64, h, csl], True, True) for h in range(4)],
                    R=[ktb, qtb], W=[pab])
            at = AT[i % 2]; atbb = atb[i % 2]
            self.tt(at, pa, Dtab.rearrange("p h n -> p (h n)"), ALU.mult, R=[pab, tb_], W=[atbb])
            po, pob = self.psum()
            mms = []
            for h in range(4):
                mms.append((po[0:64, h * 128:(h + 1) * 128], Vtok[:, i, h * 64:(h + 1) * 64], at[:, h * 128:(h + 1) * 128], True, False))
                mms.append((po[0:64, h * 128:(h + 1) * 128], Sin[:, i, h * 64:(h + 1) * 64], Qd2[:, h, csl], False, True))
            self.mm(mms, R=[vtb[i], atbb, sinb[i][0], sinb[i][1], qd2b], W=[pob])
            j = i % NBF
            self.cp(o32s[j][0:64, :], po[0:64, :], R=[pob], W=[o32bs[j]], eng="act")
            self.cp(obfs[j][0:64, :], po[0:64, :], R=[pob], W=[obfbs[j]], eng="act")

        def stageB(tb, ci):
            i = tb * 4 + ci
            j = i % NBF
            pm, pmb = self.psum()
            self.mm([(pm[0:64, :], self.ones[0:64, 0:64], obfs[j][0:64, :], True, True)], R=[obfbs[j], self.cb], W=[pmb])
            self.stt(xcs[j][0:64, :], pm[0:64, :], -1.0 / 64, o32s[j][0:64, :], ALU.mult, ALU.add, R=[pmb, o32bs[j]], W=[xcbs[j]])
            self.act(sqs[j][0:64, :], xcs[j][0:64, :], AF.Square, R=[xcbs[j]], W=[sqbs[j]])

        def stageC(tb, ci):
            i = tb * 4 + ci
            j = i % NBF
            csl = slice(ci * 128, (ci + 1) * 128)
            pv, pvb = self.psum()
            self.mm([(pv[0:64, :], self.ones[0:64, 0:64], sqs[j][0:64, :], True, True)], R=[sqbs[j], self.cb], W=[pvb])
            self.act(rss[j][0:64, :], pv[0:64, :], AF.Sqrt, bias=self.eps_t[0:64, :], scale=1.0 / 64, R=[pvb], W=[rsbs[j]])
            self.recip(rss[j][0:64, :], rss[j][0:64, :], R=[rsbs[j]], W=[rsbs[j]])
            self.tt(xcs[j][0:64, :], xcs[j][0:64, :], rss[j][0:64, :], ALU.mult, R=[xcbs[j], rsbs[j]], W=[xcbs[j]], eng="pool")
            for h in range(4):
                p0 = (h % 2) * 64
                self.tt(yT[p0:p0 + 64, h // 2, i * 128:(i + 1) * 128], xcs[j][0:64, h * 128:(h + 1) * 128], sg4[0:64, h, csl], ALU.mult,
                        R=[xcbs[j], sg4b], W=[yb[tb]])

        for tb in range(4):
            sl = slice(tb * 512, (tb + 1) * 512)
            for h in range(4):
                pq, pqb = self.psum()
                self.mm([(pq, Wq2[:, k, h, :], self.hT[:, k, sl], k == 0, k == 7) for k in range(8)], R=[wq2b] + hall(tb), W=[pqb])
                self.cp(QT[0:64, h, :], pq[0:64, :], R=[pqb], W=[qtb], eng="act")
                for ci in range(4):
                    self.tt(Qd2[:, h, ci * 128:(ci + 1) * 128], pq[:, ci * 128:(ci + 1) * 128], Qtab[:, h, :], ALU.mult,
                            R=[pqb, tb_], W=[qd2b])
                pk, pkb = self.psum()
                self.mm([(pk[0:64, :], Wk[:, k, h * 64:(h + 1) * 64], self.hT[:, k, sl], k == 0, k == 7) for k in range(8)], R=[wkb_] + hall(tb), W=[pkb])
                self.cp(KT[0:64, h, :], pk[0:64, :], R=[pkb], W=[ktb], eng="act")
                pg, pgb = self.psum()
                self.mm([(pg[0:64, :], Wvg[:, k, 256 + h * 64:256 + (h + 1) * 64], self.hT[:, k, sl], k == 0, k == 7) for k in range(8)], R=[wvgb2] + hall(tb), W=[pgb])
                self.act(sg4[0:64, h, :], pg[0:64, :], AF.Silu, R=[pgb], W=[sg4b])
            for s_ in range(6):
                if s_ < 4:
                    stageA(tb, s_)
                if 0 <= s_ - 2 < 4:
                    stageC(tb, s_ - 2)
                if 0 <= s_ - 1 < 4:
                    stageB(tb, s_ - 1)

    def win_load(self, l, c0, ncol):
        assert ncol * 8 <= RING_ELEMS
        i = self.ring_next
        self.ring_next = (i + 1) % RING_SLOTS
        dst = self.ring[i][:, 0:8 * ncol].rearrange("p (k c) -> p k c", k=8)
        src = self.dram["w_in"][l][:, :, c0:c0 + ncol]
        b = self.ringb[i]
        self.P.dma(lambda h: h.dma_start(out=dst, in_=src), writes=[b], queue="pool", nofence=True)
        return dst, b

    def branch_conf(self, l, yT, yb):
        A, P = self.A, self.P
        V = self.vecs
        zp = A.alloc([2, NSEG, 286], BF16)
        zpb = P.bufs("zp", 4)
        diag = A.alloc([62, 128], BF16)
        diagb = P.buf("diag")
        for k in range(31):
            for ch in range(2):
                self.ts(diag[:, k * 2 + ch, :], self.ident, V[:, V_CDW + k * 2 + ch:V_CDW + k * 2 + ch + 1], None,
                        ALU.mult, R=[self.cb, self.vecsb], W=[diagb])
        sig = [A.alloc([512], F32) for _ in range(2)]
        sigb = P.bufs("sig", 2)
        wa, wab = self.win_load(l, 2208, 512)
        n = 0
        for ch in range(2):
            for tb in range(4):
                sl = slice(tb * 512, (tb + 1) * 512)
                pa, pab = self.psum()
                self.mm([(pa, wa[:, k, ch * 128:(ch + 1) * 128], self.hT[:, k, sl], k == 0, k == 7) for k in range(8)],
                        R=[wab] + [self.hb[k][tb] for k in range(8)], W=[pab])
                pbb, pbbb = self.psum()
                self.mm([(pbb, wa[:, k, 256 + ch * 128:256 + (ch + 1) * 128], self.hT[:, k, sl], k == 0, k == 7) for k in range(8)],
                        R=[wab] + [self.hb[k][tb] for k in range(8)], W=[pbbb])
                s = sig[n % 2]; sb = sigb[n % 2]; n += 1
                self.act(s, pbb, AF.Sigmoid, R=[pbbb], W=[sb])
                self.tt(zp[:, ch, 2 * tb:2 * tb + 2, 15:271], pa.rearrange("p (s t) -> p s t", s=2),
                        s.rearrange("p (s t) -> p s t", s=2), ALU.mult, R=[pab, sb], W=[zpb[tb]])
        mcol = self.flags[:, FL_HALO:FL_HALO + 1]
        for ch in range(2):
            self.memset(zp[:, ch, 0, 0:15], 0.0, W=zpb)
            self.memset(zp[:, ch, 7, 271:286], 0.0, W=zpb)
            self.ts(zp[:, ch, 1:8, 0:15], zp[:, ch, 0:7, 256:271], mcol, None, ALU.mult, R=zpb + [self.flagsb], W=zpb)
            self.ts(zp[:, ch, 0:7, 271:286], zp[:, ch, 1:8, 15:30], mcol, None, ALU.mult, R=zpb + [self.flagsb], W=zpb)
        cv = A.alloc([2, 512], F32)
        cvb = P.bufs("cv", 2)
        xc = A.alloc([2, 512], F32)
        xcb = P.buf("xc")
        sq = A.alloc([2, 512], BF16)
        sqb = P.buf("csq")
        rs = A.alloc([512], F32)
        rsb = P.buf("crs")
        for tb in range(4):
            sl = slice(tb * 512, (tb + 1) * 512)
            for ch in range(2):
                ps, pb = self.psum()
                self.mm([(ps.rearrange("p (s t) -> p s t", s=2), diag[:, k * 2 + ch, :], zp[:, ch, 2 * tb:2 * tb + 2, k:k + 256], k == 0, k == 30)
                         for k in range(31)], R=[diagb] + zpb, W=[pb])
                self.act(cv[:, ch, :], ps, AF.Identity, bias=V[:, V_CDB + ch:V_CDB + ch + 1], R=[pb, self.vecsb], W=[cvb[ch]])
            pm, pmb = self.psum()
            cvh = sq
            for ch in range(2):
                self.cp(cvh[:, ch, :], cv[:, ch, :], R=[cvb[ch]], W=[sqb])
            self.mm([(pm, self.ones, cvh[:, ch, :], ch == 0, ch == 1) for ch in range(2)], R=[sqb, self.cb], W=[pmb])
            for ch in range(2):
                self.stt(xc[:, ch, :], pm, -1.0 / 256, cv[:, ch, :], ALU.mult, ALU.add, R=[pmb, cvb[ch]], W=[xcb])
            for ch in range(2):
                self.act(sq[:, ch, :], xc[:, ch, :], AF.Square, R=[xcb], W=[sqb])
            pv, pvb = self.psum()
            self.mm([(pv, self.ones, sq[:, ch, :], ch == 0, ch == 1) for ch in range(2)], R=[sqb, self.cb], W=[pvb])
            self.act(rs, pv, AF.Sqrt, bias=self.eps_t, scale=1.0 / 256, R=[pvb], W=[rsb])
            self.recip(rs, rs, R=[rsb], W=[rsb])
            for ch in range(2):
                self.stt(xc[:, ch, :], xc[:, ch, :], V[:, V_CLG + ch:V_CLG + ch + 1], rs, ALU.mult, ALU.mult,
                         R=[xcb, rsb, self.vecsb], W=[xcb])
                self.act(yT[:, ch, sl], xc[:, ch, :], AF.Silu, bias=V[:, V_CLB + ch:V_CLB + ch + 1], R=[xcb, self.vecsb], W=[yb[tb]])

    def merge(self, l, ys):
        A, P, Dm = self.A, self.P, self.dram
        V = self.vecs
        mg = A.alloc([8, T], BF16)
        mgb = [[P.buf(f"mg{j}_{tb}") for tb in range(4)] for j in range(8)]
        acc = [A.alloc([512], F32) for _ in range(2)]
        accb = P.bufs("acc", 2)
        sg = [A.alloc([512], F32) for _ in range(3)]
        sgb = P.bufs("sg", 3)
        n = 0
        na = 0
        for j in range(8):
            gw, gwb = self.wload(Dm["gate_w"][l, j], 4096)
            gw3 = gw.rearrange("p (k c) -> p k c", k=8)
            owj, owb = self.wload(Dm["outw"][l, j], 1024)
            ow4 = owj[:, 0:1024].rearrange("p (i k c) -> p i k c", i=4, k=2)
            for tb in range(4):
                sl = slice(tb * 512, (tb + 1) * 512)
                a = acc[na % 2]; ab = accb[na % 2]; na += 1
                for i in range(4):
                    yT, yb = ys[i]
                    pg, pgb = self.psum()
                    self.mm([(pg, gw3[:, k, i * 128:(i + 1) * 128], self.hT[:, k, sl], k == 0, k == 7) for k in range(8)],
                            R=[gwb] + [self.hb[k][tb] for k in range(8)], W=[pgb])
                    po, pob = self.psum()
                    self.mm([(po, ow4[:, i, k, :], yT[:, k, sl], k == 0, k == 1) for k in range(2)],
                            R=[owb, yb[tb]], W=[pob])
                    s = sg[n % 3]; sb = sgb[n % 3]; n += 1
                    self.act(s, pg, AF.Sigmoid, bias=V[:, V_GATEB + i * 8 + j:V_GATEB + i * 8 + j + 1], R=[pgb, self.vecsb], W=[sb])
                    if i == 0:
                        self.tt(a, s, po, ALU.mult, R=[sb, pob], W=[ab])
                    elif i < 3:
                        self.tt(s, s, po, ALU.mult, R=[sb, pob], W=[sb])
                        self.tt(a, a, s, ALU.add, R=[sb, ab], W=[ab], eng="pool")
                    else:
                        self.tt(s, s, po, ALU.mult, R=[sb, pob], W=[sb])
                        self.tt(mg[:, j, sl], a, s, ALU.add, R=[sb, ab], W=[mgb[j][tb]], eng="pool")
        self.dump(f"mg{l}", mg, [b for r in mgb for b in r], [128, 8, T])
        for jg in range(2):
            w, wb = self.wload(Dm["w_o"][l, jg], 4096)
            w3 = w.rearrange("p (k c) -> p k c", k=8)
            for jj in range(4):
                j = jg * 4 + jj
                for tb in range(4):
                    sl = slice(tb * 512, (tb + 1) * 512)
                    ps, pb = self.psum()
                    self.mm([(ps, w3[:, k, jj * 128:(jj + 1) * 128], mg[:, k, sl], k == 0, k == 7) for k in range(8)],
                            R=[wb] + [mgb[k][tb] for k in range(8)], W=[pb])
                    self.stt(self.xT[:, j, sl], ps, self.modT[:, 16 + j:17 + j], self.xT[:, j, sl], ALU.mult, ALU.add,
                             R=[pb, self.modb, self.xb[j][tb]], W=[self.xb[j][tb]])

    def ffn(self, l):
        A, P, Dm = self.A, self.P, self.dram
        m = A.mark()
        hid = A.alloc([11, T], BF16)
        hidb = [[P.buf(f"hid{f}_{tb}") for tb in range(4)] for f in range(11)]
        sa = [A.alloc([512], F32) for _ in range(3)]
        sab = P.bufs("sa", 3)
        n = 0
        for half in range(2):
            for ff in range(11):
                f = half * 11 + ff
                w, wb = self.wload(Dm["ffn_w1"][l, f], 2048)
                w3 = w[:, 0:2048].rearrange("p (k c) -> p k c", k=8)
                for tb in range(4):
                    sl = slice(tb * 512, (tb + 1) * 512)
                    pa, pab = self.psum()
                    self.mm([(pa, w3[:, k, 0:128], self.hT[:, k, sl], k == 0, k == 7) for k in range(8)],
                            R=[wb] + [self.hb[k][tb] for k in range(8)], W=[pab])
                    pb_, pbb = self.psum()
                    self.mm([(pb_, w3[:, k, 128:256], self.hT[:, k, sl], k == 0, k == 7) for k in range(8)],
                            R=[wb] + [self.hb[k][tb] for k in range(8)], W=[pbb])
                    s = sa[n % 3]; sb = sab[n % 3]; n += 1
                    self.act(s, pa, AF.Silu, R=[pab], W=[sb])
                    self.tt(hid[:, ff, sl], s, pb_, ALU.mult, R=[sb, pbb], W=[hidb[ff][tb]])
            for jp in range(4):
                w, wb = self.wload(Dm["ffn_w2"][l, half * 4 + jp], 11 * 256)
                w3 = w[:, 0:11 * 256].rearrange("p (f c) -> p f c", f=11)
                for jj in range(2):
                    j = jp * 2 + jj
                    for tb in range(4):
                        sl = slice(tb * 512, (tb + 1) * 512)
                        ps, pb = self.psum()
                        self.mm([(ps, w3[:, ff, jj * 128:(jj + 1) * 128], hid[:, ff, sl], ff == 0, ff == 10) for ff in range(11)],
                                R=[wb] + [hidb[ff][tb] for ff in range(11)], W=[pb])
                        self.stt(self.xT[:, j, sl], ps, self.modT[:, 40 + j:41 + j], self.xT[:, j, sl], ALU.mult, ALU.add,
                                 R=[pb, self.modb, self.xb[j][tb]], W=[self.xb[j][tb]])
        A.release(m)
        P.set_fence()

    def final(self):
        P = self.P
        self.norm(self.fing, None, inplace=True)
        for k in range(8):
            src = self.xT[:, k, :]
            dst = self.outs["yT"][:, k, :]
            P.dma(lambda h, s=src, t=dst: h.dma_start(out=t, in_=s), reads=self.xb[k], queue="sp", out=True)


def _chunkT(v, n):
    return np.ascontiguousarray(np.asarray(v, np.float32).reshape(n, 128).T)


_HYC = {}


def _hy_consts(L, pos):
    key = L
    if key in _HYC:
        return _HYC[key]
    f32 = np.float32
    N = 4096
    zT = np.zeros((33, T), f32)
    j = np.arange(L)
    t = np.linspace(0.0, 1.0, L, dtype=f32)
    w = (2.0 * math.pi * j.astype(f32) / L).astype(f32)
    fb = np.linspace(1e-4, 15, 16, dtype=f32)
    zT[0, :L] = t
    zT[1:17, :L] = np.cos(fb[:, None] * w[None, :])
    zT[17:33, :L] = -np.sin(fb[:, None] * w[None, :])
    max_decay = math.log(1e-2) / 0.3
    min_decay = math.log(1e-2) / 1.5
    deltas = np.abs(np.linspace(min_decay, max_decay, 256, dtype=f32))
    dec = np.zeros((T, 256), f32)
    dec[:L] = np.exp(-t[:, None] * deltas[None, :])
    decay = np.ascontiguousarray(dec.reshape(16, 128, 256).transpose(1, 0, 2))
    om = 2.0 * math.pi * (np.arange(2048, dtype=np.float64) + 0.5) / N

    def fwd_units(positions):
        ang = positions.astype(np.float64)[:, None] * om[None, :]
        c = np.cos(ang).astype(f32); s_ = np.sin(ang).astype(f32)
        cs = np.stack([c, s_], 0)
        u = cs.reshape(2, 16, 128, 16, 128).transpose(3, 2, 1, 0, 4)
        return np.ascontiguousarray(u.reshape(16, 128, 4096)).astype(NB)

    FwT = fwd_units(np.asarray(pos))
    lagpos = np.arange(T)
    ClT = FwT if L == 2048 else fwd_units(lagpos)
    ang = om[:, None] * np.asarray(pos).astype(np.float64)[None, :]
    g = np.concatenate([np.cos(ang), np.sin(ang)], 0) * (2.0 / N)
    g = g.astype(f32).reshape(8, 4, 128, 2, 1024).transpose(3, 0, 2, 1, 4)
    GiT = np.ascontiguousarray(g.reshape(2, 8, 128, 4096)).astype(NB)
    _HYC[key] = dict(zT=zT, decay=decay, ClT=ClT, FwT=FwT, GiT=GiT)
    return _HYC[key]


def host_prep(inp):
    f32 = np.float32
    shared = {}
    vecs = np.zeros((DEPTH, 128, NV), f32)
    for l in range(DEPTH):
        v = vecs[l]
        v[:, V_N1G:V_N1G + 8] = _chunkT(inp["norm1_g"][l], 8)
        v[:, V_N2G:V_N2G + 8] = _chunkT(inp["norm2_g"][l], 8)
        v[:, V_ADAB:V_ADAB + 48] = _chunkT(inp["ada_b"][l], 48)
        gb = np.asarray(inp["gate_b"][l], f32).reshape(4, 8, 128)
        v[:, V_GATEB:V_GATEB + 32] = gb.transpose(2, 0, 1).reshape(128, 32)
        cw = np.asarray(inp["hy_conv_w"][l], f32).reshape(3, 6, 128)
        v[:, V_HYCW:V_HYCW + 18] = cw.transpose(2, 0, 1).reshape(128, 18)
        v[:, V_HYCB:V_HYCB + 6] = _chunkT(inp["hy_conv_b"][l], 6)
        hb = np.asarray(inp["hy_bias"][l], f32).reshape(2, 2, 128)
        v[:, V_HYBIAS:V_HYBIAS + 4] = hb.transpose(2, 0, 1).reshape(128, 4)
        v[:, V_QN:V_QN + 2] = _chunkT(inp["mla_q_norm"][l], 2)
        v[:, V_KVN:V_KVN + 1] = _chunkT(inp["mla_kv_norm"][l], 1)
        dw = np.asarray(inp["conf_dw_w"][l], f32).reshape(31, 2, 128)
        v[:, V_CDW:V_CDW + 62] = dw.transpose(2, 0, 1).reshape(128, 62)
        v[:, V_CDB:V_CDB + 2] = _chunkT(inp["conf_dw_b"][l], 2)
        v[:, V_CLG:V_CLG + 2] = _chunkT(inp["conf_ln_g"][l], 2)
        v[:, V_CLB:V_CLB + 2] = _chunkT(inp["conf_ln_b"][l], 2)
        rd = np.asarray(inp["ret_decay"][l], f32)
        v[:, V_RDALL:V_RDALL + 8] = rd.reshape(1, 8)
        v[0:64, V_RDFB:V_RDFB + 4] = rd[0][None]
        v[64:128, V_RDFB:V_RDFB + 4] = rd[1][None]
        v[0:64, V_HB1] = np.asarray(inp["hy_b1"][l], f32)
        v[0:64, V_HB2] = np.asarray(inp["hy_b2"][l], f32)
    shared["vecs"] = vecs
    shared["final_g"] = _chunkT(inp["final_norm_g"], 8)
    shared["ident"] = np.eye(128, dtype=f32)

    def kmajor(w, ncol_unit):
        Dp, KK, C = w.shape
        K = KK // 128
        nu = C // ncol_unit
        a = np.asarray(w, f32).reshape(Dp, K, 128, nu, ncol_unit).transpose(0, 3, 2, 1, 4)
        return np.ascontiguousarray(a.reshape(Dp, nu, 128, K * ncol_unit))

    shared["ada_w"] = kmajor(inp["ada_w"], 512)
    shared["w_in"] = np.ascontiguousarray(np.asarray(inp["w_in"], f32).reshape(DEPTH, 8, 128, 2720).transpose(0, 2, 1, 3))
    gw = np.asarray(inp["gate_w"], f32).reshape(DEPTH, 8, 128, 4, 8, 128)
    shared["gate_w"] = np.ascontiguousarray(gw.transpose(0, 4, 2, 1, 3, 5).reshape(DEPTH, 8, 128, 8 * 512))
    shared["w_o"] = kmajor(inp["w_o"], 512)
    w1 = np.asarray(inp["ffn_w1"], f32).reshape(DEPTH, 8, 128, 2, 22, 128)
    shared["ffn_w1"] = np.ascontiguousarray(w1.transpose(0, 4, 2, 1, 3, 5).reshape(DEPTH, 22, 128, 8 * 256))
    w2 = np.asarray(inp["ffn_w2"], f32).reshape(DEPTH, 2, 11, 128, 4, 256)
    shared["ffn_w2"] = np.ascontiguousarray(w2.transpose(0, 1, 4, 3, 2, 5).reshape(DEPTH, 8, 128, 11 * 256))
    ow = np.stack([np.asarray(inp[nm], f32) for nm in ("hy_out", "mla_out", "ret_out", "conf_out")], 1)
    ow = ow.reshape(DEPTH, 4, 2, 128, 8, 128).transpose(0, 4, 3, 1, 2, 5)
    ow = np.ascontiguousarray(ow.reshape(DEPTH, 8, 128, 1024))
    shared["outw"] = ow

    shared["w_uq"] = np.ascontiguousarray(np.asarray(inp["mla_w_uq"], f32).reshape(DEPTH, 2, 128, 384).transpose(0, 2, 1, 3).reshape(DEPTH, 128, 768))
    shared["w_ukv"] = np.ascontiguousarray(np.asarray(inp["mla_w_ukv"], f32))
    tt_ = np.arange(T)
    inv = (10000.0 ** (-np.arange(8, dtype=np.float64) / 8))
    ang = np.concatenate([(tt_ // 64)[:, None] * inv, (tt_ % 64)[:, None] * inv], axis=1)
    cos_s = np.cos(ang).T.astype(f32); sin_s = np.sin(ang).T.astype(f32)
    rope_sample = np.stack([np.concatenate([cos_s, cos_s], 0), np.concatenate([sin_s, sin_s], 0)], 0)
    rope_prompt = np.stack([np.ones((32, T), f32), np.zeros((32, T), f32)], 0)
    oh_sample = np.zeros((32, NKEY), f32); oh_sample[0, :] = BIGC
    oh_prompt = np.zeros((32, NKEY), f32)
    for sgi in range(8):
        oh_prompt[sgi, sgi * 256:(sgi + 1) * 256] = BIGC
    zc = np.zeros((DEPTH, 128, 256), f32); zk = np.zeros((DEPTH, 32, 256), f32)
    rcon = np.zeros((128, 642), f32)
    mm_ = np.arange(128)[:, None]; nn_ = np.arange(128)[None, :]
    rcon[:, 0:128] = np.maximum(nn_ - mm_, 0); rcon[:, 128:256] = (nn_ >= mm_)
    rcon[:, 256:384] = np.maximum(mm_ - nn_, 0); rcon[:, 384:512] = (mm_ >= nn_)
    rcon[0:64, 512:640] = nn_ + 1; rcon[64:128, 512:640] = 128 - nn_
    rcon[:, 640] = 127 - np.arange(128); rcon[:, 641] = np.arange(128)
    shared["rcon"] = rcon
    zs0 = np.zeros((DEPTH, 128, 256), f32)

    for nm in ("hy_w1", "hy_w2", "hy_w3"):
        shared[nm] = np.ascontiguousarray(np.asarray(inp[nm], f32))
    hy_s = _hy_consts(2048, np.arange(T))
    hy_p = _hy_consts(256, 512 * (np.arange(T) // 256) + (np.arange(T) % 256))

    in_maps = []
    for core in range(8):
        m = dict(shared)
        if core < 4:
            m["cacheT"] = np.ascontiguousarray(np.asarray(inp["cache_mla_ckv"][core], f32).transpose(0, 2, 1))
            m["kropeC"] = np.ascontiguousarray(np.asarray(inp["cache_mla_krope"][core], f32).transpose(0, 2, 1))
            m["oh"] = oh_sample; m["ropeCS"] = rope_sample
            m["s0"] = np.ascontiguousarray(np.asarray(inp["state_ret"][core], f32).transpose(0, 1, 3, 2, 4).reshape(DEPTH, 128, 256))
        else:
            m["cacheT"] = zc; m["kropeC"] = zk; m["oh"] = oh_prompt; m["ropeCS"] = rope_prompt
            m["s0"] = zs0
        if core < 4:
            x = np.asarray(inp["x_sample"][core], f32)
            cond = np.asarray(inp["c"][core], f32)
        else:
            b0 = 8 * (core - 4)
            x = np.asarray(inp["x_prompt"][b0:b0 + 8], f32).reshape(T, D)
            cond = np.asarray(inp["c_ctx"], f32)
        m["xT"] = np.ascontiguousarray(x.T.reshape(8, 128, T).transpose(1, 0, 2))
        m["condT"] = _chunkT(cond, 8)
        fl = np.zeros((128, NFL), f32)
        fl[:, FL_HALO] = 1.0 if core < 4 else 0.0
        for i in range(16):
            fl[:, FL_KF + i] = 1.0 if core < 4 else (0.0 if i % 2 == 0 else 1.0)
            fl[:, FL_KB + i] = 1.0 if core < 4 else (0.0 if i % 2 == 1 else 1.0)
        fl[:, FL_M0] = 1.0; fl[0, FL_M0] = 0.0
        fl[:, FL_M0 + 1] = -1.0; fl[0, FL_M0 + 1] = 0.0
        m["flags"] = fl
        hc = hy_s if core < 4 else hy_p
        m["zT"] = hc["zT"]; m["decay"] = hc["decay"]; m["ClT"] = hc["ClT"]; m["FwT"] = hc["FwT"]; m["GiT"] = hc["GiT"]
        in_maps.append(m)
    return in_maps


_CACHE = {}


def run(inp, dbg=()):
    key = tuple(sorted(dbg))
    if key not in _CACHE:
        kb = KB(dbg)
        kb.build()
        _CACHE[key] = kb
    kb = _CACHE[key]
    in_maps = host_prep(inp)
    names = set(kb.dram.keys())
    in_maps = [{k: v for k, v in m.items() if k in names} for m in in_maps]
    res = run_bass_kernel_spmd(kb.nc, in_maps, core_ids=list(range(8)))
    return res.results


def kernel(**inputs):
    res = run(inputs)
    f32 = np.float32
    yp = np.zeros((32, 256, D), f32)
    ys = np.zeros((4, 2048, D), f32)
    for core in range(8):
        yT = np.asarray(res[core]["yT"], f32)
        y = yT.transpose(1, 0, 2).reshape(D, T).T
        if core < 4:
            ys[core] = y
        else:
            yp[8 * (core - 4):8 * (core - 4) + 8] = y.reshape(8, 256, D)
    ckv = np.zeros((32, DEPTH, 256, 128), f32)
    krope = np.zeros((32, DEPTH, 256, 32), f32)
    for core in range(4, 8):
        b0 = 8 * (core - 4)
        ckv[b0:b0 + 8] = np.asarray(res[core]["ckv_out"], f32).reshape(DEPTH, 8, 256, 128).transpose(1, 0, 2, 3)
        krope[b0:b0 + 8] = np.asarray(res[core]["krope_out"], f32).reshape(DEPTH, 8, 256, 32).transpose(1, 0, 2, 3)
    st = np.zeros((32, DEPTH, 2, 4, 64, 64), f32)
    for core in range(4, 8):
        b0 = 8 * (core - 4)
        so = np.asarray(res[core]["st_out"], f32).reshape(DEPTH, 2, 64, 8, 4, 64)
        st[b0:b0 + 8] = so.transpose(3, 0, 1, 4, 2, 5)
    return (yp, ys, ckv, krope, st)
```

```python
import contextlib
import math
import numpy as np
import ml_dtypes
import concourse.bass as bass
import concourse.mybir as mybir
from concourse.bass_utils import run_bass_kernel_spmd

F32 = mybir.dt.float32
BF16 = mybir.dt.bfloat16
I32 = mybir.dt.int32
AF = mybir.ActivationFunctionType
ALU = mybir.AluOpType

NB = ml_dtypes.bfloat16
D = 1024
T = 2048
NSEG = 8
SEGL = 256
DEPTH = 2
EPS = 1e-6
DFF = 2816
NKEY = 2304
BIGC = 32.0

V_N1G, V_N2G, V_ADAB, V_GATEB, V_FING = 0, 8, 16, 64, 96
V_HYCW, V_HYCB, V_HYBIAS = 104, 122, 128
V_QN, V_KVN = 132, 134
V_CDW, V_CDB, V_CLG, V_CLB = 135, 197, 199, 201
V_RDALL, V_RDFB, V_HB1, V_HB2 = 203, 211, 215, 216
NV = 224
FL_HALO, FL_KF, FL_KB, FL_M0 = 0, 1, 17, 33
NFL = 40

ENGS = ("pe", "act", "dve", "pool", "sp")
SAME_ENGINE_SYNC = {"pe": False, "act": True, "dve": True, "pool": True, "sp": False}


class Buf:
    __slots__ = ("name", "wr", "rd", "dma_sem", "dma_cnt", "const", "excl", "scratch")

    def __init__(self, name):
        self.name = name
        self.wr = None
        self.rd = []
        self.dma_sem = None
        self.dma_cnt = 0
        self.const = False
        self.scratch = True
        self.excl = False


class Prog:
    def __init__(self, nc):
        self.nc = nc
        self.ops = {e: [] for e in ENGS}
        self.seq = {e: 0 for e in ENGS}
        self.known = {e: {} for e in ENGS}
        self.n_dsem = 0
        self.out_dma = []
        self.fence = None
        self.fence_d = {}
        self.pending_rd = {}

    def buf(self, name):
        return Buf(name)

    def bufs(self, name, n):
        return [Buf(f"{name}{i}") for i in range(n)]

    def _need(self, eng, dep, waits):
        if dep is None:
            return
        if dep[0] == "dma":
            key = ("d", dep[1]); val = dep[2]
        else:
            if dep[0] == eng and not SAME_ENGINE_SYNC[eng]:
                return
            key = ("e", dep[0]); val = dep[1]
        if self.known[eng].get(key, 0) >= val:
            return
        self.known[eng][key] = val
        waits[key] = max(waits.get(key, 0), val)

    def _deps(self, eng, reads, writes, nofence=False):
        waits = {}
        if not any(b.scratch for b in writes):
            nofence = True
        if self.fence is not None and not nofence:
            for e, v in self.fence.items():
                if v > 0:
                    self._need(eng, (e, v), waits)
            for si, v in self.fence_d.items():
                self._need(eng, ("dma", si, v), waits)
        for b in reads:
            self._need(eng, b.wr, waits)
            if b.excl:
                for r in b.rd:
                    if r[0] != eng:
                        self._need(eng, r, waits)
        for b in writes:
            self._need(eng, b.wr, waits)
            for r in b.rd:
                self._need(eng, r, waits)
        return waits

    def set_fence(self):
        self.fence = dict(self.seq)
        self.fence_d = dict(self.pending_rd)

    def op(self, eng, fn, reads=(), writes=()):
        waits = self._deps(eng, reads, writes)
        self.seq[eng] += 1
        me = (eng, self.seq[eng])
        for b in reads:
            if not b.const:
                b.rd.append(me)
        for b in writes:
            b.wr = me
            b.rd = []
        self.ops[eng].append((waits, fn, ("e", eng)))
        return me

    def dma(self, fn, reads=(), writes=(), queue="sp", out=False, nofence=False):
        waits = self._deps(queue, reads, writes, nofence)
        if writes:
            tgt = writes[0]
        else:
            tgt = reads[0]
        if tgt.dma_sem is None:
            tgt.dma_sem = {}
        if queue not in tgt.dma_sem:
            tgt.dma_sem[queue] = [self.n_dsem, 0]
            self.n_dsem += 1
        ent = tgt.dma_sem[queue]
        ent[1] += 16
        dep = ("dma", ent[0], ent[1])
        if reads:
            self.pending_rd[ent[0]] = ent[1]
        for b in reads:
            if not b.const:
                b.rd.append(dep)
        for b in writes:
            b.wr = dep
            b.rd = []
        self.ops[queue].append((waits, fn, ("d", ent[0])))
        if out:
            self.out_dma.append(dep)
        return dep

    def _outbuf(self, queue):
        key = "_out_" + queue
        if not hasattr(self, key):
            setattr(self, key, Buf(key))
        return getattr(self, key)

    def emit(self):
        nc = self.nc
        with contextlib.ExitStack() as st:
            esem = {e: st.enter_context(nc.semaphore("s_" + e)) for e in ENGS}
            dsem = [st.enter_context(nc.semaphore(f"d{i}")) for i in range(self.n_dsem)]
            block = st.enter_context(nc.Block())

            def sem_of(key):
                return esem[key[1]] if key[0] == "e" else dsem[key[1]]

            def run(eng, h):
                for waits, fn, inc in self.ops[eng]:
                    for key, val in waits.items():
                        h.wait_ge(sem_of(key), val)
                    ins = fn(h)
                    if inc[0] == "e":
                        ins.then_inc(esem[inc[1]], 1)
                    else:
                        ins.then_inc(dsem[inc[1]], 16)
                if eng == "sp":
                    for dep in self.out_dma:
                        h.wait_ge(dsem[dep[1]], dep[2])
                    for e in ENGS:
                        if e != "sp" and self.seq[e] > 0:
                            h.wait_ge(esem[e], self.seq[e])

            @block.tensor
            def _(h):
                run("pe", h)

            @block.scalar
            def _(h):
                run("act", h)

            @block.vector
            def _(h):
                run("dve", h)

            @block.gpsimd
            def _(h):
                run("pool", h)

            @block.sync
            def _(h):
                run("sp", h)


class Arena:
    def __init__(self, nc, st, nbytes):
        self.words = nbytes // 4
        self.t = st.enter_context(nc.sbuf_tensor("arena", [128, self.words], F32))
        self.top = 0
        self.peak = 0

    def alloc(self, shape, dt):
        esz = 2 if dt == BF16 else 4
        n = 1
        for s in shape:
            n *= s
        nbytes = (n * esz + 31) // 32 * 32
        off = self.top
        self.top += nbytes
        self.peak = max(self.peak, self.top)
        assert self.top <= self.words * 4, f"arena overflow {self.top}"
        ap = self.t[:, off // 4:(off + nbytes) // 4]
        if dt != F32:
            ap = ap.bitcast(dt)
        ap = ap[:, 0:n]
        if len(shape) == 1:
            return ap
        if len(shape) == 2:
            ap = ap.rearrange("p (a b) -> p a b", a=shape[0])
        elif len(shape) == 3:
            ap = ap.rearrange("p (a b c) -> p a b c", a=shape[0], b=shape[1])
        elif len(shape) == 4:
            ap = ap.rearrange("p (a b c d) -> p a b c d", a=shape[0], b=shape[1], c=shape[2])
        return ap

    def mark(self):
        return self.top

    def release(self, m):
        self.top = m


RING_SLOTS = 3
RING_ELEMS = 4096


class KB:
    def __init__(self, dbg=()):
        self.nc = bass.Bass("TRN2", target_bir_lowering=False)
        self.dbg = set(dbg)
        self.dram = {}
        self.outs = {}

    def din(self, name, shape, dt=F32):
        self.dram[name] = self.nc.dram_tensor(name, list(shape), dt, kind="ExternalInput").ap()
        return self.dram[name]

    def dout(self, name, shape, dt=F32):
        self.outs[name] = self.nc.dram_tensor(name, list(shape), dt, kind="ExternalOutput").ap()
        return self.outs[name]

    def act(self, out, in_, func, bias=0.0, scale=1.0, R=(), W=()):
        return self.P.op("act", lambda h: h.activation(out=out, in_=in_, func=func, bias=bias, scale=scale), R, W)

    def tt(self, out, in0, in1, op, R=(), W=(), eng="dve"):
        return self.P.op(eng, lambda h: h.tensor_tensor(out=out, in0=in0, in1=in1, op=op), R, W)

    def ts(self, out, in0, s1, s2, op0, op1=None, R=(), W=()):
        if op1 is None:
            return self.P.op("dve", lambda h: h.tensor_single_scalar(out=out, in_=in0, scalar=s1, op=op0), R, W)
        return self.P.op("dve", lambda h: h.tensor_scalar(out=out, in0=in0, scalar1=s1, scalar2=s2, op0=op0, op1=op1), R, W)

    def stt(self, out, in0, scalar, in1, op0, op1, R=(), W=()):
        return self.P.op("dve", lambda h: h.scalar_tensor_tensor(out=out, in0=in0, scalar=scalar, in1=in1, op0=op0, op1=op1), R, W)

    def cp(self, out, in_, R=(), W=(), eng="dve"):
        if eng == "act":
            return self.P.op("act", lambda h: h.copy(out=out, in_=in_), R, W)
        return self.P.op(eng, lambda h: h.tensor_copy(out=out, in_=in_), R, W)

    def memset(self, ap, val, W=(), eng="dve"):
        return self.P.op(eng, lambda h: h.memset(ap, val), (), W)

    def recip(self, out, in_, R=(), W=()):
        return self.P.op("dve", lambda h: h.reciprocal(out=out, in_=in_), R, W)

    def mm(self, mms, R=(), W=()):
        def fn(h):
            ins = None
            for (o, l, r, s0, s1) in mms:
                ins = h.matmul(o, lhsT=l, rhs=r, start=s0, stop=s1)
            return ins
        return self.P.op("pe", fn, R, W)

    def psum(self):
        while True:
            i = self.ps_next
            self.ps_next = (i + 1) % 8
            if i not in self.ps_held:
                return self.PS[i], self.PSB[i]

    def psum_hold(self):
        ps, pb = self.psum()
        i = self.PS.index(ps) if False else [k for k in range(8) if self.PSB[k] is pb][0]
        self.ps_held.add(i)
        return ps, pb, i

    def wload(self, src, n, cast=True, parts=128):
        assert n <= RING_ELEMS
        i = self.ring_next
        self.ring_next = (i + 1) % RING_SLOTS
        dst = self.ring[i][0:parts, 0:n]
        b = self.ringb[i]
        q = "pool" if cast else "sp"
        self.P.dma(lambda h: h.dma_start(out=dst, in_=src), writes=[b], queue=q, nofence=True)
        return self.ring[i], b

    def dump(self, name, ap, bufs, shape):
        if name not in self.dbg:
            return
        o = self.dout("dbg_" + name, shape)
        self.P.dma(lambda h: h.dma_start(out=o, in_=ap), reads=list(bufs), queue="pool", out=True)

    def build(self):
        nc = self.nc
        d = self.din
        d("xT", [128, 8, T]); d("condT", [128, 8]); d("vecs", [DEPTH, 128, NV]); d("flags", [128, NFL])
        d("ada_w", [DEPTH, 12, 128, 8 * 512])
        d("w_in", [DEPTH, 128, 8, 2720])
        d("gate_w", [DEPTH, 8, 128, 8 * 512])
        d("w_o", [DEPTH, 2, 128, 8 * 512])
        d("ffn_w1", [DEPTH, 22, 128, 8 * 256])
        d("ffn_w2", [DEPTH, 8, 128, 11 * 256])
        d("outw", [DEPTH, 8, 128, 1024])
        d("final_g", [128, 8])
        d("w_uq", [DEPTH, 128, 768]); d("w_ukv", [DEPTH, 128, 512])
        d("cacheT", [DEPTH, 128, 256]); d("kropeC", [DEPTH, 32, 256])
        d("oh", [32, NKEY]); d("ropeCS", [2, 32, T])
        self.dout("ckv_out", [DEPTH, T, 128]); self.dout("krope_out", [DEPTH, T, 32])
        d("rcon", [128, 642]); d("s0", [DEPTH, 128, 256])
        d("hy_w1", [DEPTH, 33, 64]); d("hy_w2", [DEPTH, 64, 64]); d("hy_w3", [DEPTH, 64, 1024])
        d("zT", [33, T]); d("decay", [128, 16, 256])
        d("ClT", [16, 128, 4096], BF16); d("FwT", [16, 128, 4096], BF16); d("GiT", [2, 8, 128, 4096], BF16)
        self.dout("st_out", [DEPTH, 128, 8, 256])
        self.dout("yT", [128, 8, T])

        with contextlib.ExitStack() as st:
            self.P = Prog(nc)
            self.A = Arena(nc, st, 207 * 1024)
            self.PS = [st.enter_context(nc.psum_tensor(f"ps{i}", [128, 512], F32))[:, :] for i in range(8)]
            self.PSB = self.P.bufs("ps", 8)
            for b_ in self.PSB:
                b_.excl = True
                b_.scratch = False
            self.ps_next = 0
            self.ps_held = set()
            self.persist()
            for l in range(DEPTH):
                self.layer(l)
            self.final()
            self.P.emit()
        return nc

    def persist(self):
        A, P, nc = self.A, self.P, self.nc
        Dm = self.dram
        self.xT = A.alloc([8, T], F32)
        self.xb = [[P.buf(f"x{k}_{tb}") for tb in range(4)] for k in range(8)]
        self.hT = A.alloc([8, T], BF16)
        self.hb = [[P.buf(f"h{k}_{tb}") for tb in range(4)] for k in range(8)]
        self.ring = [A.alloc([RING_ELEMS], BF16) for _ in range(RING_SLOTS)]
        self.ringb = P.bufs("ring", RING_SLOTS)
        self.ring_next = 0
        self.vecs = A.alloc([NV], F32)
        self.vecsb = P.buf("vecs")
        self.flags = A.alloc([NFL], F32)
        self.flagsb = P.buf("flags")
        self.modT = A.alloc([48], F32)
        self.modb = P.buf("modT")
        self.mods = A.alloc([16], F32)
        self.scb = A.alloc([8], BF16)
        self.scbb = P.buf("scb")
        self.condT = A.alloc([8], F32)
        self.condb = P.buf("cond")
        self.fing = A.alloc([8], F32)
        self.fingb = P.buf("fing")
        self.ident = A.alloc([128], BF16)
        self.ones = A.alloc([128], BF16)
        self.one1 = A.alloc([8], F32)
        self.cb = P.buf("consts")
        identf = A.alloc([128], F32)
        onesf = A.alloc([128], F32)
        self.onesf = onesf
        self.negbig_t = A.alloc([8], F32)[:, 0:1]
        self.memset(self.negbig_t, -float(BIGC * BIGC) * (96 ** -0.5), W=[P.buf("negbig")])
        Dm_ident = self.din("ident", [128, 128])
        for k in range(8):
            src = Dm["xT"][:, k, :]
            dst = self.xT[:, k, :]
            P.dma(lambda h, s=src, t=dst: h.dma_start(out=t, in_=s), writes=self.xb[k], queue="sp")
        P.dma(lambda h: h.dma_start(out=self.condT, in_=Dm["condT"]), writes=[self.condb], queue="sp")
        P.dma(lambda h: h.dma_start(out=self.flags, in_=Dm["flags"]), writes=[self.flagsb], queue="sp")
        P.dma(lambda h: h.dma_start(out=self.fing, in_=Dm["final_g"]), writes=[self.fingb], queue="sp")
        tb_ = P.buf("identf")
        P.dma(lambda h: h.dma_start(out=identf, in_=Dm_ident), writes=[tb_], queue="sp")
        self.cp(self.ident, identf, R=[tb_], W=[self.cb])
        self.memset(onesf, 1.0, W=[tb_])
        self.cp(self.ones, onesf, R=[tb_], W=[self.cb])
        self.memset(self.one1, 1.0, W=[self.cb])
        self.act(self.scb, self.condT, AF.Silu, R=[self.condb], W=[self.scbb])
        self.cb.const = True
        self.flagsb.const = True
        for b_ in ([x for r in self.xb for x in r] + [x for r in self.hb for x in r] + self.ringb +
                   [self.vecsb, self.flagsb, self.modb, self.scbb, self.condb, self.fingb, self.cb]):
            b_.scratch = False

    def modulation(self, l):
        P, Dm = self.P, self.dram
        mk = self.A.mark()
        self.modrow = self.A.alloc([6144], F32)
        self.modrowb = P.buf("modrow")
        P.dma(lambda h: h.dma_start(out=self.vecs, in_=Dm["vecs"][l]), writes=[self.vecsb], queue="sp")
        for cbk in range(12):
            w, wb = self.wload(Dm["ada_w"][l, cbk], 4096)
            w3 = w.rearrange("p (k c) -> p k c", k=8)
            ps, pb = self.psum()
            self.mm([(ps[0:1, :], self.scb[:, k:k + 1], w3[:, k, :], k == 0, k == 7) for k in range(8)],
                    R=[wb, self.scbb], W=[pb])
            self.cp(self.modrow[0:1, cbk * 512:(cbk + 1) * 512], ps[0:1, :], R=[pb], W=[self.modrowb], eng="act")
        ps, pb = self.psum()
        self.mm([(ps[:, c:c + 1], self.modrow[0:1, c * 128:(c + 1) * 128], self.one1[0:1, 0:1], True, True)
                 for c in range(48)], R=[self.modrowb, self.cb], W=[pb])
        self.tt(self.modT, ps[:, 0:48], self.vecs[:, V_ADAB:V_ADAB + 48], ALU.add, R=[pb, self.vecsb], W=[self.modb])
        self.stt(self.mods[:, 0:8], self.modT[:, 8:16], 1.0, self.vecs[:, V_N1G:V_N1G + 8], ALU.add, ALU.mult,
                 R=[self.modb, self.vecsb], W=[self.modb])
        self.stt(self.mods[:, 8:16], self.modT[:, 32:40], 1.0, self.vecs[:, V_N2G:V_N2G + 8], ALU.add, ALU.mult,
                 R=[self.modb, self.vecsb], W=[self.modb])
        self.A.release(mk)
        P.set_fence()

    def norm(self, Acols, Bcols, inplace=False):
        A, P = self.A, self.P
        m = A.mark()
        sq = [A.alloc([512], BF16) for _ in range(4)]
        sqb = P.bufs("sq", 4)
        rs = [A.alloc([512], F32) for _ in range(2)]
        rsb = P.bufs("rs", 2)
        tmp = [A.alloc([512], F32) for _ in range(3)]
        tmpb = P.bufs("ntmp", 3)
        ti = 0
        for tb in range(4):
            sl = slice(tb * 512, (tb + 1) * 512)
            ps, pb = self.psum()
            for k in range(8):
                j = (tb * 8 + k) % 4
                self.act(sq[j], self.xT[:, k, sl], AF.Square, R=[self.xb[k][tb]], W=[sqb[j]])
                self.mm([(ps, self.ones, sq[j], k == 0, k == 7)], R=[sqb[j], self.cb], W=[pb])
            r = rs[tb % 2]; rb = rsb[tb % 2]
            self.act(r, ps, AF.Sqrt, bias=self.eps_t, scale=1.0 / D, R=[pb, self.cb], W=[rb])
            self.recip(r, r, R=[rb], W=[rb])
            for k in range(8):
                if inplace:
                    self.stt(self.xT[:, k, sl], self.xT[:, k, sl], Acols[:, k:k + 1], r, ALU.mult, ALU.mult,
                             R=[self.xb[k][tb], rb, self.fingb], W=[self.xb[k][tb]])
                else:
                    t = tmp[ti % 3]; tbf = tmpb[ti % 3]; ti += 1
                    self.stt(t, self.xT[:, k, sl], Acols[:, k:k + 1], r, ALU.mult, ALU.mult,
                             R=[self.xb[k][tb], rb, self.modb], W=[tbf])
                    self.act(self.hT[:, k, sl], t, AF.Identity, bias=Bcols[:, k:k + 1], scale=1.0,
                             R=[tbf, self.modb], W=[self.hb[k][tb]])
        A.release(m)
        P.set_fence()

    def layer(self, l):
        A, P, Dm = self.A, self.P, self.dram
        if l == 0:
            self.eps_t = A.alloc([8], F32)[:, 0:1]
            self.memset(self.eps_t, EPS, W=[P.buf("eps")])
        self.modulation(l)
        self.norm(self.mods[:, 0:8], self.modT[:, 0:8])
        self.dump(f"h{l}", self.hT, [b for r in self.hb for b in r], [128, 8, T])
        self.dump(f"mod{l}", self.modT, [self.modb], [128, 48])
        mtop = A.mark()
        ys = []
        for name in ("hy", "mla", "ret", "conf"):
            yT = A.alloc([2, T], BF16)
            yb = P.bufs("y" + name, 4)
            m = A.mark()
            getattr(self, "branch_" + name)(l, yT, yb)
            A.release(m)
            P.set_fence()
            ys.append((yT, yb))
            self.dump(f"y{name}{l}", yT, yb, [128, 2, T])
        self.merge(l, ys)
        A.release(mtop)
        P.set_fence()
        self.norm(self.mods[:, 8:16], self.modT[:, 24:32])
        self.ffn(l)
        self.dump(f"x{l}", self.xT, [b for r in self.xb for b in r], [128, 8, T])

    def zero_branch(self, yT, yb):
        for tb in range(4):
            self.memset(yT[:, :, tb * 512:(tb + 1) * 512], 0.0, W=[yb[tb]])

    def branch_hy(self, l, yT, yb):
        A, P, Dm = self.A, self.P, self.dram
        V = self.vecs
        FLG = self.flags
        TWO_PI = 2.0 * math.pi
        hall = lambda tb: [self.hb[k][tb] for k in range(8)]
        vT = A.alloc([2, T], BF16); vTb = P.bufs("hvT", 4)
        Ksp = A.alloc([16, 2, 256], BF16); kspb = P.buf("Ksp")
        h2all = A.alloc([T], BF16); h2allb = P.bufs("h2all", 4)
        w1 = A.alloc([64], F32); w2 = A.alloc([64], F32); wsb = P.buf("hyw")
        P.dma(lambda h: h.dma_start(out=w1[0:33, :], in_=Dm["hy_w1"][l]), writes=[wsb], queue="sp")
        P.dma(lambda h: h.dma_start(out=w2[0:64, :], in_=Dm["hy_w2"][l]), writes=[wsb], queue="sp")

        def hy_proj(which, dst, dstb):
            m = A.mark()
            wv, wvb = self.win_load(l, which * 256, 256)
            upad = A.alloc([NSEG, 258], BF16); upb = P.buf("upad")
            tmp = A.alloc([4, 256], F32); tmpb = P.buf("hytmp")
            mcol = FLG[:, FL_HALO:FL_HALO + 1]
            for cc in range(2):
                ch = which * 2 + cc
                for tb in range(4):
                    sl = slice(tb * 512, (tb + 1) * 512)
                    ps, pb = self.psum()
                    self.mm([(ps, wv[:, k, cc * 128:(cc + 1) * 128], self.hT[:, k, sl], k == 0, k == 7) for k in range(8)],
                            R=[wvb] + hall(tb), W=[pb])
                    self.cp(upad[:, 2 * tb:2 * tb + 2, 1:257], ps.rearrange("p (s t) -> p s t", s=2), R=[pb], W=[upb], eng="act")
                self.memset(upad[:, 0, 0:1], 0.0, W=[upb])
                self.memset(upad[:, 7, 257:258], 0.0, W=[upb])
                self.ts(upad[:, 1:8, 0:1], upad[:, 0:7, 256:257], mcol, None, ALU.mult, R=[upb, self.flagsb], W=[upb])
                self.ts(upad[:, 0:7, 257:258], upad[:, 1:8, 1:2], mcol, None, ALU.mult, R=[upb, self.flagsb], W=[upb])
                for hf in range(2):
                    sg = slice(hf * 4, hf * 4 + 4)
                    self.act(tmp, upad[:, sg, 1:257], AF.Identity, bias=V[:, V_HYCB + ch:V_HYCB + ch + 1],
                             scale=V[:, V_HYCW + 6 + ch:V_HYCW + 7 + ch], R=[upb, self.vecsb], W=[tmpb])
                    self.stt(tmp, upad[:, sg, 0:256], V[:, V_HYCW + ch:V_HYCW + ch + 1], tmp, ALU.mult, ALU.add,
                             R=[upb, tmpb, self.vecsb], W=[tmpb])
                    self.stt(dst[:, cc, hf * 1024:(hf + 1) * 1024].rearrange("p (s t) -> p s t", s=4), upad[:, sg, 2:258],
                             V[:, V_HYCW + 12 + ch:V_HYCW + 13 + ch], tmp, ALU.mult, ALU.add,
                             R=[upb, tmpb, self.vecsb], W=[dstb[2 * hf], dstb[2 * hf + 1]])
            A.release(m)

        def filt(o):
            m = A.mark()
            w3 = A.alloc([512], BF16); w3b = P.buf("w3")
            P.dma(lambda h: h.dma_start(out=w3[0:64, :], in_=Dm["hy_w3"][l][:, o * 512:(o + 1) * 512]), writes=[w3b], queue="pool")
            hp = A.alloc([16, 256], BF16); hm = A.alloc([16, 256], BF16); hpb = P.bufs("hp", 16)
            dec = [A.alloc([256], F32) for _ in range(2)]; decb = P.bufs("dec", 2)
            if o == 0:
                zt = [A.alloc([512], F32) for _ in range(2)]; ztb = P.bufs("zt", 2)
                u = A.alloc([512], F32); kf = A.alloc([512], F32); ki = A.alloc([512], I32)
                h1 = A.alloc([512], F32)
                mb = P.buf("mlp")
            hds = [A.alloc([2, 256], F32) for _ in range(3)]; hdbs = P.bufs("hd", 3)
            abs_ = [A.alloc([512], BF16) for _ in range(2)]; abbs = P.bufs("ab", 2)
            rn = A.alloc([256], F32); rnb = P.buf("rn")
            psN, psNb, iN = self.psum_hold()

            def sin_layer(ps, pb, bcol, out, outb):
                self.ts(u[0:64, :], ps[0:64, :], bcol, 1.0 / TWO_PI, ALU.add, ALU.mult, R=[pb, self.vecsb], W=[mb])
                self.cp(ki[0:64, :], u[0:64, :], R=[mb], W=[mb])
                self.cp(kf[0:64, :], ki[0:64, :], R=[mb], W=[mb])
                self.tt(u[0:64, :], u[0:64, :], kf[0:64, :], ALU.subtract, R=[mb], W=[mb])
                self.act(out[0:64, :], u[0:64, :], AF.Sin, scale=TWO_PI, R=[mb], W=[outb])

            for jb in range(4):
                h2, h2b = h2all[:, jb * 512:(jb + 1) * 512], h2allb[jb]
                if o == 0:
                    z, zb = zt[jb % 2], ztb[jb % 2]
                    P.dma(lambda h, z=z, jb=jb: h.dma_start(out=z[0:33, :], in_=Dm["zT"][:, jb * 512:(jb + 1) * 512]), writes=[zb], queue="sp")
                    ps, pb = self.psum()
                    self.mm([(ps[0:64, :], w1[0:33, :], z[0:33, :], True, True)], R=[wsb, zb], W=[pb])
                    sin_layer(ps, pb, V[0:64, V_HB1:V_HB1 + 1], h1, mb)
                    ps, pb = self.psum()
                    self.mm([(ps[0:64, :], w2[0:64, :], h1[0:64, :], True, True)], R=[wsb, mb], W=[pb])
                    sin_layer(ps, pb, V[0:64, V_HB2:V_HB2 + 1], h2, h2b)
                for ii in range(4):
                    i = jb * 4 + ii
                    hd, hdb = hds[i % 3], hdbs[i % 3]
                    ab, abb = abs_[i % 2], abbs[i % 2]
                    dc, dcb = dec[i % 2], decb[i % 2]
                    P.dma(lambda h, dc=dc, i=i: h.dma_start(out=dc, in_=Dm["decay"][:, i, :]), writes=[dcb], queue="sp")
                    p3, p3b = self.psum()
                    self.mm([(p3, h2[0:64, ii * 128:(ii + 1) * 128], w3[0:64, :], True, True)], R=[h2b, w3b], W=[p3b])
                    for dr in range(2):
                        self.tt(hd[:, dr, :], p3[:, dr * 256:(dr + 1) * 256], dc, ALU.mult, R=[p3b, dcb], W=[hdb])
                    self.act(ab, hd.rearrange("p a c -> p (a c)"), AF.Abs, R=[hdb], W=[abb])
                    self.mm([(psN, self.ones, ab, i == 0, i == 15)], R=[abb, self.cb], W=[psNb])
                    m0 = FLG[:, FL_M0:FL_M0 + 1] if i == 0 else 1.0
                    nm0 = FLG[:, FL_M0 + 1:FL_M0 + 2] if i == 0 else -1.0
                    self.stt(hp[:, i, :], hd[:, 1, :], m0, hd[:, 0, :], ALU.mult, ALU.add, R=[hdb, self.flagsb], W=[hpb[i]])
                    self.stt(hm[:, i, :], hd[:, 1, :], nm0, hd[:, 0, :], ALU.mult, ALU.add, R=[hdb, self.flagsb], W=[hpb[i]])
            self.cp(rn, psN[:, 0:256], R=[psNb], W=[rnb], eng="act")
            self.tt(rn, rn, psN[:, 256:512], ALU.add, R=[rnb, psNb], W=[rnb])
            self.recip(rn, rn, R=[rnb], W=[rnb])
            self.ps_held.discard(iN)
            for j in range(16):
                cl, clb = self.wload(Dm["ClT"][j], 4096, cast=False)
                cl4 = cl.rearrange("p (t c f) -> p t c f", t=16, c=2)
                ps, pb = self.psum()
                mms = []
                for cs_ in range(2):
                    src = hp if cs_ == 0 else hm
                    for tc in range(16):
                        mms.append((ps[:, cs_ * 256:(cs_ + 1) * 256], cl4[:, tc, cs_, :], src[:, tc, :], tc == 0, tc == 15))
                self.mm(mms, R=[clb] + hpb, W=[pb])
                for cs_ in range(2):
                    self.tt(Ksp[:, j, cs_, :], ps[:, cs_ * 256:(cs_ + 1) * 256], rn, ALU.mult, R=[pb, rnb], W=[kspb])
            A.release(m)
            P.set_fence()

        def conv(o, xwhich, final):
            m = A.mark()
            vtok = A.alloc([16, 256], BF16); vtkb = P.buf("vtok")
            Y = A.alloc([32, 256], BF16); Yb = P.buf("Y")
            m1 = [A.alloc([512], F32) for _ in range(2)]; m2 = [A.alloc([512], F32) for _ in range(2)]
            m1b = P.bufs("m1", 2); m2b = P.bufs("m2", 2)
            for i in range(16):
                ps, pb = self.psum()
                self.mm([(ps[:, cc * 128:(cc + 1) * 128], vT[:, cc, i * 128:(i + 1) * 128], self.ident, True, True) for cc in range(2)],
                        R=[vTb[i // 4], self.cb], W=[pb])
                self.cp(vtok[:, i, :], ps[:, 0:256], R=[pb], W=[vtkb], eng=("act" if i % 2 else "dve"))
            for j in range(16):
                fw, fwb = self.wload(Dm["FwT"][j], 4096, cast=False)
                fw4 = fw.rearrange("p (t c f) -> p t c f", t=16, c=2)
                ps, pb = self.psum()
                mms = []
                for cs_ in range(2):
                    for tc in range(16):
                        mms.append((ps[:, cs_ * 256:(cs_ + 1) * 256], fw4[:, tc, cs_, :], vtok[:, tc, :], tc == 0, tc == 15))
                self.mm(mms, R=[fwb, vtkb], W=[pb])
                a1, a1b = m1[j % 2], m1b[j % 2]
                a2, a2b = m2[j % 2], m2b[j % 2]
                self.tt(a1, ps, Ksp[:, j, :, :].rearrange("p a c -> p (a c)"), ALU.mult, R=[pb, kspb], W=[a1b])
                self.tt(a2[:, 0:256], ps[:, 0:256], Ksp[:, j, 1, :], ALU.mult, R=[pb, kspb], W=[a2b])
                self.tt(a2[:, 256:512], ps[:, 256:512], Ksp[:, j, 0, :], ALU.mult, R=[pb, kspb], W=[a2b])
                self.tt(Y[:, j, :], a1[:, 0:256], a1[:, 256:512], ALU.subtract, R=[a1b], W=[Yb], eng="pool")
                self.tt(Y[:, 16 + j, :], a2[:, 0:256], a2[:, 256:512], ALU.add, R=[a2b], W=[Yb], eng="pool")
            for th in range(2):
                accs = [self.psum_hold() for _ in range(4)]
                for g in range(8):
                    gi, gib = self.wload(Dm["GiT"][th, g], 4096, cast=False)
                    gi3 = gi.rearrange("p (r t) -> p r t", r=4)
                    mms = []
                    for rr in range(4):
                        r = g * 4 + rr
                        for n in range(4):
                            cc, tbb = divmod(n, 2)
                            mms.append((accs[n][0], Y[:, r, cc * 128:(cc + 1) * 128], gi3[:, rr, tbb * 512:(tbb + 1) * 512], r == 0, r == 31))
                    self.mm(mms, R=[gib, Yb], W=[a[1] for a in accs])
                for n in range(4):
                    cc, tbb = divmod(n, 2)
                    tb = th * 2 + tbb
                    sl = slice(tb * 512, (tb + 1) * 512)
                    self.stt(vT[:, cc, sl], vT[:, cc, sl], V[:, V_HYBIAS + o * 2 + cc:V_HYBIAS + o * 2 + cc + 1], accs[n][0], ALU.mult, ALU.add,
                             R=[vTb[tb], accs[n][1], self.vecsb], W=[vTb[tb]])
                    self.ps_held.discard(accs[n][2])
            A.release(m)
            P.set_fence()
            m = A.mark()
            xT = A.alloc([2, T], BF16); xTb = P.bufs("hxT", 4)
            hy_proj(xwhich, xT, xTb)
            for tb in range(4):
                sl = slice(tb * 512, (tb + 1) * 512)
                dst = yT if final else vT
                dstb = yb if final else vTb
                self.tt(dst[:, :, sl], xT[:, :, sl], vT[:, :, sl], ALU.mult, R=[xTb[tb], vTb[tb]], W=[dstb[tb]])
            A.release(m)
            P.set_fence()

        hy_proj(0, vT, vTb)
        P.set_fence()
        filt(0)
        conv(0, 1, False)
        filt(1)
        conv(1, 2, True)

    def branch_mla(self, l, yT, yb):
        A, P, Dm = self.A, self.P, self.dram
        V = self.vecs
        scale = (64 + 32) ** -0.5
        wq, wqb = self.wload(Dm["w_uq"][l], 768)
        wq3 = wq[:, 0:768].rearrange("p (k c) -> p k c", k=2)
        WqA = A.alloc([4, 2, 128], BF16); WqB = A.alloc([4, 2, 64], BF16)
        wb_ = P.buf("mlaw")
        self.memset(WqA, 0.0, W=[wb_]); self.memset(WqB, 0.0, W=[wb_])
        for h in range(4):
            c0 = h * 96
            self.cp(WqA[:, h, :, 64:128], wq3[:, :, c0:c0 + 64], R=[wqb], W=[wb_])
            self.cp(WqA[:, h, :, 32:64], wq3[:, :, c0 + 64:c0 + 96], R=[wqb], W=[wb_], eng="act")
            self.ts(WqB[:, h, :, 32:48], wq3[:, :, c0 + 80:c0 + 96], -1.0, None, ALU.mult, R=[wqb], W=[wb_])
            self.cp(WqB[:, h, :, 48:64], wq3[:, :, c0 + 64:c0 + 80], R=[wqb], W=[wb_], eng="act")
        wkv, wkvb = self.wload(Dm["w_ukv"][l], 512)
        WkN = A.alloc([4, 128], BF16); WV = A.alloc([4, 64], BF16)
        self.memset(WkN, 0.0, W=[wb_])
        wkv3 = wkv[:, 0:512].rearrange("p (h c) -> p h c", h=4)
        self.cp(WkN[:, :, 64:128], wkv3[:, :, 0:64], R=[wkvb], W=[wb_])
        self.cp(WV, wkv3[:, :, 64:128], R=[wkvb], W=[wb_], eng="act")
        wi, wib = self.win_load(l, 768, 416)
        WkrA = A.alloc([8, 64], BF16); WkrB = A.alloc([8, 64], BF16)
        self.memset(WkrA, 0.0, W=[wb_]); self.memset(WkrB, 0.0, W=[wb_])
        self.cp(WkrA[:, :, 32:64], wi[:, :, 384:416], R=[wib], W=[wb_])
        self.ts(WkrB[:, :, 32:48], wi[:, :, 400:416], -1.0, None, ALU.mult, R=[wib], W=[wb_])
        self.cp(WkrB[:, :, 48:64], wi[:, :, 384:400], R=[wib], W=[wb_], eng="act")
        csl = [A.alloc([2, 512], F32) for _ in range(2)]; cslb = P.bufs("ropecs", 2)
        self._ncs = 0

        def load_cs(tb):
            j = self._ncs % 2; self._ncs += 1
            c, cb_ = csl[j], cslb[j]
            P.dma(lambda h: h.dma_start(out=c[32:64, :, :], in_=Dm["ropeCS"][:, :, tb * 512:(tb + 1) * 512].rearrange("a p t -> p a t")),
                  writes=[cb_], queue="sp")
            return c, cb_
        Ka = [A.alloc([NKEY], BF16)]; Kab = P.bufs("Kaug", 1)
        Qa = [A.alloc([T], BF16)]; Qab = P.bufs("Qaug", 1)
        for i in range(1):
            P.dma(lambda h, i=i: h.dma_start(out=Ka[i][0:32, :], in_=Dm["oh"]), writes=[Kab[i]], queue="pool")
            P.dma(lambda h, i=i: h.dma_start(out=Qa[i][0:32, :], in_=Dm["oh"][:, 0:T]), writes=[Qab[i]], queue="pool")
            P.dma(lambda h, i=i: h.dma_start(out=Ka[i][32:64, T:NKEY], in_=Dm["kropeC"][l]), writes=[Kab[i]], queue="pool")
        ckn = A.alloc([NKEY], BF16); cknb = P.bufs("ckn", 5)
        P.dma(lambda h: h.dma_start(out=ckn[:, T:NKEY], in_=Dm["cacheT"][l]), writes=[cknb[4]], queue="pool")
        cqn = A.alloc([2, T], BF16); cqnb = P.bufs("cqn", 4)
        Va = A.alloc([18, 4, 65], BF16); Vab = P.buf("Vaug")
        self.memset(Va[:, :, :, 64:65], 1.0, W=[Vab])
        sq = [A.alloc([512], BF16) for _ in range(2)]; sqb = P.bufs("msq", 2)
        rs = [A.alloc([512], F32) for _ in range(2)]; rsb = P.bufs("mrs", 2)
        t1 = A.alloc([512], F32); t2 = A.alloc([512], F32); t12b = P.bufs("mt", 2)
        stg = A.alloc([4, 128], F32); stgb = P.buf("stg")
        stk = A.alloc([4, 32], F32); stkb = P.buf("stk")
        krr = A.alloc([512], BF16); krrb = P.buf("krr")
        hall = lambda tb: [self.hb[k][tb] for k in range(8)]
        stop = 99
        sub = ""
        if stop <= 1:
            self.zero_branch(yT, yb); return
        for tb in range(4):
            sl = slice(tb * 512, (tb + 1) * 512)
            pcs = []
            pq, pqb = self.psum()
            for c in range(2):
                pc, pcb = self.psum()
                self.mm([(pc, wi[:, k, c * 128:(c + 1) * 128], self.hT[:, k, sl], k == 0, k == 7) for k in range(8)],
                        R=[wib] + hall(tb), W=[pcb])
                self.act(sq[c], pc, AF.Square, R=[pcb], W=[sqb[c]])
                self.mm([(pq, self.ones, sq[c], c == 0, c == 1)], R=[sqb[c], self.cb], W=[pqb])
                pcs.append((pc, pcb))
            self.act(rs[0], pq, AF.Sqrt, bias=self.eps_t, scale=1.0 / 256, R=[pqb], W=[rsb[0]])
            self.recip(rs[0], rs[0], R=[rsb[0]], W=[rsb[0]])
            for c in range(2):
                self.stt(cqn[:, c, sl], pcs[c][0], V[:, V_QN + c:V_QN + c + 1], rs[0], ALU.mult, ALU.mult,
                         R=[pcs[c][1], rsb[0], self.vecsb], W=[cqnb[tb]])
            pc, pcb = self.psum()
            self.mm([(pc, wi[:, k, 256:384], self.hT[:, k, sl], k == 0, k == 7) for k in range(8)], R=[wib] + hall(tb), W=[pcb])
            self.act(sq[0], pc, AF.Square, R=[pcb], W=[sqb[0]])
            pq, pqb = self.psum()
            self.mm([(pq, self.ones, sq[0], True, True)], R=[sqb[0], self.cb], W=[pqb])
            self.act(rs[1], pq, AF.Sqrt, bias=self.eps_t, scale=1.0 / 128, R=[pqb], W=[rsb[1]])
            self.recip(rs[1], rs[1], R=[rsb[1]], W=[rsb[1]])
            self.stt(ckn[:, sl], pc, V[:, V_KVN:V_KVN + 1], rs[1], ALU.mult, ALU.mult, R=[pcb, rsb[1], self.vecsb], W=[cknb[tb]])
            if "nock" in sub:
                continue
            pt, ptb = self.psum()
            self.mm([(pt[:, i * 128:(i + 1) * 128], ckn[:, tb * 512 + i * 128: tb * 512 + (i + 1) * 128], self.ident, True, True)
                     for i in range(4)], R=[cknb[tb], self.cb], W=[ptb])
            self.cp(stg, pt.rearrange("p (i c) -> p i c", i=4), R=[ptb], W=[stgb], eng="act")
            P.dma(lambda h, tb=tb: h.dma_start(out=self.outs["ckv_out"][l, tb * 512:(tb + 1) * 512, :].rearrange("(i p) c -> p i c", p=128), in_=stg),
                  reads=[stgb], queue="sp", out=True)
            if "nokr" in sub:
                continue
            pa, pab = self.psum()
            self.mm([(pa[0:64, :], WkrA[:, k, :], self.hT[:, k, sl], k == 0, k == 7) for k in range(8)], R=[wb_] + hall(tb), W=[pab])
            pb2, pbb = self.psum()
            self.mm([(pb2[0:64, :], WkrB[:, k, :], self.hT[:, k, sl], k == 0, k == 7) for k in range(8)], R=[wb_] + hall(tb), W=[pbb])
            if "mmonly" in sub:
                continue
            if "noact" not in sub:
                self.cp(krr[32:64, :], pa[32:64, :], R=[pab], W=[krrb], eng="act")
            if "nodve" in sub:
                continue
            if "nocs" in sub:
                self.tt(t1[32:64, :], pa[32:64, :], pa[32:64, :], ALU.mult, R=[pab], W=[t12b[0]])
                self.tt(t2[32:64, :], pb2[32:64, :], pb2[32:64, :], ALU.mult, R=[pbb], W=[t12b[1]])
                continue
            cs, csb = load_cs(tb)
            self.tt(t1[32:64, :], pa[32:64, :], cs[32:64, 0, :], ALU.mult, R=[pab, csb], W=[t12b[0]])
            self.tt(t2[32:64, :], pb2[32:64, :], cs[32:64, 1, :], ALU.mult, R=[pbb, csb], W=[t12b[1]])
            if "noadd" in sub:
                continue
            self.tt(Ka[0][32:64, sl], t1[32:64, :], t2[32:64, :], ALU.add, R=t12b, W=[Kab[0]])
            if "nokt" in sub:
                continue
            pt, ptb = self.psum()
            self.mm([(pt[:, i * 32:(i + 1) * 32], krr[32:64, i * 128:(i + 1) * 128], self.ident[32:64, 32:64], True, True)
                     for i in range(4)], R=[krrb, self.cb], W=[ptb])
            self.cp(stk, pt[:, 0:128].rearrange("p (i c) -> p i c", i=4), R=[ptb], W=[stkb], eng="act")
            P.dma(lambda h, tb=tb: h.dma_start(out=self.outs["krope_out"][l, tb * 512:(tb + 1) * 512, :].rearrange("(i p) c -> p i c", p=128), in_=stk),
                  reads=[stkb], queue="sp", out=True)
        if stop <= 2:
            self.zero_branch(yT, yb); return
        for kt in range(18):
            pv, pvb = self.psum()
            src_b = cknb[kt // 4] if kt < 16 else cknb[4]
            self.mm([(pv[:, 0:256], ckn[:, kt * 128:(kt + 1) * 128], WV.rearrange("p h c -> p (h c)"), True, True)],
                    R=[src_b, wb_], W=[pvb])
            self.cp(Va[:, kt, :, 0:64], pv[:, 0:256].rearrange("p (h c) -> p h c", h=4), R=[pvb], W=[Vab],
                    eng=("act" if kt % 2 else "dve"))
        if stop <= 3:
            self.zero_branch(yT, yb); return
        PT = [A.alloc([512], BF16) for _ in range(4)]; PTb = P.bufs("PT", 4)
        assert len(PT) == 4
        rrow = A.alloc([512], F32); rrowb = P.buf("rrow")
        rbs = A.alloc([512], F32); rbsb = P.buf("rbs")
        npt = 0
        for h in range(4):
            Kh, Khb = Ka[0], Kab[0]
            Qh, Qhb = Qa[0], Qab[0]
            for kb_ in range(5):
                n0 = kb_ * 512
                nn = 512 if kb_ < 4 else 256
                pk, pkb = self.psum()
                self.mm([(pk[:, 0:nn], WkN[:, h, :], ckn[:, n0:n0 + nn], True, True)], R=[wb_, cknb[kb_]], W=[pkb])
                self.cp(Kh[64:128, n0:n0 + nn], pk[64:128, 0:nn], R=[pkb], W=[Khb], eng=("act" if kb_ % 2 else "dve"))
            for tb in range(4):
                sl = slice(tb * 512, (tb + 1) * 512)
                pa, pab = self.psum()
                self.mm([(pa, WqA[:, h, k, :], cqn[:, k, sl], k == 0, k == 1) for k in range(2)], R=[wb_, cqnb[tb]], W=[pab])
                pb2, pbb = self.psum()
                self.mm([(pb2[0:64, :], WqB[:, h, k, :], cqn[:, k, sl], k == 0, k == 1) for k in range(2)], R=[wb_, cqnb[tb]], W=[pbb])
                self.cp(Qh[64:128, sl], pa[64:128, :], R=[pab], W=[Qhb], eng="act")
                cs, csb = load_cs(tb)
                self.tt(t1[32:64, :], pa[32:64, :], cs[32:64, 0, :], ALU.mult, R=[pab, csb], W=[t12b[0]])
                self.tt(t2[32:64, :], pb2[32:64, :], cs[32:64, 1, :], ALU.mult, R=[pbb, csb], W=[t12b[1]])
                self.tt(Qh[32:64, sl], t1[32:64, :], t2[32:64, :], ALU.add, R=t12b, W=[Qhb], eng="pool")
            if stop <= 4:
                if h == 3:
                    self.zero_branch(yT, yb)
                continue
            def tail(tb, po, pob, ih):
                sl = slice(tb * 512, (tb + 1) * 512)
                self.recip(rrow[64:65, :], po[64:65, :], R=[pob], W=[rrowb])
                pr, prb = self.psum()
                self.mm([(pr[0:64, :], self.onesf[64:65, 0:64], rrow[64:65, :], True, True)], R=[rrowb, self.cb], W=[prb])
                self.cp(rbs[0:64, :], pr[0:64, :], R=[prb], W=[rbsb], eng="act")
                p0 = (h % 2) * 64
                self.tt(yT[p0:p0 + 64, h // 2, sl], po[0:64, :], rbs[0:64, :], ALU.mult, R=[pob, rbsb], W=[yb[tb]])
                self.ps_held.discard(ih)

            def score(tb, kt):
                sl = slice(tb * 512, (tb + 1) * 512)
                pS, pSb = self.psum()
                self.mm([(pS, Kh[:, kt * 128:(kt + 1) * 128], Qh[:, sl], True, True)], R=[Khb, Qhb], W=[pSb])
                return pS, pSb

            pend = None
            for tb in range(4):
                po, pob, ih = self.psum_hold()
                q = [score(tb, 0), score(tb, 1), score(tb, 2)]
                if pend is not None:
                    tail(*pend)
                for kt in range(18):
                    pS, pSb = q.pop(0)
                    pt_, ptb_ = PT[npt % 4], PTb[npt % 4]; npt += 1
                    self.act(pt_, pS, AF.Exp, bias=self.negbig_t, scale=scale, R=[pSb], W=[ptb_])
                    if kt + 3 < 18:
                        q.append(score(tb, kt + 3))
                    self.mm([(po[0:65, :], Va[:, kt, h, :], pt_, kt == 0, kt == 17)], R=[Vab, ptb_], W=[pob])
                pend = (tb, po, pob, ih)
            tail(*pend)

    def branch_ret(self, l, yT, yb):
        A, P, Dm = self.A, self.P, self.dram
        V = self.vecs
        FLG = self.flags
        Dtab = A.alloc([4, 128], BF16)
        Qtab = A.alloc([4, 128], F32)
        Vtok = A.alloc([16, 256], BF16); vtb = P.bufs("vtok", 16)
        Sin = A.alloc([16, 256], BF16); sinb = [[P.buf(f"sin{i}_{hf}") for hf in range(2)] for i in range(16)]
        m1 = A.mark()
        KTAB = A.alloc([2, 4], F32)
        GCt = A.alloc([4, 64], F32)
        gc4 = A.alloc([4], F32)
        rc = A.alloc([642], F32); rcb = P.buf("rcon")
        P.dma(lambda h: h.dma_start(out=rc, in_=Dm["rcon"]), writes=[rcb], queue="sp")
        s0t = A.alloc([256], F32); s0b = P.buf("s0")
        P.dma(lambda h: h.dma_start(out=s0t, in_=Dm["s0"][l]), writes=[s0b], queue="sp")
        lg = A.alloc([12], F32); lgb = P.buf("lg")
        self.act(lg[:, 0:8], V[:, V_RDALL:V_RDALL + 8], AF.Exp, scale=-1.0, R=[self.vecsb], W=[lgb])
        self.act(lg[:, 8:12], V[:, V_RDFB:V_RDFB + 4], AF.Exp, scale=-1.0, R=[self.vecsb], W=[lgb])
        self.act(lg, lg, AF.Ln, bias=self.one1[:, 0:1], scale=1.0, R=[lgb, self.cb], W=[lgb])
        self.ts(lg, lg, -1.0, None, ALU.mult, R=[lgb], W=[lgb])
        tb_ = P.buf("rtab")
        e1 = A.alloc([128], F32); e2 = A.alloc([128], F32); eb = P.bufs("re", 2)
        for h in range(4):
            self.act(e1, rc[:, 0:128], AF.Exp, scale=lg[:, h:h + 1], R=[rcb, lgb], W=[eb[0]])
            self.stt(e1, e1, 0.125, rc[:, 128:256], ALU.mult, ALU.mult, R=[eb[0], rcb], W=[eb[0]])
            self.act(e2, rc[:, 256:384], AF.Exp, scale=lg[:, 4 + h:5 + h], R=[rcb, lgb], W=[eb[1]])
            self.stt(e2, e2, 0.125, rc[:, 384:512], ALU.mult, ALU.mult, R=[eb[1], rcb], W=[eb[1]])
            self.tt(Dtab[:, h, :], e1, e2, ALU.add, R=eb, W=[tb_])
            self.act(Qtab[:, h, :], rc[:, 512:640], AF.Exp, scale=lg[:, 8 + h:9 + h], R=[rcb, lgb], W=[tb_])
        for dr in range(2):
            self.act(KTAB[:, dr, :], lg[:, dr * 4:(dr + 1) * 4], AF.Exp, scale=rc[:, 640 + dr:641 + dr], R=[rcb, lgb], W=[tb_])
        self.ts(KTAB, KTAB, 0.125, None, ALU.mult, R=[tb_], W=[tb_])
        self.act(gc4, lg[:, 8:12], AF.Exp, scale=128.0, R=[lgb], W=[tb_])
        self.memset(GCt, 1.0, W=[tb_])
        for h in range(4):
            self.ts(GCt[:, h, :], GCt[:, h, :], gc4[:, h:h + 1], None, ALU.mult, R=[tb_], W=[tb_])
        wqk, wqkb = self.win_load(l, 1184, 512)
        wvg, wvgb = self.win_load(l, 1696, 512)
        Wk = wqk[:, :, 256:512]; wkb_ = wqkb
        Wvg = wvg; wvgb2 = wvgb
        hall = lambda tb: [self.hb[k][tb] for k in range(8)]
        KVst = A.alloc([16, 256], F32); kvb = P.bufs("kvst", 16)
        Kd2 = [A.alloc([4, 2, 64], BF16) for _ in range(2)]; kd2b = P.bufs("kd2", 2)
        def kv_mm(i, kd, kdb):
            pkv, pkvb = self.psum()
            self.mm([(pkv[:, h * 64:(h + 1) * 64], kd[:, h, :, :].rearrange("p a d -> p (a d)"), Vtok[:, i, h * 64:(h + 1) * 64], True, True)
                     for h in range(4)], R=[kdb, vtb[i]], W=[pkvb])
            self.cp(KVst[:, i, :], pkv[:, 0:256], R=[pkvb], W=[kvb[i]])

        pend_kv = None
        for i in range(16):
            tb = i // 4
            tsl = slice(i * 128, (i + 1) * 128)
            pk, pkb = self.psum()
            self.mm([(pk[:, 0:256], self.hT[:, k, tsl], Wk[:, k, :], k == 0, k == 7) for k in range(8)], R=[wkb_] + hall(tb), W=[pkb])
            pv, pvb = self.psum()
            self.mm([(pv[:, 0:256], self.hT[:, k, tsl], Wvg[:, k, 0:256], k == 0, k == 7) for k in range(8)], R=[wvgb2] + hall(tb), W=[pvb])
            self.cp(Vtok[:, i, :], pv[:, 0:256], R=[pvb], W=[vtb[i]], eng="act")
            kd = Kd2[i % 2]; kdb = kd2b[i % 2]
            for h in range(4):
                self.ts(kd[:, h, 0, :], pk[:, h * 64:(h + 1) * 64], KTAB[:, 0, h:h + 1], None, ALU.mult, R=[pkb, tb_], W=[kdb])
            for h in range(4):
                self.act(kd[:, h, 1, :], pk[:, h * 64:(h + 1) * 64], AF.Identity, scale=KTAB[:, 1, h:h + 1], R=[pkb, tb_], W=[kdb])
            if pend_kv is not None:
                kv_mm(*pend_kv)
            pend_kv = (i, kd, kdb)
        kv_mm(*pend_kv)
        Ust = A.alloc([8, 256], F32); ustbs = P.bufs("ust", 2)
        Utmp = A.alloc([256], F32)
        Scur = A.alloc([256], F32); scb_ = P.bufs("scur", 2)
        tmp = A.alloc([256], F32)
        GC2 = GCt.rearrange("p h e -> p (h e)")
        orders = [list(range(16)), list(range(15, -1, -1))]
        for hf in range(2):
            r0, r1 = hf * 64, hf * 64 + 64
            self.cp(Scur[r0:r1, :], s0t[r0:r1, :], R=[s0b], W=[scb_[hf]])
            self.cp(Sin[r0:r1, orders[hf][0], :], s0t[r0:r1, :], R=[s0b], W=[sinb[orders[hf][0]][hf]], eng="act")
        for n in range(16):
            for hf in range(2):
                r0, r1 = hf * 64, hf * 64 + 64
                order = orders[hf]
                sb = scb_[hf]
                i = order[n]
                self.tt(tmp[r0:r1, :], Scur[r0:r1, :], GC2[r0:r1, :], ALU.mult, R=[sb, tb_], W=[sb])
                is_out = (i % 2 == 1) if hf == 0 else (i % 2 == 0)
                dst = Ust[r0:r1, i // 2, :] if is_out else Utmp[r0:r1, :]
                self.tt(dst, tmp[r0:r1, :], KVst[r0:r1, i, :], ALU.add, R=[sb, kvb[i]], W=[sb, ustbs[hf]])
                if n < 15:
                    nxt = order[n + 1]
                    kcol = (FL_KF + nxt) if hf == 0 else (FL_KB + nxt)
                    self.ts(Scur[r0:r1, :], dst, FLG[:, kcol:kcol + 1][r0:r1, :], None, ALU.mult, R=[sb, ustbs[hf]], W=[sb])
                    self.cp(Sin[r0:r1, nxt, :], Scur[r0:r1, :], R=[sb], W=[sinb[nxt][hf]], eng="act")
        P.dma(lambda h: h.dma_start(out=self.outs["st_out"][l], in_=Ust), reads=ustbs, queue="sp", out=True)
        A.release(m1)
        P.set_fence()
        wqk, wqkb = self.win_load(l, 1184, 512)
        wvg, wvgb = self.win_load(l, 1696, 512)
        Wk = wqk[:, :, 256:512]; wkb_ = wqkb
        Wvg = wvg; wvgb2 = wvgb
        Wq2 = A.alloc([8, 4, 128], BF16); wq2b = P.buf("wq2")
        qv = wqk[:, :, 0:256].rearrange("p k (h c) -> p k h c", h=4)
        self.cp(Wq2[:, :, :, 0:64], qv, R=[wqkb], W=[wq2b])
        self.cp(Wq2[:, :, :, 64:128], qv, R=[wqkb], W=[wq2b], eng="act")
        QT = A.alloc([4, 512], BF16); qtb = P.buf("QT")
        Qd2 = A.alloc([4, 512], BF16); qd2b = P.buf("Qd2")
        KT = A.alloc([4, 512], BF16); ktb = P.buf("KT")
        sg4 = A.alloc([4, 512], BF16); sg4b = P.buf("sg4")
        NBF = 2
        AT = [A.alloc([512], BF16) for _ in range(2)]; atb = P.bufs("AT", 2)
        o32s = [A.alloc([512], F32) for _ in range(NBF)]; o32bs = P.bufs("o32", NBF)
        obfs = [A.alloc([512], BF16) for _ in range(NBF)]; obfbs = P.bufs("obf", NBF)
        xcs = [A.alloc([512], F32)] * NBF; xcbs = [P.buf("rxc")] * NBF
        sqs = [A.alloc([512], BF16)] * NBF; sqbs = [P.buf("rsq")] * NBF
        rss = [A.alloc([512], F32)] * NBF; rsbs = [P.buf("rrs")] * NBF

        def stageA(tb, ci):
            i = tb * 4 + ci
            csl = slice(ci * 128, (ci + 1) * 128)
            pa, pab = self.psum()
            self.mm([(pa[:, h * 128:(h + 1) * 128], KT[0:64, h, csl], QT[0:64, h, csl], True, True) for h in range(4)],
                    R=[ktb, qtb], W=[pab])
            at = AT[i % 2]; atbb = atb[i % 2]
            self.tt(at, pa, Dtab.rearrange("p h n -> p (h n)"), ALU.mult, R=[pab, tb_], W=[atbb])
            po, pob = self.psum()
            mms = []
            for h in range(4):
                mms.append((po[0:64, h * 128:(h + 1) * 128], Vtok[:, i, h * 64:(h + 1) * 64], at[:, h * 128:(h + 1) * 128], True, False))
                mms.append((po[0:64, h * 128:(h + 1) * 128], Sin[:, i, h * 64:(h + 1) * 64], Qd2[:, h, csl], False, True))
            self.mm(mms, R=[vtb[i], atbb, sinb[i][0], sinb[i][1], qd2b], W=[pob])
            j = i % NBF
            self.cp(o32s[j][0:64, :], po[0:64, :], R=[pob], W=[o32bs[j]], eng="act")
            self.cp(obfs[j][0:64, :], po[0:64, :], R=[pob], W=[obfbs[j]], eng="act")

        def stageB(tb, ci):
            i = tb * 4 + ci
            j = i % NBF
            pm, pmb = self.psum()
            self.mm([(pm[0:64, :], self.ones[0:64, 0:64], obfs[j][0:64, :], True, True)], R=[obfbs[j], self.cb], W=[pmb])
            self.stt(xcs[j][0:64, :], pm[0:64, :], -1.0 / 64, o32s[j][0:64, :], ALU.mult, ALU.add, R=[pmb, o32bs[j]], W=[xcbs[j]])
            self.act(sqs[j][0:64, :], xcs[j][0:64, :], AF.Square, R=[xcbs[j]], W=[sqbs[j]])

        def stageC(tb, ci):
            i = tb * 4 + ci
            j = i % NBF
            csl = slice(ci * 128, (ci + 1) * 128)
            pv, pvb = self.psum()
            self.mm([(pv[0:64, :], self.ones[0:64, 0:64], sqs[j][0:64, :], True, True)], R=[sqbs[j], self.cb], W=[pvb])
            self.act(rss[j][0:64, :], pv[0:64, :], AF.Sqrt, bias=self.eps_t[0:64, :], scale=1.0 / 64, R=[pvb], W=[rsbs[j]])
            self.recip(rss[j][0:64, :], rss[j][0:64, :], R=[rsbs[j]], W=[rsbs[j]])
            self.tt(xcs[j][0:64, :], xcs[j][0:64, :], rss[j][0:64, :], ALU.mult, R=[xcbs[j], rsbs[j]], W=[xcbs[j]], eng="pool")
            for h in range(4):
                p0 = (h % 2) * 64
                self.tt(yT[p0:p0 + 64, h // 2, i * 128:(i + 1) * 128], xcs[j][0:64, h * 128:(h + 1) * 128], sg4[0:64, h, csl], ALU.mult,
                        R=[xcbs[j], sg4b], W=[yb[tb]])

        for tb in range(4):
            sl = slice(tb * 512, (tb + 1) * 512)
            for h in range(4):
                pq, pqb = self.psum()
                self.mm([(pq, Wq2[:, k, h, :], self.hT[:, k, sl], k == 0, k == 7) for k in range(8)], R=[wq2b] + hall(tb), W=[pqb])
                self.cp(QT[0:64, h, :], pq[0:64, :], R=[pqb], W=[qtb], eng="act")
                for ci in range(4):
                    self.tt(Qd2[:, h, ci * 128:(ci + 1) * 128], pq[:, ci * 128:(ci + 1) * 128], Qtab[:, h, :], ALU.mult,
                            R=[pqb, tb_], W=[qd2b])
                pk, pkb = self.psum()
                self.mm([(pk[0:64, :], Wk[:, k, h * 64:(h + 1) * 64], self.hT[:, k, sl], k == 0, k == 7) for k in range(8)], R=[wkb_] + hall(tb), W=[pkb])
                self.cp(KT[0:64, h, :], pk[0:64, :], R=[pkb], W=[ktb], eng="act")
                pg, pgb = self.psum()
                self.mm([(pg[0:64, :], Wvg[:, k, 256 + h * 64:256 + (h + 1) * 64], self.hT[:, k, sl], k == 0, k == 7) for k in range(8)], R=[wvgb2] + hall(tb), W=[pgb])
                self.act(sg4[0:64, h, :], pg[0:64, :], AF.Silu, R=[pgb], W=[sg4b])
            for s_ in range(6):
                if s_ < 4:
                    stageA(tb, s_)
                if 0 <= s_ - 2 < 4:
                    stageC(tb, s_ - 2)
                if 0 <= s_ - 1 < 4:
                    stageB(tb, s_ - 1)

    def win_load(self, l, c0, ncol):
        assert ncol * 8 <= RING_ELEMS
        i = self.ring_next
        self.ring_next = (i + 1) % RING_SLOTS
        dst = self.ring[i][:, 0:8 * ncol].rearrange("p (k c) -> p k c", k=8)
        src = self.dram["w_in"][l][:, :, c0:c0 + ncol]
        b = self.ringb[i]
        self.P.dma(lambda h: h.dma_start(out=dst, in_=src), writes=[b], queue="pool", nofence=True)
        return dst, b

    def branch_conf(self, l, yT, yb):
        A, P = self.A, self.P
        V = self.vecs
        zp = A.alloc([2, NSEG, 286], BF16)
        zpb = P.bufs("zp", 4)
        diag = A.alloc([62, 128], BF16)
        diagb = P.buf("diag")
        for k in range(31):
            for ch in range(2):
                self.ts(diag[:, k * 2 + ch, :], self.ident, V[:, V_CDW + k * 2 + ch:V_CDW + k * 2 + ch + 1], None,
                        ALU.mult, R=[self.cb, self.vecsb], W=[diagb])
        sig = [A.alloc([512], F32) for _ in range(2)]
        sigb = P.bufs("sig", 2)
        wa, wab = self.win_load(l, 2208, 512)
        n = 0
        for ch in range(2):
            for tb in range(4):
                sl = slice(tb * 512, (tb + 1) * 512)
                pa, pab = self.psum()
                self.mm([(pa, wa[:, k, ch * 128:(ch + 1) * 128], self.hT[:, k, sl], k == 0, k == 7) for k in range(8)],
                        R=[wab] + [self.hb[k][tb] for k in range(8)], W=[pab])
                pbb, pbbb = self.psum()
                self.mm([(pbb, wa[:, k, 256 + ch * 128:256 + (ch + 1) * 128], self.hT[:, k, sl], k == 0, k == 7) for k in range(8)],
                        R=[wab] + [self.hb[k][tb] for k in range(8)], W=[pbbb])
                s = sig[n % 2]; sb = sigb[n % 2]; n += 1
                self.act(s, pbb, AF.Sigmoid, R=[pbbb], W=[sb])
                self.tt(zp[:, ch, 2 * tb:2 * tb + 2, 15:271], pa.rearrange("p (s t) -> p s t", s=2),
                        s.rearrange("p (s t) -> p s t", s=2), ALU.mult, R=[pab, sb], W=[zpb[tb]])
        mcol = self.flags[:, FL_HALO:FL_HALO + 1]
        for ch in range(2):
            self.memset(zp[:, ch, 0, 0:15], 0.0, W=zpb)
            self.memset(zp[:, ch, 7, 271:286], 0.0, W=zpb)
            self.ts(zp[:, ch, 1:8, 0:15], zp[:, ch, 0:7, 256:271], mcol, None, ALU.mult, R=zpb + [self.flagsb], W=zpb)
            self.ts(zp[:, ch, 0:7, 271:286], zp[:, ch, 1:8, 15:30], mcol, None, ALU.mult, R=zpb + [self.flagsb], W=zpb)
        cv = A.alloc([2, 512], F32)
        cvb = P.bufs("cv", 2)
        xc = A.alloc([2, 512], F32)
        xcb = P.buf("xc")
        sq = A.alloc([2, 512], BF16)
        sqb = P.buf("csq")
        rs = A.alloc([512], F32)
        rsb = P.buf("crs")
        for tb in range(4):
            sl = slice(tb * 512, (tb + 1) * 512)
            for ch in range(2):
                ps, pb = self.psum()
                self.mm([(ps.rearrange("p (s t) -> p s t", s=2), diag[:, k * 2 + ch, :], zp[:, ch, 2 * tb:2 * tb + 2, k:k + 256], k == 0, k == 30)
                         for k in range(31)], R=[diagb] + zpb, W=[pb])
                self.act(cv[:, ch, :], ps, AF.Identity, bias=V[:, V_CDB + ch:V_CDB + ch + 1], R=[pb, self.vecsb], W=[cvb[ch]])
            pm, pmb = self.psum()
            cvh = sq
            for ch in range(2):
                self.cp(cvh[:, ch, :], cv[:, ch, :], R=[cvb[ch]], W=[sqb])
            self.mm([(pm, self.ones, cvh[:, ch, :], ch == 0, ch == 1) for ch in range(2)], R=[sqb, self.cb], W=[pmb])
            for ch in range(2):
                self.stt(xc[:, ch, :], pm, -1.0 / 256, cv[:, ch, :], ALU.mult, ALU.add, R=[pmb, cvb[ch]], W=[xcb])
            for ch in range(2):
                self.act(sq[:, ch, :], xc[:, ch, :], AF.Square, R=[xcb], W=[sqb])
            pv, pvb = self.psum()
            self.mm([(pv, self.ones, sq[:, ch, :], ch == 0, ch == 1) for ch in range(2)], R=[sqb, self.cb], W=[pvb])
            self.act(rs, pv, AF.Sqrt, bias=self.eps_t, scale=1.0 / 256, R=[pvb], W=[rsb])
            self.recip(rs, rs, R=[rsb], W=[rsb])
            for ch in range(2):
                self.stt(xc[:, ch, :], xc[:, ch, :], V[:, V_CLG + ch:V_CLG + ch + 1], rs, ALU.mult, ALU.mult,
                         R=[xcb, rsb, self.vecsb], W=[xcb])
                self.act(yT[:, ch, sl], xc[:, ch, :], AF.Silu, bias=V[:, V_CLB + ch:V_CLB + ch + 1], R=[xcb, self.vecsb], W=[yb[tb]])

    def merge(self, l, ys):
        A, P, Dm = self.A, self.P, self.dram
        V = self.vecs
        mg = A.alloc([8, T], BF16)
        mgb = [[P.buf(f"mg{j}_{tb}") for tb in range(4)] for j in range(8)]
        acc = [A.alloc([512], F32) for _ in range(2)]
        accb = P.bufs("acc", 2)
        sg = [A.alloc([512], F32) for _ in range(3)]
        sgb = P.bufs("sg", 3)
        n = 0
        na = 0
        for j in range(8):
            gw, gwb = self.wload(Dm["gate_w"][l, j], 4096)
            gw3 = gw.rearrange("p (k c) -> p k c", k=8)
            owj, owb = self.wload(Dm["outw"][l, j], 1024)
            ow4 = owj[:, 0:1024].rearrange("p (i k c) -> p i k c", i=4, k=2)
            for tb in range(4):
                sl = slice(tb * 512, (tb + 1) * 512)
                a = acc[na % 2]; ab = accb[na % 2]; na += 1
                for i in range(4):
                    yT, yb = ys[i]
                    pg, pgb = self.psum()
                    self.mm([(pg, gw3[:, k, i * 128:(i + 1) * 128], self.hT[:, k, sl], k == 0, k == 7) for k in range(8)],
                            R=[gwb] + [self.hb[k][tb] for k in range(8)], W=[pgb])
                    po, pob = self.psum()
                    self.mm([(po, ow4[:, i, k, :], yT[:, k, sl], k == 0, k == 1) for k in range(2)],
                            R=[owb, yb[tb]], W=[pob])
                    s = sg[n % 3]; sb = sgb[n % 3]; n += 1
                    self.act(s, pg, AF.Sigmoid, bias=V[:, V_GATEB + i * 8 + j:V_GATEB + i * 8 + j + 1], R=[pgb, self.vecsb], W=[sb])
                    if i == 0:
                        self.tt(a, s, po, ALU.mult, R=[sb, pob], W=[ab])
                    elif i < 3:
                        self.tt(s, s, po, ALU.mult, R=[sb, pob], W=[sb])
                        self.tt(a, a, s, ALU.add, R=[sb, ab], W=[ab], eng="pool")
                    else:
                        self.tt(s, s, po, ALU.mult, R=[sb, pob], W=[sb])
                        self.tt(mg[:, j, sl], a, s, ALU.add, R=[sb, ab], W=[mgb[j][tb]], eng="pool")
        self.dump(f"mg{l}", mg, [b for r in mgb for b in r], [128, 8, T])
        for jg in range(2):
            w, wb = self.wload(Dm["w_o"][l, jg], 4096)
            w3 = w.rearrange("p (k c) -> p k c", k=8)
            for jj in range(4):
                j = jg * 4 + jj
                for tb in range(4):
                    sl = slice(tb * 512, (tb + 1) * 512)
                    ps, pb = self.psum()
                    self.mm([(ps, w3[:, k, jj * 128:(jj + 1) * 128], mg[:, k, sl], k == 0, k == 7) for k in range(8)],
                            R=[wb] + [mgb[k][tb] for k in range(8)], W=[pb])
                    self.stt(self.xT[:, j, sl], ps, self.modT[:, 16 + j:17 + j], self.xT[:, j, sl], ALU.mult, ALU.add,
                             R=[pb, self.modb, self.xb[j][tb]], W=[self.xb[j][tb]])

    def ffn(self, l):
        A, P, Dm = self.A, self.P, self.dram
        m = A.mark()
        hid = A.alloc([11, T], BF16)
        hidb = [[P.buf(f"hid{f}_{tb}") for tb in range(4)] for f in range(11)]
        sa = [A.alloc([512], F32) for _ in range(3)]
        sab = P.bufs("sa", 3)
        n = 0
        for half in range(2):
            for ff in range(11):
                f = half * 11 + ff
                w, wb = self.wload(Dm["ffn_w1"][l, f], 2048)
                w3 = w[:, 0:2048].rearrange("p (k c) -> p k c", k=8)
                for tb in range(4):
                    sl = slice(tb * 512, (tb + 1) * 512)
                    pa, pab = self.psum()
                    self.mm([(pa, w3[:, k, 0:128], self.hT[:, k, sl], k == 0, k == 7) for k in range(8)],
                            R=[wb] + [self.hb[k][tb] for k in range(8)], W=[pab])
                    pb_, pbb = self.psum()
                    self.mm([(pb_, w3[:, k, 128:256], self.hT[:, k, sl], k == 0, k == 7) for k in range(8)],
                            R=[wb] + [self.hb[k][tb] for k in range(8)], W=[pbb])
                    s = sa[n % 3]; sb = sab[n % 3]; n += 1
                    self.act(s, pa, AF.Silu, R=[pab], W=[sb])
                    self.tt(hid[:, ff, sl], s, pb_, ALU.mult, R=[sb, pbb], W=[hidb[ff][tb]])
            for jp in range(4):
                w, wb = self.wload(Dm["ffn_w2"][l, half * 4 + jp], 11 * 256)
                w3 = w[:, 0:11 * 256].rearrange("p (f c) -> p f c", f=11)
                for jj in range(2):
                    j = jp * 2 + jj
                    for tb in range(4):
                        sl = slice(tb * 512, (tb + 1) * 512)
                        ps, pb = self.psum()
                        self.mm([(ps, w3[:, ff, jj * 128:(jj + 1) * 128], hid[:, ff, sl], ff == 0, ff == 10) for ff in range(11)],
                                R=[wb] + [hidb[ff][tb] for ff in range(11)], W=[pb])
                        self.stt(self.xT[:, j, sl], ps, self.modT[:, 40 + j:41 + j], self.xT[:, j, sl], ALU.mult, ALU.add,
                                 R=[pb, self.modb, self.xb[j][tb]], W=[self.xb[j][tb]])
        A.release(m)
        P.set_fence()

    def final(self):
        P = self.P
        self.norm(self.fing, None, inplace=True)
        for k in range(8):
            src = self.xT[:, k, :]
            dst = self.outs["yT"][:, k, :]
            P.dma(lambda h, s=src, t=dst: h.dma_start(out=t, in_=s), reads=self.xb[k], queue="sp", out=True)


def _chunkT(v, n):
    return np.ascontiguousarray(np.asarray(v, np.float32).reshape(n, 128).T)


_HYC = {}


def _hy_consts(L, pos):
    key = L
    if key in _HYC:
        return _HYC[key]
    f32 = np.float32
    N = 4096
    zT = np.zeros((33, T), f32)
    j = np.arange(L)
    t = np.linspace(0.0, 1.0, L, dtype=f32)
    w = (2.0 * math.pi * j.astype(f32) / L).astype(f32)
    fb = np.linspace(1e-4, 15, 16, dtype=f32)
    zT[0, :L] = t
    zT[1:17, :L] = np.cos(fb[:, None] * w[None, :])
    zT[17:33, :L] = -np.sin(fb[:, None] * w[None, :])
    max_decay = math.log(1e-2) / 0.3
    min_decay = math.log(1e-2) / 1.5
    deltas = np.abs(np.linspace(min_decay, max_decay, 256, dtype=f32))
    dec = np.zeros((T, 256), f32)
    dec[:L] = np.exp(-t[:, None] * deltas[None, :])
    decay = np.ascontiguousarray(dec.reshape(16, 128, 256).transpose(1, 0, 2))
    om = 2.0 * math.pi * (np.arange(2048, dtype=np.float64) + 0.5) / N

    def fwd_units(positions):
        ang = positions.astype(np.float64)[:, None] * om[None, :]
        c = np.cos(ang).astype(f32); s_ = np.sin(ang).astype(f32)
        cs = np.stack([c, s_], 0)
        u = cs.reshape(2, 16, 128, 16, 128).transpose(3, 2, 1, 0, 4)
        return np.ascontiguousarray(u.reshape(16, 128, 4096)).astype(NB)

    FwT = fwd_units(np.asarray(pos))
    lagpos = np.arange(T)
    ClT = FwT if L == 2048 else fwd_units(lagpos)
    ang = om[:, None] * np.asarray(pos).astype(np.float64)[None, :]
    g = np.concatenate([np.cos(ang), np.sin(ang)], 0) * (2.0 / N)
    g = g.astype(f32).reshape(8, 4, 128, 2, 1024).transpose(3, 0, 2, 1, 4)
    GiT = np.ascontiguousarray(g.reshape(2, 8, 128, 4096)).astype(NB)
    _HYC[key] = dict(zT=zT, decay=decay, ClT=ClT, FwT=FwT, GiT=GiT)
    return _HYC[key]


def host_prep(inp):
    f32 = np.float32
    shared = {}
    vecs = np.zeros((DEPTH, 128, NV), f32)
    for l in range(DEPTH):
        v = vecs[l]
        v[:, V_N1G:V_N1G + 8] = _chunkT(inp["norm1_g"][l], 8)
        v[:, V_N2G:V_N2G + 8] = _chunkT(inp["norm2_g"][l], 8)
        v[:, V_ADAB:V_ADAB + 48] = _chunkT(inp["ada_b"][l], 48)
        gb = np.asarray(inp["gate_b"][l], f32).reshape(4, 8, 128)
        v[:, V_GATEB:V_GATEB + 32] = gb.transpose(2, 0, 1).reshape(128, 32)
        cw = np.asarray(inp["hy_conv_w"][l], f32).reshape(3, 6, 128)
        v[:, V_HYCW:V_HYCW + 18] = cw.transpose(2, 0, 1).reshape(128, 18)
        v[:, V_HYCB:V_HYCB + 6] = _chunkT(inp["hy_conv_b"][l], 6)
        hb = np.asarray(inp["hy_bias"][l], f32).reshape(2, 2, 128)
        v[:, V_HYBIAS:V_HYBIAS + 4] = hb.transpose(2, 0, 1).reshape(128, 4)
        v[:, V_QN:V_QN + 2] = _chunkT(inp["mla_q_norm"][l], 2)
        v[:, V_KVN:V_KVN + 1] = _chunkT(inp["mla_kv_norm"][l], 1)
        dw = np.asarray(inp["conf_dw_w"][l], f32).reshape(31, 2, 128)
        v[:, V_CDW:V_CDW + 62] = dw.transpose(2, 0, 1).reshape(128, 62)
        v[:, V_CDB:V_CDB + 2] = _chunkT(inp["conf_dw_b"][l], 2)
        v[:, V_CLG:V_CLG + 2] = _chunkT(inp["conf_ln_g"][l], 2)
        v[:, V_CLB:V_CLB + 2] = _chunkT(inp["conf_ln_b"][l], 2)
        rd = np.asarray(inp["ret_decay"][l], f32)
        v[:, V_RDALL:V_RDALL + 8] = rd.reshape(1, 8)
        v[0:64, V_RDFB:V_RDFB + 4] = rd[0][None]
        v[64:128, V_RDFB:V_RDFB + 4] = rd[1][None]
        v[0:64, V_HB1] = np.asarray(inp["hy_b1"][l], f32)
        v[0:64, V_HB2] = np.asarray(inp["hy_b2"][l], f32)
    shared["vecs"] = vecs
    shared["final_g"] = _chunkT(inp["final_norm_g"], 8)
    shared["ident"] = np.eye(128, dtype=f32)

    def kmajor(w, ncol_unit):
        Dp, KK, C = w.shape
        K = KK // 128
        nu = C // ncol_unit
        a = np.asarray(w, f32).reshape(Dp, K, 128, nu, ncol_unit).transpose(0, 3, 2, 1, 4)
        return np.ascontiguousarray(a.reshape(Dp, nu, 128, K * ncol_unit))

    shared["ada_w"] = kmajor(inp["ada_w"], 512)
    shared["w_in"] = np.ascontiguousarray(np.asarray(inp["w_in"], f32).reshape(DEPTH, 8, 128, 2720).transpose(0, 2, 1, 3))
    gw = np.asarray(inp["gate_w"], f32).reshape(DEPTH, 8, 128, 4, 8, 128)
    shared["gate_w"] = np.ascontiguousarray(gw.transpose(0, 4, 2, 1, 3, 5).reshape(DEPTH, 8, 128, 8 * 512))
    shared["w_o"] = kmajor(inp["w_o"], 512)
    w1 = np.asarray(inp["ffn_w1"], f32).reshape(DEPTH, 8, 128, 2, 22, 128)
    shared["ffn_w1"] = np.ascontiguousarray(w1.transpose(0, 4, 2, 1, 3, 5).reshape(DEPTH, 22, 128, 8 * 256))
    w2 = np.asarray(inp["ffn_w2"], f32).reshape(DEPTH, 2, 11, 128, 4, 256)
    shared["ffn_w2"] = np.ascontiguousarray(w2.transpose(0, 1, 4, 3, 2, 5).reshape(DEPTH, 8, 128, 11 * 256))
    ow = np.stack([np.asarray(inp[nm], f32) for nm in ("hy_out", "mla_out", "ret_out", "conf_out")], 1)
    ow = ow.reshape(DEPTH, 4, 2, 128, 8, 128).transpose(0, 4, 3, 1, 2, 5)
    ow = np.ascontiguousarray(ow.reshape(DEPTH, 8, 128, 1024))
    shared["outw"] = ow

    shared["w_uq"] = np.ascontiguousarray(np.asarray(inp["mla_w_uq"], f32).reshape(DEPTH, 2, 128, 384).transpose(0, 2, 1, 3).reshape(DEPTH, 128, 768))
    shared["w_ukv"] = np.ascontiguousarray(np.asarray(inp["mla_w_ukv"], f32))
    tt_ = np.arange(T)
    inv = (10000.0 ** (-np.arange(8, dtype=np.float64) / 8))
    ang = np.concatenate([(tt_ // 64)[:, None] * inv, (tt_ % 64)[:, None] * inv], axis=1)
    cos_s = np.cos(ang).T.astype(f32); sin_s = np.sin(ang).T.astype(f32)
    rope_sample = np.stack([np.concatenate([cos_s, cos_s], 0), np.concatenate([sin_s, sin_s], 0)], 0)
    rope_prompt = np.stack([np.ones((32, T), f32), np.zeros((32, T), f32)], 0)
    oh_sample = np.zeros((32, NKEY), f32); oh_sample[0, :] = BIGC
    oh_prompt = np.zeros((32, NKEY), f32)
    for sgi in range(8):
        oh_prompt[sgi, sgi * 256:(sgi + 1) * 256] = BIGC
    zc = np.zeros((DEPTH, 128, 256), f32); zk = np.zeros((DEPTH, 32, 256), f32)
    rcon = np.zeros((128, 642), f32)
    mm_ = np.arange(128)[:, None]; nn_ = np.arange(128)[None, :]
    rcon[:, 0:128] = np.maximum(nn_ - mm_, 0); rcon[:, 128:256] = (nn_ >= mm_)
    rcon[:, 256:384] = np.maximum(mm_ - nn_, 0); rcon[:, 384:512] = (mm_ >= nn_)
    rcon[0:64, 512:640] = nn_ + 1; rcon[64:128, 512:640] = 128 - nn_
    rcon[:, 640] = 127 - np.arange(128); rcon[:, 641] = np.arange(128)
    shared["rcon"] = rcon
    zs0 = np.zeros((DEPTH, 128, 256), f32)

    for nm in ("hy_w1", "hy_w2", "hy_w3"):
        shared[nm] = np.ascontiguousarray(np.asarray(inp[nm], f32))
    hy_s = _hy_consts(2048, np.arange(T))
    hy_p = _hy_consts(256, 512 * (np.arange(T) // 256) + (np.arange(T) % 256))

    in_maps = []
    for core in range(8):
        m = dict(shared)
        if core < 4:
            m["cacheT"] = np.ascontiguousarray(np.asarray(inp["cache_mla_ckv"][core], f32).transpose(0, 2, 1))
            m["kropeC"] = np.ascontiguousarray(np.asarray(inp["cache_mla_krope"][core], f32).transpose(0, 2, 1))
            m["oh"] = oh_sample; m["ropeCS"] = rope_sample
            m["s0"] = np.ascontiguousarray(np.asarray(inp["state_ret"][core], f32).transpose(0, 1, 3, 2, 4).reshape(DEPTH, 128, 256))
        else:
            m["cacheT"] = zc; m["kropeC"] = zk; m["oh"] = oh_prompt; m["ropeCS"] = rope_prompt
            m["s0"] = zs0
        if core < 4:
            x = np.asarray(inp["x_sample"][core], f32)
            cond = np.asarray(inp["c"][core], f32)
        else:
            b0 = 8 * (core - 4)
            x = np.asarray(inp["x_prompt"][b0:b0 + 8], f32).reshape(T, D)
            cond = np.asarray(inp["c_ctx"], f32)
        m["xT"] = np.ascontiguousarray(x.T.reshape(8, 128, T).transpose(1, 0, 2))
        m["condT"] = _chunkT(cond, 8)
        fl = np.zeros((128, NFL), f32)
        fl[:, FL_HALO] = 1.0 if core < 4 else 0.0
        for i in range(16):
            fl[:, FL_KF + i] = 1.0 if core < 4 else (0.0 if i % 2 == 0 else 1.0)
            fl[:, FL_KB + i] = 1.0 if core < 4 else (0.0 if i % 2 == 1 else 1.0)
        fl[:, FL_M0] = 1.0; fl[0, FL_M0] = 0.0
        fl[:, FL_M0 + 1] = -1.0; fl[0, FL_M0 + 1] = 0.0
        m["flags"] = fl
        hc = hy_s if core < 4 else hy_p
        m["zT"] = hc["zT"]; m["decay"] = hc["decay"]; m["ClT"] = hc["ClT"]; m["FwT"] = hc["FwT"]; m["GiT"] = hc["GiT"]
        in_maps.append(m)
    return in_maps


_CACHE = {}


def run(inp, dbg=()):
    key = tuple(sorted(dbg))
    if key not in _CACHE:
        kb = KB(dbg)
        kb.build()
        _CACHE[key] = kb
    kb = _CACHE[key]
    in_maps = host_prep(inp)
    names = set(kb.dram.keys())
    in_maps = [{k: v for k, v in m.items() if k in names} for m in in_maps]
    res = run_bass_kernel_spmd(kb.nc, in_maps, core_ids=list(range(8)))
    return res.results


def kernel(**inputs):
    res = run(inputs)
    f32 = np.float32
    yp = np.zeros((32, 256, D), f32)
    ys = np.zeros((4, 2048, D), f32)
    for core in range(8):
        yT = np.asarray(res[core]["yT"], f32)
        y = yT.transpose(1, 0, 2).reshape(D, T).T
        if core < 4:
            ys[core] = y
        else:
            yp[8 * (core - 4):8 * (core - 4) + 8] = y.reshape(8, 256, D)
    ckv = np.zeros((32, DEPTH, 256, 128), f32)
    krope = np.zeros((32, DEPTH, 256, 32), f32)
    for core in range(4, 8):
        b0 = 8 * (core - 4)
        ckv[b0:b0 + 8] = np.asarray(res[core]["ckv_out"], f32).reshape(DEPTH, 8, 256, 128).transpose(1, 0, 2, 3)
        krope[b0:b0 + 8] = np.asarray(res[core]["krope_out"], f32).reshape(DEPTH, 8, 256, 32).transpose(1, 0, 2, 3)
    st = np.zeros((32, DEPTH, 2, 4, 64, 64), f32)
    for core in range(4, 8):
        b0 = 8 * (core - 4)
        so = np.asarray(res[core]["st_out"], f32).reshape(DEPTH, 2, 64, 8, 4, 64)
        st[b0:b0 + 8] = so.transpose(3, 0, 1, 4, 2, 5)
    return (yp, ys, ckv, krope, st)
```

```python
import contextlib
import math
import numpy as np
import ml_dtypes
import concourse.bass as bass
import concourse.mybir as mybir
from concourse.bass_utils import run_bass_kernel_spmd

F32 = mybir.dt.float32
BF16 = mybir.dt.bfloat16
I32 = mybir.dt.int32
AF = mybir.ActivationFunctionType
ALU = mybir.AluOpType

NB = ml_dtypes.bfloat16
D = 1024
T = 2048
NSEG = 8
SEGL = 256
DEPTH = 2
EPS = 1e-6
DFF = 2816
NKEY = 2304
BIGC = 32.0

V_N1G, V_N2G, V_ADAB, V_GATEB, V_FING = 0, 8, 16, 64, 96
V_HYCW, V_HYCB, V_HYBIAS = 104, 122, 128
V_QN, V_KVN = 132, 134
V_CDW, V_CDB, V_CLG, V_CLB = 135, 197, 199, 201
V_RDALL, V_RDFB, V_HB1, V_HB2 = 203, 211, 215, 216
NV = 224
FL_HALO, FL_KF, FL_KB, FL_M0 = 0, 1, 17, 33
NFL = 40

ENGS = ("pe", "act", "dve", "pool", "sp")
SAME_ENGINE_SYNC = {"pe": False, "act": True, "dve": True, "pool": True, "sp": False}


class Buf:
    __slots__ = ("name", "wr", "rd", "dma_sem", "dma_cnt", "const", "excl", "scratch")

    def __init__(self, name):
        self.name = name
        self.wr = None
        self.rd = []
        self.dma_sem = None
        self.dma_cnt = 0
        self.const = False
        self.scratch = True
        self.excl = False


class Prog:
    def __init__(self, nc):
        self.nc = nc
        self.ops = {e: [] for e in ENGS}
        self.seq = {e: 0 for e in ENGS}
        self.known = {e: {} for e in ENGS}
        self.n_dsem = 0
        self.out_dma = []
        self.fence = None
        self.fence_d = {}
        self.pending_rd = {}

    def buf(self, name):
        return Buf(name)

    def bufs(self, name, n):
        return [Buf(f"{name}{i}") for i in range(n)]

    def _need(self, eng, dep, waits):
        if dep is None:
            return
        if dep[0] == "dma":
            key = ("d", dep[1]); val = dep[2]
        else:
            if dep[0] == eng and not SAME_ENGINE_SYNC[eng]:
                return
            key = ("e", dep[0]); val = dep[1]
        if self.known[eng].get(key, 0) >= val:
            return
        self.known[eng][key] = val
        waits[key] = max(waits.get(key, 0), val)

    def _deps(self, eng, reads, writes, nofence=False):
        waits = {}
        if not any(b.scratch for b in writes):
            nofence = True
        if self.fence is not None and not nofence:
            for e, v in self.fence.items():
                if v > 0:
                    self._need(eng, (e, v), waits)
            for si, v in self.fence_d.items():
                self._need(eng, ("dma", si, v), waits)
        for b in reads:
            self._need(eng, b.wr, waits)
            if b.excl:
                for r in b.rd:
                    if r[0] != eng:
                        self._need(eng, r, waits)
        for b in writes:
            self._need(eng, b.wr, waits)
            for r in b.rd:
                self._need(eng, r, waits)
        return waits

    def set_fence(self):
        self.fence = dict(self.seq)
        self.fence_d = dict(self.pending_rd)

    def op(self, eng, fn, reads=(), writes=()):
        waits = self._deps(eng, reads, writes)
        self.seq[eng] += 1
        me = (eng, self.seq[eng])
        for b in reads:
            if not b.const:
                b.rd.append(me)
        for b in writes:
            b.wr = me
            b.rd = []
        self.ops[eng].append((waits, fn, ("e", eng)))
        return me

    def dma(self, fn, reads=(), writes=(), queue="sp", out=False, nofence=False):
        waits = self._deps(queue, reads, writes, nofence)
        if writes:
            tgt = writes[0]
        else:
            tgt = reads[0]
        if tgt.dma_sem is None:
            tgt.dma_sem = {}
        if queue not in tgt.dma_sem:
            tgt.dma_sem[queue] = [self.n_dsem, 0]
            self.n_dsem += 1
        ent = tgt.dma_sem[queue]
        ent[1] += 16
        dep = ("dma", ent[0], ent[1])
        if reads:
            self.pending_rd[ent[0]] = ent[1]
        for b in reads:
            if not b.const:
                b.rd.append(dep)
        for b in writes:
            b.wr = dep
            b.rd = []
        self.ops[queue].append((waits, fn, ("d", ent[0])))
        if out:
            self.out_dma.append(dep)
        return dep

    def _outbuf(self, queue):
        key = "_out_" + queue
        if not hasattr(self, key):
            setattr(self, key, Buf(key))
        return getattr(self, key)

    def emit(self):
        nc = self.nc
        with contextlib.ExitStack() as st:
            esem = {e: st.enter_context(nc.semaphore("s_" + e)) for e in ENGS}
            dsem = [st.enter_context(nc.semaphore(f"d{i}")) for i in range(self.n_dsem)]
            block = st.enter_context(nc.Block())

            def sem_of(key):
                return esem[key[1]] if key[0] == "e" else dsem[key[1]]

            def run(eng, h):
                for waits, fn, inc in self.ops[eng]:
                    for key, val in waits.items():
                        h.wait_ge(sem_of(key), val)
                    ins = fn(h)
                    if inc[0] == "e":
                        ins.then_inc(esem[inc[1]], 1)
                    else:
                        ins.then_inc(dsem[inc[1]], 16)
                if eng == "sp":
                    for dep in self.out_dma:
                        h.wait_ge(dsem[dep[1]], dep[2])
                    for e in ENGS:
                        if e != "sp" and self.seq[e] > 0:
                            h.wait_ge(esem[e], self.seq[e])

            @block.tensor
            def _(h):
                run("pe", h)

            @block.scalar
            def _(h):
                run("act", h)

            @block.vector
            def _(h):
                run("dve", h)

            @block.gpsimd
            def _(h):
                run("pool", h)

            @block.sync
            def _(h):
                run("sp", h)


class Arena:
    def __init__(self, nc, st, nbytes):
        self.words = nbytes // 4
        self.t = st.enter_context(nc.sbuf_tensor("arena", [128, self.words], F32))
        self.top = 0
        self.peak = 0

    def alloc(self, shape, dt):
        esz = 2 if dt == BF16 else 4
        n = 1
        for s in shape:
            n *= s
        nbytes = (n * esz + 31) // 32 * 32
        off = self.top
        self.top += nbytes
        self.peak = max(self.peak, self.top)
        assert self.top <= self.words * 4, f"arena overflow {self.top}"
        ap = self.t[:, off // 4:(off + nbytes) // 4]
        if dt != F32:
            ap = ap.bitcast(dt)
        ap = ap[:, 0:n]
        if len(shape) == 1:
            return ap
        if len(shape) == 2:
            ap = ap.rearrange("p (a b) -> p a b", a=shape[0])
        elif len(shape) == 3:
            ap = ap.rearrange("p (a b c) -> p a b c", a=shape[0], b=shape[1])
        elif len(shape) == 4:
            ap = ap.rearrange("p (a b c d) -> p a b c d", a=shape[0], b=shape[1], c=shape[2])
        return ap

    def mark(self):
        return self.top

    def release(self, m):
        self.top = m


RING_SLOTS = 3
RING_ELEMS = 4096


class KB:
    def __init__(self, dbg=()):
        self.nc = bass.Bass("TRN2", target_bir_lowering=False)
        self.dbg = set(dbg)
        self.dram = {}
        self.outs = {}

    def din(self, name, shape, dt=F32):
        self.dram[name] = self.nc.dram_tensor(name, list(shape), dt, kind="ExternalInput").ap()
        return self.dram[name]

    def dout(self, name, shape, dt=F32):
        self.outs[name] = self.nc.dram_tensor(name, list(shape), dt, kind="ExternalOutput").ap()
        return self.outs[name]

    def act(self, out, in_, func, bias=0.0, scale=1.0, R=(), W=()):
        return self.P.op("act", lambda h: h.activation(out=out, in_=in_, func=func, bias=bias, scale=scale), R, W)

    def tt(self, out, in0, in1, op, R=(), W=(), eng="dve"):
        return self.P.op(eng, lambda h: h.tensor_tensor(out=out, in0=in0, in1=in1, op=op), R, W)

    def ts(self, out, in0, s1, s2, op0, op1=None, R=(), W=()):
        if op1 is None:
            return self.P.op("dve", lambda h: h.tensor_single_scalar(out=out, in_=in0, scalar=s1, op=op0), R, W)
        return self.P.op("dve", lambda h: h.tensor_scalar(out=out, in0=in0, scalar1=s1, scalar2=s2, op0=op0, op1=op1), R, W)

    def stt(self, out, in0, scalar, in1, op0, op1, R=(), W=()):
        return self.P.op("dve", lambda h: h.scalar_tensor_tensor(out=out, in0=in0, scalar=scalar, in1=in1, op0=op0, op1=op1), R, W)

    def cp(self, out, in_, R=(), W=(), eng="dve"):
        if eng == "act":
            return self.P.op("act", lambda h: h.copy(out=out, in_=in_), R, W)
        return self.P.op(eng, lambda h: h.tensor_copy(out=out, in_=in_), R, W)

    def memset(self, ap, val, W=(), eng="dve"):
        return self.P.op(eng, lambda h: h.memset(ap, val), (), W)

    def recip(self, out, in_, R=(), W=()):
        return self.P.op("dve", lambda h: h.reciprocal(out=out, in_=in_), R, W)

    def mm(self, mms, R=(), W=()):
        def fn(h):
            ins = None
            for (o, l, r, s0, s1) in mms:
                ins = h.matmul(o, lhsT=l, rhs=r, start=s0, stop=s1)
            return ins
        return self.P.op("pe", fn, R, W)

    def psum(self):
        while True:
            i = self.ps_next
            self.ps_next = (i + 1) % 8
            if i not in self.ps_held:
                return self.PS[i], self.PSB[i]

    def psum_hold(self):
        ps, pb = self.psum()
        i = self.PS.index(ps) if False else [k for k in range(8) if self.PSB[k] is pb][0]
        self.ps_held.add(i)
        return ps, pb, i

    def wload(self, src, n, cast=True, parts=128):
        assert n <= RING_ELEMS
        i = self.ring_next
        self.ring_next = (i + 1) % RING_SLOTS
        dst = self.ring[i][0:parts, 0:n]
        b = self.ringb[i]
        q = "pool" if cast else "sp"
        self.P.dma(lambda h: h.dma_start(out=dst, in_=src), writes=[b], queue=q, nofence=True)
        return self.ring[i], b

    def dump(self, name, ap, bufs, shape):
        if name not in self.dbg:
            return
        o = self.dout("dbg_" + name, shape)
        self.P.dma(lambda h: h.dma_start(out=o, in_=ap), reads=list(bufs), queue="pool", out=True)

    def build(self):
        nc = self.nc
        d = self.din
        d("xT", [128, 8, T]); d("condT", [128, 8]); d("vecs", [DEPTH, 128, NV]); d("flags", [128, NFL])
        d("ada_w", [DEPTH, 12, 128, 8 * 512])
        d("w_in", [DEPTH, 128, 8, 2720])
        d("gate_w", [DEPTH, 8, 128, 8 * 512])
        d("w_o", [DEPTH, 2, 128, 8 * 512])
        d("ffn_w1", [DEPTH, 22, 128, 8 * 256])
        d("ffn_w2", [DEPTH, 8, 128, 11 * 256])
        d("outw", [DEPTH, 8, 128, 1024])
        d("final_g", [128, 8])
        d("w_uq", [DEPTH, 128, 768]); d("w_ukv", [DEPTH, 128, 512])
        d("cacheT", [DEPTH, 128, 256]); d("kropeC", [DEPTH, 32, 256])
        d("oh", [32, NKEY]); d("ropeCS", [2, 32, T])
        self.dout("ckv_out", [DEPTH, T, 128]); self.dout("krope_out", [DEPTH, T, 32])
        d("rcon", [128, 642]); d("s0", [DEPTH, 128, 256])
        d("hy_w1", [DEPTH, 33, 64]); d("hy_w2", [DEPTH, 64, 64]); d("hy_w3", [DEPTH, 64, 1024])
        d("zT", [33, T]); d("decay", [128, 16, 256])
        d("ClT", [16, 128, 4096], BF16); d("FwT", [16, 128, 4096], BF16); d("GiT", [2, 8, 128, 4096], BF16)
        self.dout("st_out", [DEPTH, 128, 8, 256])
        self.dout("yT", [128, 8, T])

        with contextlib.ExitStack() as st:
            self.P = Prog(nc)
            self.A = Arena(nc, st, 207 * 1024)
            self.PS = [st.enter_context(nc.psum_tensor(f"ps{i}", [128, 512], F32))[:, :] for i in range(8)]
            self.PSB = self.P.bufs("ps", 8)
            for b_ in self.PSB:
                b_.excl = True
                b_.scratch = False
            self.ps_next = 0
            self.ps_held = set()
            self.persist()
            for l in range(DEPTH):
                self.layer(l)
            self.final()
            self.P.emit()
        return nc

    def persist(self):
        A, P, nc = self.A, self.P, self.nc
        Dm = self.dram
        self.xT = A.alloc([8, T], F32)
        self.xb = [[P.buf(f"x{k}_{tb}") for tb in range(4)] for k in range(8)]
        self.hT = A.alloc([8, T], BF16)
        self.hb = [[P.buf(f"h{k}_{tb}") for tb in range(4)] for k in range(8)]
        self.ring = [A.alloc([RING_ELEMS], BF16) for _ in range(RING_SLOTS)]
        self.ringb = P.bufs("ring", RING_SLOTS)
        self.ring_next = 0
        self.vecs = A.alloc([NV], F32)
        self.vecsb = P.buf("vecs")
        self.flags = A.alloc([NFL], F32)
        self.flagsb = P.buf("flags")
        self.modT = A.alloc([48], F32)
        self.modb = P.buf("modT")
        self.mods = A.alloc([16], F32)
        self.scb = A.alloc([8], BF16)
        self.scbb = P.buf("scb")
        self.condT = A.alloc([8], F32)
        self.condb = P.buf("cond")
        self.fing = A.alloc([8], F32)
        self.fingb = P.buf("fing")
        self.ident = A.alloc([128], BF16)
        self.ones = A.alloc([128], BF16)
        self.one1 = A.alloc([8], F32)
        self.cb = P.buf("consts")
        identf = A.alloc([128], F32)
        onesf = A.alloc([128], F32)
        self.onesf = onesf
        self.negbig_t = A.alloc([8], F32)[:, 0:1]
        self.memset(self.negbig_t, -float(BIGC * BIGC) * (96 ** -0.5), W=[P.buf("negbig")])
        Dm_ident = self.din("ident", [128, 128])
        for k in range(8):
            src = Dm["xT"][:, k, :]
            dst = self.xT[:, k, :]
            P.dma(lambda h, s=src, t=dst: h.dma_start(out=t, in_=s), writes=self.xb[k], queue="sp")
        P.dma(lambda h: h.dma_start(out=self.condT, in_=Dm["condT"]), writes=[self.condb], queue="sp")
        P.dma(lambda h: h.dma_start(out=self.flags, in_=Dm["flags"]), writes=[self.flagsb], queue="sp")
        P.dma(lambda h: h.dma_start(out=self.fing, in_=Dm["final_g"]), writes=[self.fingb], queue="sp")
        tb_ = P.buf("identf")
        P.dma(lambda h: h.dma_start(out=identf, in_=Dm_ident), writes=[tb_], queue="sp")
        self.cp(self.ident, identf, R=[tb_], W=[self.cb])
        self.memset(onesf, 1.0, W=[tb_])
        self.cp(self.ones, onesf, R=[tb_], W=[self.cb])
        self.memset(self.one1, 1.0, W=[self.cb])
        self.act(self.scb, self.condT, AF.Silu, R=[self.condb], W=[self.scbb])
        self.cb.const = True
        self.flagsb.const = True
        for b_ in ([x for r in self.xb for x in r] + [x for r in self.hb for x in r] + self.ringb +
                   [self.vecsb, self.flagsb, self.modb, self.scbb, self.condb, self.fingb, self.cb]):
            b_.scratch = False

    def modulation(self, l):
        P, Dm = self.P, self.dram
        mk = self.A.mark()
        self.modrow = self.A.alloc([6144], F32)
        self.modrowb = P.buf("modrow")
        P.dma(lambda h: h.dma_start(out=self.vecs, in_=Dm["vecs"][l]), writes=[self.vecsb], queue="sp")
        for cbk in range(12):
            w, wb = self.wload(Dm["ada_w"][l, cbk], 4096)
            w3 = w.rearrange("p (k c) -> p k c", k=8)
            ps, pb = self.psum()
            self.mm([(ps[0:1, :], self.scb[:, k:k + 1], w3[:, k, :], k == 0, k == 7) for k in range(8)],
                    R=[wb, self.scbb], W=[pb])
            self.cp(self.modrow[0:1, cbk * 512:(cbk + 1) * 512], ps[0:1, :], R=[pb], W=[self.modrowb], eng="act")
        ps, pb = self.psum()
        self.mm([(ps[:, c:c + 1], self.modrow[0:1, c * 128:(c + 1) * 128], self.one1[0:1, 0:1], True, True)
                 for c in range(48)], R=[self.modrowb, self.cb], W=[pb])
        self.tt(self.modT, ps[:, 0:48], self.vecs[:, V_ADAB:V_ADAB + 48], ALU.add, R=[pb, self.vecsb], W=[self.modb])
        self.stt(self.mods[:, 0:8], self.modT[:, 8:16], 1.0, self.vecs[:, V_N1G:V_N1G + 8], ALU.add, ALU.mult,
                 R=[self.modb, self.vecsb], W=[self.modb])
        self.stt(self.mods[:, 8:16], self.modT[:, 32:40], 1.0, self.vecs[:, V_N2G:V_N2G + 8], ALU.add, ALU.mult,
                 R=[self.modb, self.vecsb], W=[self.modb])
        self.A.release(mk)
        P.set_fence()

    def norm(self, Acols, Bcols, inplace=False):
        A, P = self.A, self.P
        m = A.mark()
        sq = [A.alloc([512], BF16) for _ in range(4)]
        sqb = P.bufs("sq", 4)
        rs = [A.alloc([512], F32) for _ in range(2)]
        rsb = P.bufs("rs", 2)
        tmp = [A.alloc([512], F32) for _ in range(3)]
        tmpb = P.bufs("ntmp", 3)
        ti = 0
        for tb in range(4):
            sl = slice(tb * 512, (tb + 1) * 512)
            ps, pb = self.psum()
            for k in range(8):
                j = (tb * 8 + k) % 4
                self.act(sq[j], self.xT[:, k, sl], AF.Square, R=[self.xb[k][tb]], W=[sqb[j]])
                self.mm([(ps, self.ones, sq[j], k == 0, k == 7)], R=[sqb[j], self.cb], W=[pb])
            r = rs[tb % 2]; rb = rsb[tb % 2]
            self.act(r, ps, AF.Sqrt, bias=self.eps_t, scale=1.0 / D, R=[pb, self.cb], W=[rb])
            self.recip(r, r, R=[rb], W=[rb])
            for k in range(8):
                if inplace:
                    self.stt(self.xT[:, k, sl], self.xT[:, k, sl], Acols[:, k:k + 1], r, ALU.mult, ALU.mult,
                             R=[self.xb[k][tb], rb, self.fingb], W=[self.xb[k][tb]])
                else:
                    t = tmp[ti % 3]; tbf = tmpb[ti % 3]; ti += 1
                    self.stt(t, self.xT[:, k, sl], Acols[:, k:k + 1], r, ALU.mult, ALU.mult,
                             R=[self.xb[k][tb], rb, self.modb], W=[tbf])
                    self.act(self.hT[:, k, sl], t, AF.Identity, bias=Bcols[:, k:k + 1], scale=1.0,
                             R=[tbf, self.modb], W=[self.hb[k][tb]])
        A.release(m)
        P.set_fence()

    def layer(self, l):
        A, P, Dm = self.A, self.P, self.dram
        if l == 0:
            self.eps_t = A.alloc([8], F32)[:, 0:1]
            self.memset(self.eps_t, EPS, W=[P.buf("eps")])
        self.modulation(l)
        self.norm(self.mods[:, 0:8], self.modT[:, 0:8])
        self.dump(f"h{l}", self.hT, [b for r in self.hb for b in r], [128, 8, T])
        self.dump(f"mod{l}", self.modT, [self.modb], [128, 48])
        mtop = A.mark()
        ys = []
        for name in ("hy", "mla", "ret", "conf"):
            yT = A.alloc([2, T], BF16)
            yb = P.bufs("y" + name, 4)
            m = A.mark()
            getattr(self, "branch_" + name)(l, yT, yb)
            A.release(m)
            P.set_fence()
            ys.append((yT, yb))
            self.dump(f"y{name}{l}", yT, yb, [128, 2, T])
        self.merge(l, ys)
        A.release(mtop)
        P.set_fence()
        self.norm(self.mods[:, 8:16], self.modT[:, 24:32])
        self.ffn(l)
        self.dump(f"x{l}", self.xT, [b for r in self.xb for b in r], [128, 8, T])

    def zero_branch(self, yT, yb):
        for tb in range(4):
            self.memset(yT[:, :, tb * 512:(tb + 1) * 512], 0.0, W=[yb[tb]])

    def branch_hy(self, l, yT, yb):
        A, P, Dm = self.A, self.P, self.dram
        V = self.vecs
        FLG = self.flags
        TWO_PI = 2.0 * math.pi
        hall = lambda tb: [self.hb[k][tb] for k in range(8)]
        vT = A.alloc([2, T], BF16); vTb = P.bufs("hvT", 4)
        Ksp = A.alloc([16, 2, 256], BF16); kspb = P.buf("Ksp")
        h2all = A.alloc([T], BF16); h2allb = P.bufs("h2all", 4)
        w1 = A.alloc([64], F32); w2 = A.alloc([64], F32); wsb = P.buf("hyw")
        P.dma(lambda h: h.dma_start(out=w1[0:33, :], in_=Dm["hy_w1"][l]), writes=[wsb], queue="sp")
        P.dma(lambda h: h.dma_start(out=w2[0:64, :], in_=Dm["hy_w2"][l]), writes=[wsb], queue="sp")

        def hy_proj(which, dst, dstb):
            m = A.mark()
            wv, wvb = self.win_load(l, which * 256, 256)
            upad = A.alloc([NSEG, 258], BF16); upb = P.buf("upad")
            tmp = A.alloc([4, 256], F32); tmpb = P.buf("hytmp")
            mcol = FLG[:, FL_HALO:FL_HALO + 1]
            for cc in range(2):
                ch = which * 2 + cc
                for tb in range(4):
                    sl = slice(tb * 512, (tb + 1) * 512)
                    ps, pb = self.psum()
                    self.mm([(ps, wv[:, k, cc * 128:(cc + 1) * 128], self.hT[:, k, sl], k == 0, k == 7) for k in range(8)],
                            R=[wvb] + hall(tb), W=[pb])
                    self.cp(upad[:, 2 * tb:2 * tb + 2, 1:257], ps.rearrange("p (s t) -> p s t", s=2), R=[pb], W=[upb], eng="act")
                self.memset(upad[:, 0, 0:1], 0.0, W=[upb])
                self.memset(upad[:, 7, 257:258], 0.0, W=[upb])
                self.ts(upad[:, 1:8, 0:1], upad[:, 0:7, 256:257], mcol, None, ALU.mult, R=[upb, self.flagsb], W=[upb])
                self.ts(upad[:, 0:7, 257:258], upad[:, 1:8, 1:2], mcol, None, ALU.mult, R=[upb, self.flagsb], W=[upb])
                for hf in range(2):
                    sg = slice(hf * 4, hf * 4 + 4)
                    self.act(tmp, upad[:, sg, 1:257], AF.Identity, bias=V[:, V_HYCB + ch:V_HYCB + ch + 1],
                             scale=V[:, V_HYCW + 6 + ch:V_HYCW + 7 + ch], R=[upb, self.vecsb], W=[tmpb])
                    self.stt(tmp, upad[:, sg, 0:256], V[:, V_HYCW + ch:V_HYCW + ch + 1], tmp, ALU.mult, ALU.add,
                             R=[upb, tmpb, self.vecsb], W=[tmpb])
                    self.stt(dst[:, cc, hf * 1024:(hf + 1) * 1024].rearrange("p (s t) -> p s t", s=4), upad[:, sg, 2:258],
                             V[:, V_HYCW + 12 + ch:V_HYCW + 13 + ch], tmp, ALU.mult, ALU.add,
                             R=[upb, tmpb, self.vecsb], W=[dstb[2 * hf], dstb[2 * hf + 1]])
            A.release(m)

        def filt(o):
            m = A.mark()
            w3 = A.alloc([512], BF16); w3b = P.buf("w3")
            P.dma(lambda h: h.dma_start(out=w3[0:64, :], in_=Dm["hy_w3"][l][:, o * 512:(o + 1) * 512]), writes=[w3b], queue="pool")
            hp = A.alloc([16, 256], BF16); hm = A.alloc([16, 256], BF16); hpb = P.bufs("hp", 16)
            dec = [A.alloc([256], F32) for _ in range(2)]; decb = P.bufs("dec", 2)
            if o == 0:
                zt = [A.alloc([512], F32) for _ in range(2)]; ztb = P.bufs("zt", 2)
                u = A.alloc([512], F32); kf = A.alloc([512], F32); ki = A.alloc([512], I32)
                h1 = A.alloc([512], F32)
                mb = P.buf("mlp")
            hds = [A.alloc([2, 256], F32) for _ in range(3)]; hdbs = P.bufs("hd", 3)
            abs_ = [A.alloc([512], BF16) for _ in range(2)]; abbs = P.bufs("ab", 2)
            rn = A.alloc([256], F32); rnb = P.buf("rn")
            psN, psNb, iN = self.psum_hold()

            def sin_layer(ps, pb, bcol, out, outb):
                self.ts(u[0:64, :], ps[0:64, :], bcol, 1.0 / TWO_PI, ALU.add, ALU.mult, R=[pb, self.vecsb], W=[mb])
                self.cp(ki[0:64, :], u[0:64, :], R=[mb], W=[mb])
                self.cp(kf[0:64, :], ki[0:64, :], R=[mb], W=[mb])
                self.tt(u[0:64, :], u[0:64, :], kf[0:64, :], ALU.subtract, R=[mb], W=[mb])
                self.act(out[0:64, :], u[0:64, :], AF.Sin, scale=TWO_PI, R=[mb], W=[outb])

            pend_n = None
            for jb in range(4):
                h2, h2b = h2all[:, jb * 512:(jb + 1) * 512], h2allb[jb]
                if o == 0:
                    z, zb = zt[jb % 2], ztb[jb % 2]
                    P.dma(lambda h, z=z, jb=jb: h.dma_start(out=z[0:33, :], in_=Dm["zT"][:, jb * 512:(jb + 1) * 512]), writes=[zb], queue="sp")
                    ps, pb = self.psum()
                    self.mm([(ps[0:64, :], w1[0:33, :], z[0:33, :], True, True)], R=[wsb, zb], W=[pb])
                    sin_layer(ps, pb, V[0:64, V_HB1:V_HB1 + 1], h1, mb)
                    ps, pb = self.psum()
                    self.mm([(ps[0:64, :], w2[0:64, :], h1[0:64, :], True, True)], R=[wsb, mb], W=[pb])
                    sin_layer(ps, pb, V[0:64, V_HB2:V_HB2 + 1], h2, h2b)
                for ii in range(4):
                    i = jb * 4 + ii
                    hd, hdb = hds[i % 3], hdbs[i % 3]
                    ab, abb = abs_[i % 2], abbs[i % 2]
                    dc, dcb = dec[i % 2], decb[i % 2]
                    P.dma(lambda h, dc=dc, i=i: h.dma_start(out=dc, in_=Dm["decay"][:, i, :]), writes=[dcb], queue="sp")
                    p3, p3b = self.psum()
                    self.mm([(p3, h2[0:64, ii * 128:(ii + 1) * 128], w3[0:64, :], True, True)], R=[h2b, w3b], W=[p3b])
                    if pend_n is not None:
                        self.mm([(psN, self.ones, pend_n[0], pend_n[2] == 0, pend_n[2] == 15)], R=[pend_n[1], self.cb], W=[psNb])
                        pend_n = None
                    for dr in range(2):
                        self.tt(hd[:, dr, :], p3[:, dr * 256:(dr + 1) * 256], dc, ALU.mult, R=[p3b, dcb], W=[hdb])
                    self.act(ab, hd.rearrange("p a c -> p (a c)"), AF.Abs, R=[hdb], W=[abb])
                    pend_n = (ab, abb, i)
                    m0 = FLG[:, FL_M0:FL_M0 + 1] if i == 0 else 1.0
                    nm0 = FLG[:, FL_M0 + 1:FL_M0 + 2] if i == 0 else -1.0
                    self.stt(hp[:, i, :], hd[:, 1, :], m0, hd[:, 0, :], ALU.mult, ALU.add, R=[hdb, self.flagsb], W=[hpb[i]])
                    self.stt(hm[:, i, :], hd[:, 1, :], nm0, hd[:, 0, :], ALU.mult, ALU.add, R=[hdb, self.flagsb], W=[hpb[i]])
            self.mm([(psN, self.ones, pend_n[0], pend_n[2] == 0, pend_n[2] == 15)], R=[pend_n[1], self.cb], W=[psNb])
            self.cp(rn, psN[:, 0:256], R=[psNb], W=[rnb], eng="act")
            self.tt(rn, rn, psN[:, 256:512], ALU.add, R=[rnb, psNb], W=[rnb])
            self.recip(rn, rn, R=[rnb], W=[rnb])
            self.ps_held.discard(iN)
            for j in range(16):
                cl, clb = self.wload(Dm["ClT"][j], 4096, cast=False)
                cl4 = cl.rearrange("p (t c f) -> p t c f", t=16, c=2)
                ps, pb = self.psum()
                mms = []
                for cs_ in range(2):
                    src = hp if cs_ == 0 else hm
                    for tc in range(16):
                        mms.append((ps[:, cs_ * 256:(cs_ + 1) * 256], cl4[:, tc, cs_, :], src[:, tc, :], tc == 0, tc == 15))
                self.mm(mms, R=[clb] + hpb, W=[pb])
                for cs_ in range(2):
                    self.tt(Ksp[:, j, cs_, :], ps[:, cs_ * 256:(cs_ + 1) * 256], rn, ALU.mult, R=[pb, rnb], W=[kspb])
            A.release(m)
            P.set_fence()

        def conv(o, xwhich, final):
            m = A.mark()
            vtok = A.alloc([16, 256], BF16); vtkb = P.buf("vtok")
            Y = A.alloc([32, 256], BF16); Yb = P.buf("Y")
            m1 = [A.alloc([512], F32) for _ in range(2)]; m2 = [A.alloc([512], F32) for _ in range(2)]
            m1b = P.bufs("m1", 2); m2b = P.bufs("m2", 2)
            for i in range(16):
                ps, pb = self.psum()
                self.mm([(ps[:, cc * 128:(cc + 1) * 128], vT[:, cc, i * 128:(i + 1) * 128], self.ident, True, True) for cc in range(2)],
                        R=[vTb[i // 4], self.cb], W=[pb])
                self.cp(vtok[:, i, :], ps[:, 0:256], R=[pb], W=[vtkb], eng=("act" if i % 2 else "dve"))
            for j in range(16):
                fw, fwb = self.wload(Dm["FwT"][j], 4096, cast=False)
                fw4 = fw.rearrange("p (t c f) -> p t c f", t=16, c=2)
                ps, pb = self.psum()
                mms = []
                for cs_ in range(2):
                    for tc in range(16):
                        mms.append((ps[:, cs_ * 256:(cs_ + 1) * 256], fw4[:, tc, cs_, :], vtok[:, tc, :], tc == 0, tc == 15))
                self.mm(mms, R=[fwb, vtkb], W=[pb])
                a1, a1b = m1[j % 2], m1b[j % 2]
                a2, a2b = m2[j % 2], m2b[j % 2]
                self.tt(a1, ps, Ksp[:, j, :, :].rearrange("p a c -> p (a c)"), ALU.mult, R=[pb, kspb], W=[a1b])
                self.tt(a2[:, 0:256], ps[:, 0:256], Ksp[:, j, 1, :], ALU.mult, R=[pb, kspb], W=[a2b])
                self.tt(a2[:, 256:512], ps[:, 256:512], Ksp[:, j, 0, :], ALU.mult, R=[pb, kspb], W=[a2b])
                self.tt(Y[:, j, :], a1[:, 0:256], a1[:, 256:512], ALU.subtract, R=[a1b], W=[Yb], eng="pool")
                self.tt(Y[:, 16 + j, :], a2[:, 0:256], a2[:, 256:512], ALU.add, R=[a2b], W=[Yb], eng="pool")
            for th in range(2):
                accs = [self.psum_hold() for _ in range(4)]
                for g in range(8):
                    gi, gib = self.wload(Dm["GiT"][th, g], 4096, cast=False)
                    gi3 = gi.rearrange("p (r t) -> p r t", r=4)
                    mms = []
                    for rr in range(4):
                        r = g * 4 + rr
                        for n in range(4):
                            cc, tbb = divmod(n, 2)
                            mms.append((accs[n][0], Y[:, r, cc * 128:(cc + 1) * 128], gi3[:, rr, tbb * 512:(tbb + 1) * 512], r == 0, r == 31))
                    self.mm(mms, R=[gib, Yb], W=[a[1] for a in accs])
                for n in range(4):
                    cc, tbb = divmod(n, 2)
                    tb = th * 2 + tbb
                    sl = slice(tb * 512, (tb + 1) * 512)
                    self.stt(vT[:, cc, sl], vT[:, cc, sl], V[:, V_HYBIAS + o * 2 + cc:V_HYBIAS + o * 2 + cc + 1], accs[n][0], ALU.mult, ALU.add,
                             R=[vTb[tb], accs[n][1], self.vecsb], W=[vTb[tb]])
                    self.ps_held.discard(accs[n][2])
            A.release(m)
            P.set_fence()
            m = A.mark()
            xT = A.alloc([2, T], BF16); xTb = P.bufs("hxT", 4)
            hy_proj(xwhich, xT, xTb)
            for tb in range(4):
                sl = slice(tb * 512, (tb + 1) * 512)
                dst = yT if final else vT
                dstb = yb if final else vTb
                self.tt(dst[:, :, sl], xT[:, :, sl], vT[:, :, sl], ALU.mult, R=[xTb[tb], vTb[tb]], W=[dstb[tb]])
            A.release(m)
            P.set_fence()

        hy_proj(0, vT, vTb)
        P.set_fence()
        filt(0)
        conv(0, 1, False)
        filt(1)
        conv(1, 2, True)

    def branch_mla(self, l, yT, yb):
        A, P, Dm = self.A, self.P, self.dram
        V = self.vecs
        scale = (64 + 32) ** -0.5
        wq, wqb = self.wload(Dm["w_uq"][l], 768)
        wq3 = wq[:, 0:768].rearrange("p (k c) -> p k c", k=2)
        WqA = A.alloc([4, 2, 128], BF16); WqB = A.alloc([4, 2, 64], BF16)
        wb_ = P.buf("mlaw")
        self.memset(WqA, 0.0, W=[wb_]); self.memset(WqB, 0.0, W=[wb_])
        for h in range(4):
            c0 = h * 96
            self.cp(WqA[:, h, :, 64:128], wq3[:, :, c0:c0 + 64], R=[wqb], W=[wb_])
            self.cp(WqA[:, h, :, 32:64], wq3[:, :, c0 + 64:c0 + 96], R=[wqb], W=[wb_], eng="act")
            self.ts(WqB[:, h, :, 32:48], wq3[:, :, c0 + 80:c0 + 96], -1.0, None, ALU.mult, R=[wqb], W=[wb_])
            self.cp(WqB[:, h, :, 48:64], wq3[:, :, c0 + 64:c0 + 80], R=[wqb], W=[wb_], eng="act")
        wkv, wkvb = self.wload(Dm["w_ukv"][l], 512)
        WkN = A.alloc([4, 128], BF16); WV = A.alloc([4, 64], BF16)
        self.memset(WkN, 0.0, W=[wb_])
        wkv3 = wkv[:, 0:512].rearrange("p (h c) -> p h c", h=4)
        self.cp(WkN[:, :, 64:128], wkv3[:, :, 0:64], R=[wkvb], W=[wb_])
        self.cp(WV, wkv3[:, :, 64:128], R=[wkvb], W=[wb_], eng="act")
        wi, wib = self.win_load(l, 768, 416)
        WkrA = A.alloc([8, 64], BF16); WkrB = A.alloc([8, 64], BF16)
        self.memset(WkrA, 0.0, W=[wb_]); self.memset(WkrB, 0.0, W=[wb_])
        self.cp(WkrA[:, :, 32:64], wi[:, :, 384:416], R=[wib], W=[wb_])
        self.ts(WkrB[:, :, 32:48], wi[:, :, 400:416], -1.0, None, ALU.mult, R=[wib], W=[wb_])
        self.cp(WkrB[:, :, 48:64], wi[:, :, 384:400], R=[wib], W=[wb_], eng="act")
        csl = [A.alloc([2, 512], F32) for _ in range(2)]; cslb = P.bufs("ropecs", 2)
        self._ncs = 0

        def load_cs(tb):
            j = self._ncs % 2; self._ncs += 1
            c, cb_ = csl[j], cslb[j]
            P.dma(lambda h: h.dma_start(out=c[32:64, :, :], in_=Dm["ropeCS"][:, :, tb * 512:(tb + 1) * 512].rearrange("a p t -> p a t")),
                  writes=[cb_], queue="sp")
            return c, cb_
        Ka = [A.alloc([NKEY], BF16)]; Kab = P.bufs("Kaug", 1)
        Qa = [A.alloc([T], BF16)]; Qab = P.bufs("Qaug", 1)
        for i in range(1):
            P.dma(lambda h, i=i: h.dma_start(out=Ka[i][0:32, :], in_=Dm["oh"]), writes=[Kab[i]], queue="pool")
            P.dma(lambda h, i=i: h.dma_start(out=Qa[i][0:32, :], in_=Dm["oh"][:, 0:T]), writes=[Qab[i]], queue="pool")
            P.dma(lambda h, i=i: h.dma_start(out=Ka[i][32:64, T:NKEY], in_=Dm["kropeC"][l]), writes=[Kab[i]], queue="pool")
        ckn = A.alloc([NKEY], BF16); cknb = P.bufs("ckn", 5)
        P.dma(lambda h: h.dma_start(out=ckn[:, T:NKEY], in_=Dm["cacheT"][l]), writes=[cknb[4]], queue="pool")
        cqn = A.alloc([2, T], BF16); cqnb = P.bufs("cqn", 4)
        Va = A.alloc([18, 4, 65], BF16); Vab = P.buf("Vaug")
        self.memset(Va[:, :, :, 64:65], 1.0, W=[Vab])
        sq = [A.alloc([512], BF16) for _ in range(2)]; sqb = P.bufs("msq", 2)
        rs = [A.alloc([512], F32) for _ in range(2)]; rsb = P.bufs("mrs", 2)
        t1 = A.alloc([512], F32); t2 = A.alloc([512], F32); t12b = P.bufs("mt", 2)
        stg = A.alloc([4, 128], F32); stgb = P.buf("stg")
        stk = A.alloc([4, 32], F32); stkb = P.buf("stk")
        krr = A.alloc([512], BF16); krrb = P.buf("krr")
        hall = lambda tb: [self.hb[k][tb] for k in range(8)]
        stop = 99
        sub = ""
        if stop <= 1:
            self.zero_branch(yT, yb); return
        for tb in range(4):
            sl = slice(tb * 512, (tb + 1) * 512)
            pcs = []
            pq, pqb = self.psum()
            for c in range(2):
                pc, pcb = self.psum()
                self.mm([(pc, wi[:, k, c * 128:(c + 1) * 128], self.hT[:, k, sl], k == 0, k == 7) for k in range(8)],
                        R=[wib] + hall(tb), W=[pcb])
                self.act(sq[c], pc, AF.Square, R=[pcb], W=[sqb[c]])
                self.mm([(pq, self.ones, sq[c], c == 0, c == 1)], R=[sqb[c], self.cb], W=[pqb])
                pcs.append((pc, pcb))
            self.act(rs[0], pq, AF.Sqrt, bias=self.eps_t, scale=1.0 / 256, R=[pqb], W=[rsb[0]])
            self.recip(rs[0], rs[0], R=[rsb[0]], W=[rsb[0]])
            for c in range(2):
                self.stt(cqn[:, c, sl], pcs[c][0], V[:, V_QN + c:V_QN + c + 1], rs[0], ALU.mult, ALU.mult,
                         R=[pcs[c][1], rsb[0], self.vecsb], W=[cqnb[tb]])
            pc, pcb = self.psum()
            self.mm([(pc, wi[:, k, 256:384], self.hT[:, k, sl], k == 0, k == 7) for k in range(8)], R=[wib] + hall(tb), W=[pcb])
            self.act(sq[0], pc, AF.Square, R=[pcb], W=[sqb[0]])
            pq, pqb = self.psum()
            self.mm([(pq, self.ones, sq[0], True, True)], R=[sqb[0], self.cb], W=[pqb])
            self.act(rs[1], pq, AF.Sqrt, bias=self.eps_t, scale=1.0 / 128, R=[pqb], W=[rsb[1]])
            self.recip(rs[1], rs[1], R=[rsb[1]], W=[rsb[1]])
            self.stt(ckn[:, sl], pc, V[:, V_KVN:V_KVN + 1], rs[1], ALU.mult, ALU.mult, R=[pcb, rsb[1], self.vecsb], W=[cknb[tb]])
            if "nock" in sub:
                continue
            pt, ptb = self.psum()
            self.mm([(pt[:, i * 128:(i + 1) * 128], ckn[:, tb * 512 + i * 128: tb * 512 + (i + 1) * 128], self.ident, True, True)
                     for i in range(4)], R=[cknb[tb], self.cb], W=[ptb])
            self.cp(stg, pt.rearrange("p (i c) -> p i c", i=4), R=[ptb], W=[stgb], eng="act")
            P.dma(lambda h, tb=tb: h.dma_start(out=self.outs["ckv_out"][l, tb * 512:(tb + 1) * 512, :].rearrange("(i p) c -> p i c", p=128), in_=stg),
                  reads=[stgb], queue="sp", out=True)
            if "nokr" in sub:
                continue
            pa, pab = self.psum()
            self.mm([(pa[0:64, :], WkrA[:, k, :], self.hT[:, k, sl], k == 0, k == 7) for k in range(8)], R=[wb_] + hall(tb), W=[pab])
            pb2, pbb = self.psum()
            self.mm([(pb2[0:64, :], WkrB[:, k, :], self.hT[:, k, sl], k == 0, k == 7) for k in range(8)], R=[wb_] + hall(tb), W=[pbb])
            if "mmonly" in sub:
                continue
            if "noact" not in sub:
                self.cp(krr[32:64, :], pa[32:64, :], R=[pab], W=[krrb], eng="act")
            if "nodve" in sub:
                continue
            if "nocs" in sub:
                self.tt(t1[32:64, :], pa[32:64, :], pa[32:64, :], ALU.mult, R=[pab], W=[t12b[0]])
                self.tt(t2[32:64, :], pb2[32:64, :], pb2[32:64, :], ALU.mult, R=[pbb], W=[t12b[1]])
                continue
            cs, csb = load_cs(tb)
            self.tt(t1[32:64, :], pa[32:64, :], cs[32:64, 0, :], ALU.mult, R=[pab, csb], W=[t12b[0]])
            self.tt(t2[32:64, :], pb2[32:64, :], cs[32:64, 1, :], ALU.mult, R=[pbb, csb], W=[t12b[1]])
            if "noadd" in sub:
                continue
            self.tt(Ka[0][32:64, sl], t1[32:64, :], t2[32:64, :], ALU.add, R=t12b, W=[Kab[0]])
            if "nokt" in sub:
                continue
            pt, ptb = self.psum()
            self.mm([(pt[:, i * 32:(i + 1) * 32], krr[32:64, i * 128:(i + 1) * 128], self.ident[32:64, 32:64], True, True)
                     for i in range(4)], R=[krrb, self.cb], W=[ptb])
            self.cp(stk, pt[:, 0:128].rearrange("p (i c) -> p i c", i=4), R=[ptb], W=[stkb], eng="act")
            P.dma(lambda h, tb=tb: h.dma_start(out=self.outs["krope_out"][l, tb * 512:(tb + 1) * 512, :].rearrange("(i p) c -> p i c", p=128), in_=stk),
                  reads=[stkb], queue="sp", out=True)
        if stop <= 2:
            self.zero_branch(yT, yb); return
        for kt in range(18):
            pv, pvb = self.psum()
            src_b = cknb[kt // 4] if kt < 16 else cknb[4]
            self.mm([(pv[:, 0:256], ckn[:, kt * 128:(kt + 1) * 128], WV.rearrange("p h c -> p (h c)"), True, True)],
                    R=[src_b, wb_], W=[pvb])
            self.cp(Va[:, kt, :, 0:64], pv[:, 0:256].rearrange("p (h c) -> p h c", h=4), R=[pvb], W=[Vab],
                    eng=("act" if kt % 2 else "dve"))
        if stop <= 3:
            self.zero_branch(yT, yb); return
        PT = [A.alloc([512], BF16) for _ in range(4)]; PTb = P.bufs("PT", 4)
        assert len(PT) == 4
        rrow = A.alloc([512], F32); rrowb = P.buf("rrow")
        rbs = A.alloc([512], F32); rbsb = P.buf("rbs")
        npt = 0
        for h in range(4):
            Kh, Khb = Ka[0], Kab[0]
            Qh, Qhb = Qa[0], Qab[0]
            for kb_ in range(5):
                n0 = kb_ * 512
                nn = 512 if kb_ < 4 else 256
                pk, pkb = self.psum()
                self.mm([(pk[:, 0:nn], WkN[:, h, :], ckn[:, n0:n0 + nn], True, True)], R=[wb_, cknb[kb_]], W=[pkb])
                self.cp(Kh[64:128, n0:n0 + nn], pk[64:128, 0:nn], R=[pkb], W=[Khb], eng=("act" if kb_ % 2 else "dve"))
            for tb in range(4):
                sl = slice(tb * 512, (tb + 1) * 512)
                pa, pab = self.psum()
                self.mm([(pa, WqA[:, h, k, :], cqn[:, k, sl], k == 0, k == 1) for k in range(2)], R=[wb_, cqnb[tb]], W=[pab])
                pb2, pbb = self.psum()
                self.mm([(pb2[0:64, :], WqB[:, h, k, :], cqn[:, k, sl], k == 0, k == 1) for k in range(2)], R=[wb_, cqnb[tb]], W=[pbb])
                self.cp(Qh[64:128, sl], pa[64:128, :], R=[pab], W=[Qhb], eng="act")
                cs, csb = load_cs(tb)
                self.tt(t1[32:64, :], pa[32:64, :], cs[32:64, 0, :], ALU.mult, R=[pab, csb], W=[t12b[0]])
                self.tt(t2[32:64, :], pb2[32:64, :], cs[32:64, 1, :], ALU.mult, R=[pbb, csb], W=[t12b[1]])
                self.tt(Qh[32:64, sl], t1[32:64, :], t2[32:64, :], ALU.add, R=t12b, W=[Qhb], eng="pool")
            if stop <= 4:
                if h == 3:
                    self.zero_branch(yT, yb)
                continue
            def tail(tb, po, pob, ih):
                sl = slice(tb * 512, (tb + 1) * 512)
                self.recip(rrow[64:65, :], po[64:65, :], R=[pob], W=[rrowb])
                pr, prb = self.psum()
                self.mm([(pr[0:64, :], self.onesf[64:65, 0:64], rrow[64:65, :], True, True)], R=[rrowb, self.cb], W=[prb])
                self.cp(rbs[0:64, :], pr[0:64, :], R=[prb], W=[rbsb], eng="act")
                p0 = (h % 2) * 64
                self.tt(yT[p0:p0 + 64, h // 2, sl], po[0:64, :], rbs[0:64, :], ALU.mult, R=[pob, rbsb], W=[yb[tb]])
                self.ps_held.discard(ih)

            def score(tb, kt):
                sl = slice(tb * 512, (tb + 1) * 512)
                pS, pSb = self.psum()
                self.mm([(pS, Kh[:, kt * 128:(kt + 1) * 128], Qh[:, sl], True, True)], R=[Khb, Qhb], W=[pSb])
                return pS, pSb

            pend = None
            for tb in range(4):
                po, pob, ih = self.psum_hold()
                q = [score(tb, 0), score(tb, 1), score(tb, 2)]
                if pend is not None:
                    tail(*pend)
                for kt in range(18):
                    pS, pSb = q.pop(0)
                    pt_, ptb_ = PT[npt % 4], PTb[npt % 4]; npt += 1
                    self.act(pt_, pS, AF.Exp, bias=self.negbig_t, scale=scale, R=[pSb], W=[ptb_])
                    if kt + 3 < 18:
                        q.append(score(tb, kt + 3))
                    self.mm([(po[0:65, :], Va[:, kt, h, :], pt_, kt == 0, kt == 17)], R=[Vab, ptb_], W=[pob])
                pend = (tb, po, pob, ih)
            tail(*pend)

    def branch_ret(self, l, yT, yb):
        A, P, Dm = self.A, self.P, self.dram
        V = self.vecs
        FLG = self.flags
        Dtab = A.alloc([4, 128], BF16)
        Qtab = A.alloc([4, 128], F32)
        Vtok = A.alloc([16, 256], BF16); vtb = P.bufs("vtok", 16)
        Sin = A.alloc([16, 256], BF16); sinb = [[P.buf(f"sin{i}_{hf}") for hf in range(2)] for i in range(16)]
        m1 = A.mark()
        KTAB = A.alloc([2, 4], F32)
        GCt = A.alloc([4, 64], F32)
        gc4 = A.alloc([4], F32)
        rc = A.alloc([642], F32); rcb = P.buf("rcon")
        P.dma(lambda h: h.dma_start(out=rc, in_=Dm["rcon"]), writes=[rcb], queue="sp")
        s0t = A.alloc([256], F32); s0b = P.buf("s0")
        P.dma(lambda h: h.dma_start(out=s0t, in_=Dm["s0"][l]), writes=[s0b], queue="sp")
        lg = A.alloc([12], F32); lgb = P.buf("lg")
        self.act(lg[:, 0:8], V[:, V_RDALL:V_RDALL + 8], AF.Exp, scale=-1.0, R=[self.vecsb], W=[lgb])
        self.act(lg[:, 8:12], V[:, V_RDFB:V_RDFB + 4], AF.Exp, scale=-1.0, R=[self.vecsb], W=[lgb])
        self.act(lg, lg, AF.Ln, bias=self.one1[:, 0:1], scale=1.0, R=[lgb, self.cb], W=[lgb])
        self.ts(lg, lg, -1.0, None, ALU.mult, R=[lgb], W=[lgb])
        tb_ = P.buf("rtab")
        e1 = A.alloc([128], F32); e2 = A.alloc([128], F32); eb = P.bufs("re", 2)
        for h in range(4):
            self.act(e1, rc[:, 0:128], AF.Exp, scale=lg[:, h:h + 1], R=[rcb, lgb], W=[eb[0]])
            self.stt(e1, e1, 0.125, rc[:, 128:256], ALU.mult, ALU.mult, R=[eb[0], rcb], W=[eb[0]])
            self.act(e2, rc[:, 256:384], AF.Exp, scale=lg[:, 4 + h:5 + h], R=[rcb, lgb], W=[eb[1]])
            self.stt(e2, e2, 0.125, rc[:, 384:512], ALU.mult, ALU.mult, R=[eb[1], rcb], W=[eb[1]])
            self.tt(Dtab[:, h, :], e1, e2, ALU.add, R=eb, W=[tb_])
            self.act(Qtab[:, h, :], rc[:, 512:640], AF.Exp, scale=lg[:, 8 + h:9 + h], R=[rcb, lgb], W=[tb_])
        for dr in range(2):
            self.act(KTAB[:, dr, :], lg[:, dr * 4:(dr + 1) * 4], AF.Exp, scale=rc[:, 640 + dr:641 + dr], R=[rcb, lgb], W=[tb_])
        self.ts(KTAB, KTAB, 0.125, None, ALU.mult, R=[tb_], W=[tb_])
        self.act(gc4, lg[:, 8:12], AF.Exp, scale=128.0, R=[lgb], W=[tb_])
        self.memset(GCt, 1.0, W=[tb_])
        for h in range(4):
            self.ts(GCt[:, h, :], GCt[:, h, :], gc4[:, h:h + 1], None, ALU.mult, R=[tb_], W=[tb_])
        wqk, wqkb = self.win_load(l, 1184, 512)
        wvg, wvgb = self.win_load(l, 1696, 512)
        Wk = wqk[:, :, 256:512]; wkb_ = wqkb
        Wvg = wvg; wvgb2 = wvgb
        hall = lambda tb: [self.hb[k][tb] for k in range(8)]
        KVst = A.alloc([16, 256], F32); kvb = P.bufs("kvst", 16)
        Kd2 = [A.alloc([4, 2, 64], BF16) for _ in range(2)]; kd2b = P.bufs("kd2", 2)
        def kv_mm(i, kd, kdb):
            pkv, pkvb = self.psum()
            self.mm([(pkv[:, h * 64:(h + 1) * 64], kd[:, h, :, :].rearrange("p a d -> p (a d)"), Vtok[:, i, h * 64:(h + 1) * 64], True, True)
                     for h in range(4)], R=[kdb, vtb[i]], W=[pkvb])
            self.cp(KVst[:, i, :], pkv[:, 0:256], R=[pkvb], W=[kvb[i]])

        pend_kv = None
        for i in range(16):
            tb = i // 4
            tsl = slice(i * 128, (i + 1) * 128)
            pk, pkb = self.psum()
            self.mm([(pk[:, 0:256], self.hT[:, k, tsl], Wk[:, k, :], k == 0, k == 7) for k in range(8)], R=[wkb_] + hall(tb), W=[pkb])
            pv, pvb = self.psum()
            self.mm([(pv[:, 0:256], self.hT[:, k, tsl], Wvg[:, k, 0:256], k == 0, k == 7) for k in range(8)], R=[wvgb2] + hall(tb), W=[pvb])
            self.cp(Vtok[:, i, :], pv[:, 0:256], R=[pvb], W=[vtb[i]], eng="act")
            kd = Kd2[i % 2]; kdb = kd2b[i % 2]
            for h in range(4):
                self.ts(kd[:, h, 0, :], pk[:, h * 64:(h + 1) * 64], KTAB[:, 0, h:h + 1], None, ALU.mult, R=[pkb, tb_], W=[kdb])
            for h in range(4):
                self.act(kd[:, h, 1, :], pk[:, h * 64:(h + 1) * 64], AF.Identity, scale=KTAB[:, 1, h:h + 1], R=[pkb, tb_], W=[kdb])
            if pend_kv is not None:
                kv_mm(*pend_kv)
            pend_kv = (i, kd, kdb)
        kv_mm(*pend_kv)
        Ust = A.alloc([8, 256], F32); ustbs = P.bufs("ust", 2)
        Utmp = A.alloc([256], F32)
        Scur = A.alloc([256], F32); scb_ = P.bufs("scur", 2)
        tmp = A.alloc([256], F32)
        GC2 = GCt.rearrange("p h e -> p (h e)")
        orders = [list(range(16)), list(range(15, -1, -1))]
        for hf in range(2):
            r0, r1 = hf * 64, hf * 64 + 64
            self.cp(Scur[r0:r1, :], s0t[r0:r1, :], R=[s0b], W=[scb_[hf]])
            self.cp(Sin[r0:r1, orders[hf][0], :], s0t[r0:r1, :], R=[s0b], W=[sinb[orders[hf][0]][hf]], eng="act")
        for n in range(16):
            for hf in range(2):
                r0, r1 = hf * 64, hf * 64 + 64
                order = orders[hf]
                sb = scb_[hf]
                i = order[n]
                self.tt(tmp[r0:r1, :], Scur[r0:r1, :], GC2[r0:r1, :], ALU.mult, R=[sb, tb_], W=[sb])
                is_out = (i % 2 == 1) if hf == 0 else (i % 2 == 0)
                dst = Ust[r0:r1, i // 2, :] if is_out else Utmp[r0:r1, :]
                self.tt(dst, tmp[r0:r1, :], KVst[r0:r1, i, :], ALU.add, R=[sb, kvb[i]], W=[sb, ustbs[hf]])
                if n < 15:
                    nxt = order[n + 1]
                    kcol = (FL_KF + nxt) if hf == 0 else (FL_KB + nxt)
                    self.ts(Scur[r0:r1, :], dst, FLG[:, kcol:kcol + 1][r0:r1, :], None, ALU.mult, R=[sb, ustbs[hf]], W=[sb])
                    self.cp(Sin[r0:r1, nxt, :], Scur[r0:r1, :], R=[sb], W=[sinb[nxt][hf]], eng="act")
        P.dma(lambda h: h.dma_start(out=self.outs["st_out"][l], in_=Ust), reads=ustbs, queue="sp", out=True)
        A.release(m1)
        P.set_fence()
        wqk, wqkb = self.win_load(l, 1184, 512)
        wvg, wvgb = self.win_load(l, 1696, 512)
        Wk = wqk[:, :, 256:512]; wkb_ = wqkb
        Wvg = wvg; wvgb2 = wvgb
        Wq2 = A.alloc([8, 4, 128], BF16); wq2b = P.buf("wq2")
        qv = wqk[:, :, 0:256].rearrange("p k (h c) -> p k h c", h=4)
        self.cp(Wq2[:, :, :, 0:64], qv, R=[wqkb], W=[wq2b])
        self.cp(Wq2[:, :, :, 64:128], qv, R=[wqkb], W=[wq2b], eng="act")
        QT = A.alloc([4, 512], BF16); qtb = P.buf("QT")
        Qd2 = A.alloc([4, 512], BF16); qd2b = P.buf("Qd2")
        KT = A.alloc([4, 512], BF16); ktb = P.buf("KT")
        sg4 = A.alloc([4, 512], BF16); sg4b = P.buf("sg4")
        NBF = 2
        AT = [A.alloc([512], BF16) for _ in range(2)]; atb = P.bufs("AT", 2)
        o32s = [A.alloc([512], F32) for _ in range(NBF)]; o32bs = P.bufs("o32", NBF)
        obfs = [A.alloc([512], BF16) for _ in range(NBF)]; obfbs = P.bufs("obf", NBF)
        xcs = [A.alloc([512], F32)] * NBF; xcbs = [P.buf("rxc")] * NBF
        sqs = [A.alloc([512], BF16)] * NBF; sqbs = [P.buf("rsq")] * NBF
        rss = [A.alloc([512], F32)] * NBF; rsbs = [P.buf("rrs")] * NBF

        def stageA(tb, ci):
            i = tb * 4 + ci
            csl = slice(ci * 128, (ci + 1) * 128)
            pa, pab = self.psum()
            self.mm([(pa[:, h * 128:(h + 1) * 128], KT[0:64, h, csl], QT[0:64, h, csl], True, True) for h in range(4)],
                    R=[ktb, qtb], W=[pab])
            at = AT[i % 2]; atbb = atb[i % 2]
            self.tt(at, pa, Dtab.rearrange("p h n -> p (h n)"), ALU.mult, R=[pab, tb_], W=[atbb])
            po, pob = self.psum()
            mms = []
            for h in range(4):
                mms.append((po[0:64, h * 128:(h + 1) * 128], Vtok[:, i, h * 64:(h + 1) * 64], at[:, h * 128:(h + 1) * 128], True, False))
                mms.append((po[0:64, h * 128:(h + 1) * 128], Sin[:, i, h * 64:(h + 1) * 64], Qd2[:, h, csl], False, True))
            self.mm(mms, R=[vtb[i], atbb, sinb[i][0], sinb[i][1], qd2b], W=[pob])
            j = i % NBF
            self.cp(o32s[j][0:64, :], po[0:64, :], R=[pob], W=[o32bs[j]], eng="act")
            self.cp(obfs[j][0:64, :], po[0:64, :], R=[pob], W=[obfbs[j]], eng="act")

        def stageB(tb, ci):
            i = tb * 4 + ci
            j = i % NBF
            pm, pmb = self.psum()
            self.mm([(pm[0:64, :], self.ones[0:64, 0:64], obfs[j][0:64, :], True, True)], R=[obfbs[j], self.cb], W=[pmb])
            self.stt(xcs[j][0:64, :], pm[0:64, :], -1.0 / 64, o32s[j][0:64, :], ALU.mult, ALU.add, R=[pmb, o32bs[j]], W=[xcbs[j]])
            self.act(sqs[j][0:64, :], xcs[j][0:64, :], AF.Square, R=[xcbs[j]], W=[sqbs[j]])

        def stageC(tb, ci):
            i = tb * 4 + ci
            j = i % NBF
            csl = slice(ci * 128, (ci + 1) * 128)
            pv, pvb = self.psum()
            self.mm([(pv[0:64, :], self.ones[0:64, 0:64], sqs[j][0:64, :], True, True)], R=[sqbs[j], self.cb], W=[pvb])
            self.act(rss[j][0:64, :], pv[0:64, :], AF.Sqrt, bias=self.eps_t[0:64, :], scale=1.0 / 64, R=[pvb], W=[rsbs[j]])
            self.recip(rss[j][0:64, :], rss[j][0:64, :], R=[rsbs[j]], W=[rsbs[j]])
            self.tt(xcs[j][0:64, :], xcs[j][0:64, :], rss[j][0:64, :], ALU.mult, R=[xcbs[j], rsbs[j]], W=[xcbs[j]], eng="pool")
            for h in range(4):
                p0 = (h % 2) * 64
                self.tt(yT[p0:p0 + 64, h // 2, i * 128:(i + 1) * 128], xcs[j][0:64, h * 128:(h + 1) * 128], sg4[0:64, h, csl], ALU.mult,
                        R=[xcbs[j], sg4b], W=[yb[tb]])

        for tb in range(4):
            sl = slice(tb * 512, (tb + 1) * 512)
            for h in range(4):
                pq, pqb = self.psum()
                self.mm([(pq, Wq2[:, k, h, :], self.hT[:, k, sl], k == 0, k == 7) for k in range(8)], R=[wq2b] + hall(tb), W=[pqb])
                self.cp(QT[0:64, h, :], pq[0:64, :], R=[pqb], W=[qtb], eng="act")
                for ci in range(4):
                    self.tt(Qd2[:, h, ci * 128:(ci + 1) * 128], pq[:, ci * 128:(ci + 1) * 128], Qtab[:, h, :], ALU.mult,
                            R=[pqb, tb_], W=[qd2b])
                pk, pkb = self.psum()
                self.mm([(pk[0:64, :], Wk[:, k, h * 64:(h + 1) * 64], self.hT[:, k, sl], k == 0, k == 7) for k in range(8)], R=[wkb_] + hall(tb), W=[pkb])
                self.cp(KT[0:64, h, :], pk[0:64, :], R=[pkb], W=[ktb], eng="act")
                pg, pgb = self.psum()
                self.mm([(pg[0:64, :], Wvg[:, k, 256 + h * 64:256 + (h + 1) * 64], self.hT[:, k, sl], k == 0, k == 7) for k in range(8)], R=[wvgb2] + hall(tb), W=[pgb])
                self.act(sg4[0:64, h, :], pg[0:64, :], AF.Silu, R=[pgb], W=[sg4b])
            for s_ in range(6):
                if s_ < 4:
                    stageA(tb, s_)
                if 0 <= s_ - 2 < 4:
                    stageC(tb, s_ - 2)
                if 0 <= s_ - 1 < 4:
                    stageB(tb, s_ - 1)

    def win_load(self, l, c0, ncol):
        assert ncol * 8 <= RING_ELEMS
        i = self.ring_next
        self.ring_next = (i + 1) % RING_SLOTS
        dst = self.ring[i][:, 0:8 * ncol].rearrange("p (k c) -> p k c", k=8)
        src = self.dram["w_in"][l][:, :, c0:c0 + ncol]
        b = self.ringb[i]
        self.P.dma(lambda h: h.dma_start(out=dst, in_=src), writes=[b], queue="pool", nofence=True)
        return dst, b

    def branch_conf(self, l, yT, yb):
        A, P = self.A, self.P
        V = self.vecs
        zp = A.alloc([2, NSEG, 286], BF16)
        zpb = P.bufs("zp", 4)
        diag = A.alloc([62, 128], BF16)
        diagb = P.buf("diag")
        for k in range(31):
            for ch in range(2):
                self.ts(diag[:, k * 2 + ch, :], self.ident, V[:, V_CDW + k * 2 + ch:V_CDW + k * 2 + ch + 1], None,
                        ALU.mult, R=[self.cb, self.vecsb], W=[diagb])
        sig = [A.alloc([512], F32) for _ in range(2)]
        sigb = P.bufs("sig", 2)
        wa, wab = self.win_load(l, 2208, 512)
        n = 0
        for ch in range(2):
            for tb in range(4):
                sl = slice(tb * 512, (tb + 1) * 512)
                pa, pab = self.psum()
                self.mm([(pa, wa[:, k, ch * 128:(ch + 1) * 128], self.hT[:, k, sl], k == 0, k == 7) for k in range(8)],
                        R=[wab] + [self.hb[k][tb] for k in range(8)], W=[pab])
                pbb, pbbb = self.psum()
                self.mm([(pbb, wa[:, k, 256 + ch * 128:256 + (ch + 1) * 128], self.hT[:, k, sl], k == 0, k == 7) for k in range(8)],
                        R=[wab] + [self.hb[k][tb] for k in range(8)], W=[pbbb])
                s = sig[n % 2]; sb = sigb[n % 2]; n += 1
                self.act(s, pbb, AF.Sigmoid, R=[pbbb], W=[sb])
                self.tt(zp[:, ch, 2 * tb:2 * tb + 2, 15:271], pa.rearrange("p (s t) -> p s t", s=2),
                        s.rearrange("p (s t) -> p s t", s=2), ALU.mult, R=[pab, sb], W=[zpb[tb]])
        mcol = self.flags[:, FL_HALO:FL_HALO + 1]
        for ch in range(2):
            self.memset(zp[:, ch, 0, 0:15], 0.0, W=zpb)
            self.memset(zp[:, ch, 7, 271:286], 0.0, W=zpb)
            self.ts(zp[:, ch, 1:8, 0:15], zp[:, ch, 0:7, 256:271], mcol, None, ALU.mult, R=zpb + [self.flagsb], W=zpb)
            self.ts(zp[:, ch, 0:7, 271:286], zp[:, ch, 1:8, 15:30], mcol, None, ALU.mult, R=zpb + [self.flagsb], W=zpb)
        cv = A.alloc([2, 512], F32)
        cvb = P.bufs("cv", 2)
        xc = A.alloc([2, 512], F32)
        xcb = P.buf("xc")
        sq = A.alloc([2, 512], BF16)
        sqb = P.buf("csq")
        rs = A.alloc([512], F32)
        rsb = P.buf("crs")
        for tb in range(4):
            sl = slice(tb * 512, (tb + 1) * 512)
            for ch in range(2):
                ps, pb = self.psum()
                self.mm([(ps.rearrange("p (s t) -> p s t", s=2), diag[:, k * 2 + ch, :], zp[:, ch, 2 * tb:2 * tb + 2, k:k + 256], k == 0, k == 30)
                         for k in range(31)], R=[diagb] + zpb, W=[pb])
                self.act(cv[:, ch, :], ps, AF.Identity, bias=V[:, V_CDB + ch:V_CDB + ch + 1], R=[pb, self.vecsb], W=[cvb[ch]])
            pm, pmb = self.psum()
            cvh = sq
            for ch in range(2):
                self.cp(cvh[:, ch, :], cv[:, ch, :], R=[cvb[ch]], W=[sqb])
            self.mm([(pm, self.ones, cvh[:, ch, :], ch == 0, ch == 1) for ch in range(2)], R=[sqb, self.cb], W=[pmb])
            for ch in range(2):
                self.stt(xc[:, ch, :], pm, -1.0 / 256, cv[:, ch, :], ALU.mult, ALU.add, R=[pmb, cvb[ch]], W=[xcb])
            for ch in range(2):
                self.act(sq[:, ch, :], xc[:, ch, :], AF.Square, R=[xcb], W=[sqb])
            pv, pvb = self.psum()
            self.mm([(pv, self.ones, sq[:, ch, :], ch == 0, ch == 1) for ch in range(2)], R=[sqb, self.cb], W=[pvb])
            self.act(rs, pv, AF.Sqrt, bias=self.eps_t, scale=1.0 / 256, R=[pvb], W=[rsb])
            self.recip(rs, rs, R=[rsb], W=[rsb])
            for ch in range(2):
                self.stt(xc[:, ch, :], xc[:, ch, :], V[:, V_CLG + ch:V_CLG + ch + 1], rs, ALU.mult, ALU.mult,
                         R=[xcb, rsb, self.vecsb], W=[xcb])
                self.act(yT[:, ch, sl], xc[:, ch, :], AF.Silu, bias=V[:, V_CLB + ch:V_CLB + ch + 1], R=[xcb, self.vecsb], W=[yb[tb]])

    def merge(self, l, ys):
        A, P, Dm = self.A, self.P, self.dram
        V = self.vecs
        mg = A.alloc([8, T], BF16)
        mgb = [[P.buf(f"mg{j}_{tb}") for tb in range(4)] for j in range(8)]
        acc = [A.alloc([512], F32) for _ in range(2)]
        accb = P.bufs("acc", 2)
        sg = [A.alloc([512], F32) for _ in range(3)]
        sgb = P.bufs("sg", 3)
        n = 0
        na = 0
        for j in range(8):
            gw, gwb = self.wload(Dm["gate_w"][l, j], 4096)
            gw3 = gw.rearrange("p (k c) -> p k c", k=8)
            owj, owb = self.wload(Dm["outw"][l, j], 1024)
            ow4 = owj[:, 0:1024].rearrange("p (i k c) -> p i k c", i=4, k=2)
            for tb in range(4):
                sl = slice(tb * 512, (tb + 1) * 512)
                a = acc[na % 2]; ab = accb[na % 2]; na += 1
                for i in range(4):
                    yT, yb = ys[i]
                    pg, pgb = self.psum()
                    self.mm([(pg, gw3[:, k, i * 128:(i + 1) * 128], self.hT[:, k, sl], k == 0, k == 7) for k in range(8)],
                            R=[gwb] + [self.hb[k][tb] for k in range(8)], W=[pgb])
                    po, pob = self.psum()
                    self.mm([(po, ow4[:, i, k, :], yT[:, k, sl], k == 0, k == 1) for k in range(2)],
                            R=[owb, yb[tb]], W=[pob])
                    s = sg[n % 3]; sb = sgb[n % 3]; n += 1
                    self.act(s, pg, AF.Sigmoid, bias=V[:, V_GATEB + i * 8 + j:V_GATEB + i * 8 + j + 1], R=[pgb, self.vecsb], W=[sb])
                    if i == 0:
                        self.tt(a, s, po, ALU.mult, R=[sb, pob], W=[ab])
                    elif i < 3:
                        self.tt(s, s, po, ALU.mult, R=[sb, pob], W=[sb])
                        self.tt(a, a, s, ALU.add, R=[sb, ab], W=[ab], eng="pool")
                    else:
                        self.tt(s, s, po, ALU.mult, R=[sb, pob], W=[sb])
                        self.tt(mg[:, j, sl], a, s, ALU.add, R=[sb, ab], W=[mgb[j][tb]], eng="pool")
        self.dump(f"mg{l}", mg, [b for r in mgb for b in r], [128, 8, T])
        for jg in range(2):
            w, wb = self.wload(Dm["w_o"][l, jg], 4096)
            w3 = w.rearrange("p (k c) -> p k c", k=8)
            for jj in range(4):
                j = jg * 4 + jj
                for tb in range(4):
                    sl = slice(tb * 512, (tb + 1) * 512)
                    ps, pb = self.psum()
                    self.mm([(ps, w3[:, k, jj * 128:(jj + 1) * 128], mg[:, k, sl], k == 0, k == 7) for k in range(8)],
                            R=[wb] + [mgb[k][tb] for k in range(8)], W=[pb])
                    self.stt(self.xT[:, j, sl], ps, self.modT[:, 16 + j:17 + j], self.xT[:, j, sl], ALU.mult, ALU.add,
                             R=[pb, self.modb, self.xb[j][tb]], W=[self.xb[j][tb]])

    def ffn(self, l):
        A, P, Dm = self.A, self.P, self.dram
        m = A.mark()
        hid = A.alloc([11, T], BF16)
        hidb = [[P.buf(f"hid{f}_{tb}") for tb in range(4)] for f in range(11)]
        sa = [A.alloc([512], F32) for _ in range(3)]
        sab = P.bufs("sa", 3)
        n = 0
        for half in range(2):
            for ff in range(11):
                f = half * 11 + ff
                w, wb = self.wload(Dm["ffn_w1"][l, f], 2048)
                w3 = w[:, 0:2048].rearrange("p (k c) -> p k c", k=8)
                for tb in range(4):
                    sl = slice(tb * 512, (tb + 1) * 512)
                    pa, pab = self.psum()
                    self.mm([(pa, w3[:, k, 0:128], self.hT[:, k, sl], k == 0, k == 7) for k in range(8)],
                            R=[wb] + [self.hb[k][tb] for k in range(8)], W=[pab])
                    pb_, pbb = self.psum()
                    self.mm([(pb_, w3[:, k, 128:256], self.hT[:, k, sl], k == 0, k == 7) for k in range(8)],
                            R=[wb] + [self.hb[k][tb] for k in range(8)], W=[pbb])
                    s = sa[n % 3]; sb = sab[n % 3]; n += 1
                    self.act(s, pa, AF.Silu, R=[pab], W=[sb])
                    self.tt(hid[:, ff, sl], s, pb_, ALU.mult, R=[sb, pbb], W=[hidb[ff][tb]])
            for jp in range(4):
                w, wb = self.wload(Dm["ffn_w2"][l, half * 4 + jp], 11 * 256)
                w3 = w[:, 0:11 * 256].rearrange("p (f c) -> p f c", f=11)
                for jj in range(2):
                    j = jp * 2 + jj
                    for tb in range(4):
                        sl = slice(tb * 512, (tb + 1) * 512)
                        ps, pb = self.psum()
                        self.mm([(ps, w3[:, ff, jj * 128:(jj + 1) * 128], hid[:, ff, sl], ff == 0, ff == 10) for ff in range(11)],
                                R=[wb] + [hidb[ff][tb] for ff in range(11)], W=[pb])
                        self.stt(self.xT[:, j, sl], ps, self.modT[:, 40 + j:41 + j], self.xT[:, j, sl], ALU.mult, ALU.add,
                                 R=[pb, self.modb, self.xb[j][tb]], W=[self.xb[j][tb]])
        A.release(m)
        P.set_fence()

    def final(self):
        P = self.P
        self.norm(self.fing, None, inplace=True)
        for k in range(8):
            src = self.xT[:, k, :]
            dst = self.outs["yT"][:, k, :]
            P.dma(lambda h, s=src, t=dst: h.dma_start(out=t, in_=s), reads=self.xb[k], queue="sp", out=True)


def _chunkT(v, n):
    return np.ascontiguousarray(np.asarray(v, np.float32).reshape(n, 128).T)


_HYC = {}


def _hy_consts(L, pos):
    key = L
    if key in _HYC:
        return _HYC[key]
    f32 = np.float32
    N = 4096
    zT = np.zeros((33, T), f32)
    j = np.arange(L)
    t = np.linspace(0.0, 1.0, L, dtype=f32)
    w = (2.0 * math.pi * j.astype(f32) / L).astype(f32)
    fb = np.linspace(1e-4, 15, 16, dtype=f32)
    zT[0, :L] = t
    zT[1:17, :L] = np.cos(fb[:, None] * w[None, :])
    zT[17:33, :L] = -np.sin(fb[:, None] * w[None, :])
    max_decay = math.log(1e-2) / 0.3
    min_decay = math.log(1e-2) / 1.5
    deltas = np.abs(np.linspace(min_decay, max_decay, 256, dtype=f32))
    dec = np.zeros((T, 256), f32)
    dec[:L] = np.exp(-t[:, None] * deltas[None, :])
    decay = np.ascontiguousarray(dec.reshape(16, 128, 256).transpose(1, 0, 2))
    om = 2.0 * math.pi * (np.arange(2048, dtype=np.float64) + 0.5) / N

    def fwd_units(positions):
        ang = positions.astype(np.float64)[:, None] * om[None, :]
        c = np.cos(ang).astype(f32); s_ = np.sin(ang).astype(f32)
        cs = np.stack([c, s_], 0)
        u = cs.reshape(2, 16, 128, 16, 128).transpose(3, 2, 1, 0, 4)
        return np.ascontiguousarray(u.reshape(16, 128, 4096)).astype(NB)

    FwT = fwd_units(np.asarray(pos))
    lagpos = np.arange(T)
    ClT = FwT if L == 2048 else fwd_units(lagpos)
    ang = om[:, None] * np.asarray(pos).astype(np.float64)[None, :]
    g = np.concatenate([np.cos(ang), np.sin(ang)], 0) * (2.0 / N)
    g = g.astype(f32).reshape(8, 4, 128, 2, 1024).transpose(3, 0, 2, 1, 4)
    GiT = np.ascontiguousarray(g.reshape(2, 8, 128, 4096)).astype(NB)
    _HYC[key] = dict(zT=zT, decay=decay, ClT=ClT, FwT=FwT, GiT=GiT)
    return _HYC[key]


def host_prep(inp):
    f32 = np.float32
    shared = {}
    vecs = np.zeros((DEPTH, 128, NV), f32)
    for l in range(DEPTH):
        v = vecs[l]
        v[:, V_N1G:V_N1G + 8] = _chunkT(inp["norm1_g"][l], 8)
        v[:, V_N2G:V_N2G + 8] = _chunkT(inp["norm2_g"][l], 8)
        v[:, V_ADAB:V_ADAB + 48] = _chunkT(inp["ada_b"][l], 48)
        gb = np.asarray(inp["gate_b"][l], f32).reshape(4, 8, 128)
        v[:, V_GATEB:V_GATEB + 32] = gb.transpose(2, 0, 1).reshape(128, 32)
        cw = np.asarray(inp["hy_conv_w"][l], f32).reshape(3, 6, 128)
        v[:, V_HYCW:V_HYCW + 18] = cw.transpose(2, 0, 1).reshape(128, 18)
        v[:, V_HYCB:V_HYCB + 6] = _chunkT(inp["hy_conv_b"][l], 6)
        hb = np.asarray(inp["hy_bias"][l], f32).reshape(2, 2, 128)
        v[:, V_HYBIAS:V_HYBIAS + 4] = hb.transpose(2, 0, 1).reshape(128, 4)
        v[:, V_QN:V_QN + 2] = _chunkT(inp["mla_q_norm"][l], 2)
        v[:, V_KVN:V_KVN + 1] = _chunkT(inp["mla_kv_norm"][l], 1)
        dw = np.asarray(inp["conf_dw_w"][l], f32).reshape(31, 2, 128)
        v[:, V_CDW:V_CDW + 62] = dw.transpose(2, 0, 1).reshape(128, 62)
        v[:, V_CDB:V_CDB + 2] = _chunkT(inp["conf_dw_b"][l], 2)
        v[:, V_CLG:V_CLG + 2] = _chunkT(inp["conf_ln_g"][l], 2)
        v[:, V_CLB:V_CLB + 2] = _chunkT(inp["conf_ln_b"][l], 2)
        rd = np.asarray(inp["ret_decay"][l], f32)
        v[:, V_RDALL:V_RDALL + 8] = rd.reshape(1, 8)
        v[0:64, V_RDFB:V_RDFB + 4] = rd[0][None]
        v[64:128, V_RDFB:V_RDFB + 4] = rd[1][None]
        v[0:64, V_HB1] = np.asarray(inp["hy_b1"][l], f32)
        v[0:64, V_HB2] = np.asarray(inp["hy_b2"][l], f32)
    shared["vecs"] = vecs
    shared["final_g"] = _chunkT(inp["final_norm_g"], 8)
    shared["ident"] = np.eye(128, dtype=f32)

    def kmajor(w, ncol_unit):
        Dp, KK, C = w.shape
        K = KK // 128
        nu = C // ncol_unit
        a = np.asarray(w, f32).reshape(Dp, K, 128, nu, ncol_unit).transpose(0, 3, 2, 1, 4)
        return np.ascontiguousarray(a.reshape(Dp, nu, 128, K * ncol_unit))

    shared["ada_w"] = kmajor(inp["ada_w"], 512)
    shared["w_in"] = np.ascontiguousarray(np.asarray(inp["w_in"], f32).reshape(DEPTH, 8, 128, 2720).transpose(0, 2, 1, 3))
    gw = np.asarray(inp["gate_w"], f32).reshape(DEPTH, 8, 128, 4, 8, 128)
    shared["gate_w"] = np.ascontiguousarray(gw.transpose(0, 4, 2, 1, 3, 5).reshape(DEPTH, 8, 128, 8 * 512))
    shared["w_o"] = kmajor(inp["w_o"], 512)
    w1 = np.asarray(inp["ffn_w1"], f32).reshape(DEPTH, 8, 128, 2, 22, 128)
    shared["ffn_w1"] = np.ascontiguousarray(w1.transpose(0, 4, 2, 1, 3, 5).reshape(DEPTH, 22, 128, 8 * 256))
    w2 = np.asarray(inp["ffn_w2"], f32).reshape(DEPTH, 2, 11, 128, 4, 256)
    shared["ffn_w2"] = np.ascontiguousarray(w2.transpose(0, 1, 4, 3, 2, 5).reshape(DEPTH, 8, 128, 11 * 256))
    ow = np.stack([np.asarray(inp[nm], f32) for nm in ("hy_out", "mla_out", "ret_out", "conf_out")], 1)
    ow = ow.reshape(DEPTH, 4, 2, 128, 8, 128).transpose(0, 4, 3, 1, 2, 5)
    ow = np.ascontiguousarray(ow.reshape(DEPTH, 8, 128, 1024))
    shared["outw"] = ow

    shared["w_uq"] = np.ascontiguousarray(np.asarray(inp["mla_w_uq"], f32).reshape(DEPTH, 2, 128, 384).transpose(0, 2, 1, 3).reshape(DEPTH, 128, 768))
    shared["w_ukv"] = np.ascontiguousarray(np.asarray(inp["mla_w_ukv"], f32))
    tt_ = np.arange(T)
    inv = (10000.0 ** (-np.arange(8, dtype=np.float64) / 8))
    ang = np.concatenate([(tt_ // 64)[:, None] * inv, (tt_ % 64)[:, None] * inv], axis=1)
    cos_s = np.cos(ang).T.astype(f32); sin_s = np.sin(ang).T.astype(f32)
    rope_sample = np.stack([np.concatenate([cos_s, cos_s], 0), np.concatenate([sin_s, sin_s], 0)], 0)
    rope_prompt = np.stack([np.ones((32, T), f32), np.zeros((32, T), f32)], 0)
    oh_sample = np.zeros((32, NKEY), f32); oh_sample[0, :] = BIGC
    oh_prompt = np.zeros((32, NKEY), f32)
    for sgi in range(8):
        oh_prompt[sgi, sgi * 256:(sgi + 1) * 256] = BIGC
    zc = np.zeros((DEPTH, 128, 256), f32); zk = np.zeros((DEPTH, 32, 256), f32)
    rcon = np.zeros((128, 642), f32)
    mm_ = np.arange(128)[:, None]; nn_ = np.arange(128)[None, :]
    rcon[:, 0:128] = np.maximum(nn_ - mm_, 0); rcon[:, 128:256] = (nn_ >= mm_)
    rcon[:, 256:384] = np.maximum(mm_ - nn_, 0); rcon[:, 384:512] = (mm_ >= nn_)
    rcon[0:64, 512:640] = nn_ + 1; rcon[64:128, 512:640] = 128 - nn_
    rcon[:, 640] = 127 - np.arange(128); rcon[:, 641] = np.arange(128)
    shared["rcon"] = rcon
    zs0 = np.zeros((DEPTH, 128, 256), f32)

    for nm in ("hy_w1", "hy_w2", "hy_w3"):
        shared[nm] = np.ascontiguousarray(np.asarray(inp[nm], f32))
    hy_s = _hy_consts(2048, np.arange(T))
    hy_p = _hy_consts(256, 512 * (np.arange(T) // 256) + (np.arange(T) % 256))

    in_maps = []
    for core in range(8):
        m = dict(shared)
        if core < 4:
            m["cacheT"] = np.ascontiguousarray(np.asarray(inp["cache_mla_ckv"][core], f32).transpose(0, 2, 1))
            m["kropeC"] = np.ascontiguousarray(np.asarray(inp["cache_mla_krope"][core], f32).transpose(0, 2, 1))
            m["oh"] = oh_sample; m["ropeCS"] = rope_sample
            m["s0"] = np.ascontiguousarray(np.asarray(inp["state_ret"][core], f32).transpose(0, 1, 3, 2, 4).reshape(DEPTH, 128, 256))
        else:
            m["cacheT"] = zc; m["kropeC"] = zk; m["oh"] = oh_prompt; m["ropeCS"] = rope_prompt
            m["s0"] = zs0
        if core < 4:
            x = np.asarray(inp["x_sample"][core], f32)
            cond = np.asarray(inp["c"][core], f32)
        else:
            b0 = 8 * (core - 4)
            x = np.asarray(inp["x_prompt"][b0:b0 + 8], f32).reshape(T, D)
            cond = np.asarray(inp["c_ctx"], f32)
        m["xT"] = np.ascontiguousarray(x.T.reshape(8, 128, T).transpose(1, 0, 2))
        m["condT"] = _chunkT(cond, 8)
        fl = np.zeros((128, NFL), f32)
        fl[:, FL_HALO] = 1.0 if core < 4 else 0.0
        for i in range(16):
            fl[:, FL_KF + i] = 1.0 if core < 4 else (0.0 if i % 2 == 0 else 1.0)
            fl[:, FL_KB + i] = 1.0 if core < 4 else (0.0 if i % 2 == 1 else 1.0)
        fl[:, FL_M0] = 1.0; fl[0, FL_M0] = 0.0
        fl[:, FL_M0 + 1] = -1.0; fl[0, FL_M0 + 1] = 0.0
        m["flags"] = fl
        hc = hy_s if core < 4 else hy_p
        m["zT"] = hc["zT"]; m["decay"] = hc["decay"]; m["ClT"] = hc["ClT"]; m["FwT"] = hc["FwT"]; m["GiT"] = hc["GiT"]
        in_maps.append(m)
    return in_maps


_CACHE = {}


def run(inp, dbg=()):
    key = tuple(sorted(dbg))
    if key not in _CACHE:
        kb = KB(dbg)
        kb.build()
        _CACHE[key] = kb
    kb = _CACHE[key]
    in_maps = host_prep(inp)
    names = set(kb.dram.keys())
    in_maps = [{k: v for k, v in m.items() if k in names} for m in in_maps]
    res = run_bass_kernel_spmd(kb.nc, in_maps, core_ids=list(range(8)))
    return res.results


def kernel(**inputs):
    res = run(inputs)
    f32 = np.float32
    yp = np.zeros((32, 256, D), f32)
    ys = np.zeros((4, 2048, D), f32)
    for core in range(8):
        yT = np.asarray(res[core]["yT"], f32)
        y = yT.transpose(1, 0, 2).reshape(D, T).T
        if core < 4:
            ys[core] = y
        else:
            yp[8 * (core - 4):8 * (core - 4) + 8] = y.reshape(8, 256, D)
    ckv = np.zeros((32, DEPTH, 256, 128), f32)
    krope = np.zeros((32, DEPTH, 256, 32), f32)
    for core in range(4, 8):
        b0 = 8 * (core - 4)
        ckv[b0:b0 + 8] = np.asarray(res[core]["ckv_out"], f32).reshape(DEPTH, 8, 256, 128).transpose(1, 0, 2, 3)
        krope[b0:b0 + 8] = np.asarray(res[core]["krope_out"], f32).reshape(DEPTH, 8, 256, 32).transpose(1, 0, 2, 3)
    st = np.zeros((32, DEPTH, 2, 4, 64, 64), f32)
    for core in range(4, 8):
        b0 = 8 * (core - 4)
        so = np.asarray(res[core]["st_out"], f32).reshape(DEPTH, 2, 64, 8, 4, 64)
        st[b0:b0 + 8] = so.transpose(3, 0, 1, 4, 2, 5)
    return (yp, ys, ckv, krope, st)
```
